# Optimizing a Trainium2 kernel written in Bass

```python
import jax, jax.numpy as jnp
from jax import lax
import numpy as np

D_MODEL = 1024
BATCH = 8
SEQ = 2048
DEPTH = 2
DEC_BATCH = 16
DEC_SEQ = 16
PAST_LEN = 4096

CHUNK = 64
N_META = 16
D_CONV_A = D_MODEL
CONV_A_WIDTH = 31
D_CONV_B = D_MODEL
CONV_B_WIDTH = 3
D_HID = -(-8 * D_MODEL // (3 * 256)) * 256
IN_SPLITS = [D_CONV_A, D_CONV_A, D_CONV_B, D_CONV_B, D_CONV_B, D_MODEL, D_MODEL]
N_IN = sum(IN_SPLITS)
IN_OFFSETS = [int(o) for o in np.cumsum(IN_SPLITS[:-1])]
RMS_EPS = 1e-6
LN_EPS = 1e-5

kernel_name = "hybrid_conformer_shortconv_stream_step"


def rmsnorm(x, g):
    xf = x.astype(jnp.float32)
    y = xf * lax.rsqrt(jnp.mean(xf * xf, axis=-1, keepdims=True) + RMS_EPS)
    return (y * g.astype(jnp.float32)).astype(x.dtype)


def layernorm(x, g, b):
    xf = x.astype(jnp.float32)
    mu = jnp.mean(xf, axis=-1, keepdims=True)
    var = jnp.mean(jnp.square(xf - mu), axis=-1, keepdims=True)
    y = (xf - mu) * lax.rsqrt(var + LN_EPS)
    return (y * g.astype(jnp.float32) + b.astype(jnp.float32)).astype(x.dtype)


def causal_dwconv(x_full, w):
    return lax.conv_general_dilated(
        x_full, w[:, None, :].astype(x_full.dtype), window_strides=(1,), padding='VALID',
        dimension_numbers=('NWC', 'WIO', 'NWC'), feature_group_count=x_full.shape[-1])


def trunk_layer(x, st_a, st_b, norm1_g, w_in, conv_a_w, conv_a_b, ln_a_g, ln_a_b, w_a_out,
                conv_b_w, w_b_out, w_o, norm2_g, w_ffn_gate, w_ffn_up, w_ffn_down):
    xn = rmsnorm(x, norm1_g)
    proj = jnp.einsum('btd,dn->btn', xn, w_in)
    a_val, a_gate, b_b, b_c, b_x, g_a, g_b = jnp.split(proj, IN_OFFSETS, axis=-1)

    u_a = a_val * jax.nn.sigmoid(a_gate)
    u_a_full = jnp.concatenate([st_a.astype(u_a.dtype), u_a], axis=1)
    c_a = causal_dwconv(u_a_full, conv_a_w) + conv_a_b
    c_a = jax.nn.silu(layernorm(c_a, ln_a_g, ln_a_b))
    y_a = jnp.einsum('btc,cd->btd', c_a, w_a_out)

    z_b = b_c * b_x
    z_b_full = jnp.concatenate([st_b.astype(z_b.dtype), z_b], axis=1)
    c_b = causal_dwconv(z_b_full, conv_b_w)
    y_b = jnp.einsum('btc,cd->btd', b_b * c_b, w_b_out)

    merged = jax.nn.sigmoid(g_a) * y_a + jax.nn.sigmoid(g_b) * y_b
    h = x + jnp.einsum('btd,de->bte', merged, w_o)

    hn = rmsnorm(h, norm2_g)
    f = jax.nn.silu(jnp.einsum('btd,dh->bth', hn, w_ffn_gate)) * jnp.einsum('btd,dh->bth', hn, w_ffn_up)
    out = h + jnp.einsum('bth,hd->btd', f, w_ffn_down)

    new_a = u_a_full[:, -(CONV_A_WIDTH - 1):]
    new_b = z_b_full[:, -(CONV_B_WIDTH - 1):]
    return out, new_a, new_b


def setup_inputs(seed: int = 0) -> dict:
    key = jax.random.key(seed)
    ks = jax.random.split(key, 24)
    f32 = jnp.float32
    nrm = lambda k, shape, s: jax.random.normal(k, shape, f32) * s
    return {
        "x_prompt": nrm(ks[0], (BATCH, SEQ, D_MODEL), 1.0),
        "x_sample": nrm(ks[1], (DEC_BATCH, DEC_SEQ, D_MODEL), 1.0),
        "state_conv_a": nrm(ks[2], (DEPTH, DEC_BATCH, CONV_A_WIDTH - 1, D_CONV_A), 0.5),
        "state_conv_b": nrm(ks[3], (DEPTH, DEC_BATCH, CONV_B_WIDTH - 1, D_CONV_B), 0.5),
        "meta_tokens": nrm(ks[4], (N_META, D_MODEL), 1.0),
        "norm1_g": 1.0 + nrm(ks[5], (DEPTH, D_MODEL), 0.01),
        "w_in": nrm(ks[6], (DEPTH, D_MODEL, N_IN), D_MODEL ** -0.5),
        "conv_a_w": nrm(ks[7], (DEPTH, CONV_A_WIDTH, D_CONV_A), CONV_A_WIDTH ** -0.5),
        "conv_a_b": nrm(ks[8], (DEPTH, D_CONV_A), 0.01),
        "ln_a_g": 1.0 + nrm(ks[9], (DEPTH, D_CONV_A), 0.01),
        "ln_a_b": nrm(ks[10], (DEPTH, D_CONV_A), 0.01),
        "w_a_out": nrm(ks[11], (DEPTH, D_CONV_A, D_MODEL), D_CONV_A ** -0.5),
        "conv_b_w": nrm(ks[12], (DEPTH, CONV_B_WIDTH, D_CONV_B), CONV_B_WIDTH ** -0.5),
        "w_b_out": nrm(ks[13], (DEPTH, D_CONV_B, D_MODEL), D_CONV_B ** -0.5),
        "w_o": nrm(ks[14], (DEPTH, D_MODEL, D_MODEL), D_MODEL ** -0.5),
        "norm2_g": 1.0 + nrm(ks[15], (DEPTH, D_MODEL), 0.01),
        "w_ffn_gate": nrm(ks[16], (DEPTH, D_MODEL, D_HID), D_MODEL ** -0.5),
        "w_ffn_up": nrm(ks[17], (DEPTH, D_MODEL, D_HID), D_MODEL ** -0.5),
        "w_ffn_down": nrm(ks[18], (DEPTH, D_HID, D_MODEL), D_HID ** -0.5),
        "final_norm_g": 1.0 + nrm(ks[19], (D_MODEL,), 0.01),
    }


def reference(x_prompt, x_sample, state_conv_a, state_conv_b, meta_tokens, norm1_g, w_in,
              conv_a_w, conv_a_b, ln_a_g, ln_a_b, w_a_out, conv_b_w, w_b_out, w_o, norm2_g,
              w_ffn_gate, w_ffn_up, w_ffn_down, final_norm_g):
    b_p = x_prompt.shape[0]
    meta = jnp.broadcast_to(meta_tokens.astype(x_prompt.dtype)[None], (b_p, N_META, D_MODEL))
    hp = jnp.concatenate([meta, x_prompt], axis=1)
    zero_a = jnp.zeros((b_p, CONV_A_WIDTH - 1, D_CONV_A), x_prompt.dtype)
    zero_b = jnp.zeros((b_p, CONV_B_WIDTH - 1, D_CONV_B), x_prompt.dtype)
    hs = x_sample
    pa, pb, sa, sb = [], [], [], []
    for l in range(DEPTH):
        lw = (norm1_g[l], w_in[l], conv_a_w[l], conv_a_b[l], ln_a_g[l], ln_a_b[l], w_a_out[l],
              conv_b_w[l], w_b_out[l], w_o[l], norm2_g[l], w_ffn_gate[l], w_ffn_up[l], w_ffn_down[l])
        hp, na, nb = trunk_layer(hp, zero_a, zero_b, *lw)
        pa.append(na); pb.append(nb)
        hs, na, nb = trunk_layer(hs, state_conv_a[l], state_conv_b[l], *lw)
        sa.append(na); sb.append(nb)
    y_prompt = rmsnorm(hp, final_norm_g)[:, N_META:]
    y_sample = rmsnorm(hs, final_norm_g)
    new_conv_a_prompt = jnp.stack(pa, axis=0)
    new_conv_b_prompt = jnp.stack(pb, axis=0)
    new_conv_a_sample = jnp.stack(sa, axis=0)
    new_conv_b_sample = jnp.stack(sb, axis=0)
    return (y_prompt, y_sample, new_conv_a_prompt, new_conv_b_prompt, new_conv_a_sample, new_conv_b_sample)
```

```python
import numpy as np
from contextlib import ExitStack
import concourse.bass as bass
import concourse.mybir as mybir
from concourse.ap import AP
from concourse.bass_utils import run_bass_kernel_spmd

F32 = mybir.dt.float32
BF16 = mybir.dt.bfloat16
AF = mybir.ActivationFunctionType
ALU = mybir.AluOpType

NCORES = 8
D = 1024
KC = 8
NIN = 7168
DH = 2816
HC = 22
NMETA = 16
SEQ = 2048
TP = NMETA + SEQ
NSMP = 2
SL = 16
TT = TP + NSMP * SL
KA = 31
HA = 30
KB = 3
HB = 2
DEPTH = 2
OFF_AVAL, OFF_AGATE, OFF_BB, OFF_BC, OFF_BX, OFF_GA, OFF_GB = 0, 1024, 2048, 3072, 4096, 5120, 6144
RMS_EPS = 1e-6
LN_EPS = 1e-5

NPE = 19
NSLOT = 10
UW = 256
UPC = UW // 128
DIAG_ENG = "dve"
TLAG = 0
GLAG = 4
CLAG = 0
NMAX = 350
SCHED = True
PE_GHZ = 2.25
SCHED_Q = 1.0
SCHED_SLACK = 0.0
SCHED_LAT = 400.0
READY_TIE = True
RING_D = 6
FINE_A = 1
FINE_G = 1
FINE_D = 1
HOIST_LN = 1
N_XROWS = 0
N_SQROWS = 2
AC_ORDER = 0
RING_G = 8
NSG_CFG = 3
TL_LO, TL_HI = 1.0, 0.0
WINDOW = 128
DEBUG_SCHED = False
WINDOW = 128

PV_G1 = 0
PV_CAB = PV_G1 + DEPTH * KC
PV_LNG = PV_CAB + DEPTH * KC
PV_LNB = PV_LNG + DEPTH * KC
PV_G2 = PV_LNB + DEPTH * KC
PV_GF = PV_G2 + DEPTH * KC
PV_WA = PV_GF + KC
PV_WB = PV_WA + DEPTH * KC * KA
NPV = PV_WB + DEPTH * KC * KB


def make_sts():
    sizes = [699, 699, 698]
    sts = []
    g = 0
    for i, n in enumerate(sizes):
        last = i == len(sizes) - 1
        n_p = n - (NSMP * SL if last else 0)
        n0 = (n + 1) // 2
        tiles = [(0, n0), (n0, n - n0)]
        sts.append(dict(idx=i, g0=g, n=n, n_p=n_p, smp=last, tiles=tiles, first=(i == 0), last=last))
        g += n
    assert g == TT
    return sts


STS = make_sts()
TSMAX = max(s["n"] for s in STS)
UEXT = max(HA + s["n_p"] + (NSMP * (HA + SL) if s["smp"] else 0) for s in STS)
ZEXT = max(HB + s["n_p"] + (NSMP * (HB + SL) if s["smp"] else 0) for s in STS)

ENGS = ("pe", "act", "dve", "pool", "sp")


class Task:
    __slots__ = ("eng", "fn", "deps", "dma", "signal", "ev_sem", "ev_val", "name", "seq", "prio")

    def __init__(self, eng, fn, name):
        self.eng = eng
        self.fn = fn
        self.deps = set()
        self.dma = None
        self.signal = False
        self.ev_sem = None
        self.ev_val = 0
        self.name = name


class Prog:
    HI = ("sgb_", "m2_", "sq_", "ss_", "sd_", "rstd_", "nrm_", "lnsq", "lncb", "lns2", "lns1", "lnm_", "lnmsq", "lnvar", "lnsd", "lnrs", "lnnm", "lnmul", "lnadd", "lnsilu")

    def default_prio(self, name):
        return 0 if name.startswith(self.HI) else 1

    def __init__(self):
        self.tasks = {e: [] for e in ENGS}
        self.writers = {}
        self.readers = {}
        self.nseq = 0
        self._partial = {}

    def add(self, eng, fn, reads=(), writes=(), extra=(), dma=None, name="", prio=None):
        t = Task(eng, fn, name)
        t.prio = self.default_prio(name) if prio is None else prio
        t.dma = dma
        t.seq = self.nseq
        self.nseq += 1
        deps = set(x for x in extra if x is not None)
        for k in reads:
            deps.update(self.writers.get(k, ()))
        for k in writes:
            deps.update(self.readers.get(k, ()))
            deps.update(self.writers.get(k, ()))
        for k in reads:
            self.readers.setdefault(k, []).append(t)
        for k in writes:
            self.writers[k] = [t]
            self.readers[k] = []
        deps.discard(t)
        t.deps = deps
        self.tasks[eng].append(t)
        return t


def sub_ap(base, extra_off, dims):
    return AP(base.tensor, base.offset + extra_off, [list(base.ap[0])] + [list(d) for d in dims])


class Est:
    def __init__(self, eng):
        self.eng = eng
        self.ns = 0.0
        self.act_set = None
        self.dma_bytes = 0

    @staticmethod
    def _el(ap):
        n = 1
        for d in ap.shape[1:]:
            n *= d
        return n

    def matmul(self, out, lhsT, rhs, **k):
        self.ns += max(self._el(out), 64) / PE_GHZ + 2
        return self

    def activation(self, out, in_, func, bias=None, scale=None, **k):
        self.ns += 200 + 0.833 * self._el(out)
        if isinstance(scale, AP):
            self.ns += 90
        if func == AF.Sigmoid:
            self.act_set = "sig"
        elif func == AF.Silu:
            self.act_set = "silu"
        elif func == AF.Sqrt:
            self.act_set = "sqrt"
        return self

    def tensor_tensor(self, out, in0, in1, op, **k):
        self.ns += 140 + self._el(out) / 0.96
        return self

    def scalar_tensor_tensor(self, out, in0, scalar, in1, op0, op1, **k):
        self.ns += 140 + self._el(out) / 0.96
        return self

    def tensor_scalar(self, out, in0, scalar1, scalar2, op0, op1=None, **k):
        self.ns += 140 + self._el(out) / 0.96
        return self

    def reciprocal(self, out, in_):
        self.ns += 100 + 6.4 * self._el(out)
        return self

    def tensor_copy(self, out, in_, **k):
        self.ns += 150 + self._el(out)
        return self

    def memset(self, ap, c):
        self.ns += 100 + self._el(ap) * 0.5
        return self

    def affine_select(self, out, in_, **k):
        self.ns += 300
        return self

    def dma_start(self, out, in_, **k):
        self.dma_bytes += self._el(in_) * 4 * in_.shape[0]
        self.ns += 1050 if self.eng == "pool" else 150
        return self

    def then_inc(self, *a, **k):
        return self


def schedule(P, window=64, lat=150.0, dma_rate=300.0, dma_lat=2000.0):
    idx = 0
    for e in ENGS:
        pass
    allt = []
    for e in ENGS:
        allt.extend(P.tasks[e])
    allt.sort(key=lambda t: t.seq)
    est = {}
    for t in allt:
        st_ = Est(t.eng)
        t.fn(st_)
        est[t] = st_
    ndeps = {t: len(t.deps) for t in allt}
    users = {t: [] for t in allt}
    for t in allt:
        for d in t.deps:
            users[d].append(t)
    ready = {t: 0.0 for t in allt if ndeps[t] == 0}
    pending = {e: list(P.tasks[e]) for e in ENGS}
    free = {e: 0.0 for e in ENGS}
    dma_free = 0.0
    act_cur = None
    partial = {}
    pe_tl = []
    order = {e: [] for e in ENGS}
    finish = {}
    remaining = len(allt)
    while remaining:
        best = None
        for e in ENGS:
            pl = pending[e]
            cands = []
            mn = None
            for t in pl[:window]:
                r = ready.get(t)
                if r is None:
                    continue
                start = max(r, free[e])
                pen = 0.0
                if e == "act" and est[t].act_set is not None and est[t].act_set != act_cur:
                    pen = 1300.0
                cmp_start = start + (pen if t.prio >= 1 else 0.0)
                cands.append((cmp_start, start + pen, t, r))
                if mn is None or cmp_start < mn:
                    mn = cmp_start
            if not cands:
                continue
            pick = None
            for (cs, start, t, r) in cands:
                if cs <= mn + SCHED_SLACK:
                    k2 = (t.prio, cs, (r if READY_TIE else 0), t.seq)
                    if pick is None or k2 < pick[0]:
                        pick = (k2, cs, start, t)
            key = (pick[1], pick[3].prio, pick[3].seq, pick[2])
            if best is None or key < best[0]:
                best = (key, e, pick[3])
        assert best is not None, "scheduler deadlock"
        start = best[0][3]
        e, t = best[1], best[2]
        es_ = est[t]
        if t.dma is not None:
            issue_end = start + es_.ns
            x0 = max(issue_end, dma_free)
            dma_free = x0 + es_.dma_bytes / dma_rate
            fin = dma_free + dma_lat
            free[e] = issue_end
        else:
            fin = start + es_.ns
            free[e] = fin
            if e == "act" and es_.act_set is not None:
                act_cur = es_.act_set
        finish[t] = fin
        if e == "pe":
            pe_tl.append((start, fin, t.name))
        if DEBUG_SCHED and TL_LO <= start / 1e3 <= TL_HI:
            print(f"  [s] {e:4s} {t.name:22s} start={start/1e3:8.2f} fin={fin/1e3:8.2f}")
        order[e].append(t)
        pending[e].remove(t)
        remaining -= 1
        for u in users[t]:
            partial[u] = max(partial.get(u, 0.0), fin + lat)
            ndeps[u] -= 1
            if ndeps[u] == 0:
                ready[u] = partial[u]
    if DEBUG_SCHED:
        gaps = {}
        prev = 0.0
        for (st0, fn0, nm) in pe_tl:
            ph = nm.split("_")[0]
            if st0 > prev:
                gaps[ph] = gaps.get(ph, 0.0) + (st0 - prev)
            prev = max(prev, fn0)
        print("[sched] PE gaps before (us):", {k: round(v / 1e3, 1) for k, v in sorted(gaps.items(), key=lambda kv: -kv[1])})
        busy = {e: sum(est[t].ns for t in order[e]) for e in ENGS}
        print("[sched] busy us:", {e: round(busy[e] / 1e3, 1) for e in ENGS}, "free:", {e: round(free[e] / 1e3, 1) for e in ENGS}, "dma_free", round(dma_free / 1e3, 1))
    P.tasks = order
    return max(finish.values())


def build():
    nc = bass.Bass("TRN2", target_bir_lowering=False)

    def din(name, shape):
        return nc.dram_tensor(name, list(shape), F32, kind="ExternalInput").ap()

    def dout(name, shape):
        return nc.dram_tensor(name, list(shape), F32, kind="ExternalOutput").ap()

    xT = din("xT", [128, KC, TT])
    pvec_d = din("pvec", [128, NPV])
    shA_d = din("shA", [128, DEPTH * NSMP * KC * HA])
    shB_d = din("shB", [128, DEPTH * NSMP * KC * HB])
    w_in = din("w_in", [DEPTH, D, NIN])
    w_a_out = din("w_a_out", [DEPTH, D, D])
    w_b_out = din("w_b_out", [DEPTH, D, D])
    w_o = din("w_o", [DEPTH, D, D])
    w_gate = din("w_ffn_gate", [DEPTH, D, DH])
    w_up = din("w_ffn_up", [DEPTH, D, DH])
    w_down = din("w_ffn_down", [DEPTH, DH, D])
    yT = dout("yT", [128, KC, TT])
    oA_d = dout("oA", [128, DEPTH * KC * 3 * HA])
    oB_d = dout("oB", [128, DEPTH * KC * 3 * HB])

    P = Prog()
    es = ExitStack()
    with es:
        def sb(name, shape, dt):
            return es.enter_context(nc.sbuf_tensor(name, list(shape), dt))

        BUF2 = [sb(f"XBUF{i}", [128, KC * TSMAX * 4], mybir.dt.uint8) for i in range(2)]

        def st_views(par):
            Xv = BUF2[par][:].bitcast(F32).rearrange("p (k t) -> p k t", k=KC)
            o = BUF2[1 - par][:]
            XNv = sub_ap(o, 0, [[1, KC * TSMAX * 2]]).bitcast(BF16).rearrange("p (k t) -> p k t", k=KC)
            BBv = sub_ap(o, KC * TSMAX * 2, [[1, KC * TSMAX * 2]]).bitcast(BF16).rearrange("p (k t) -> p k t", k=KC)
            return Xv, XNv, BBv
        BIG = sb("BIG", [128, KC * TSMAX * 4 + KC * TSMAX * 2], mybir.dt.uint8)
        big_all = BIG[:]
        CA = sub_ap(big_all, 0, [[1, KC * TSMAX * 4]]).bitcast(F32)
        CAACT = sub_ap(big_all, KC * TSMAX * 4, [[1, KC * TSMAX * 2]]).bitcast(BF16)
        MERGED = sub_ap(big_all, 0, [[1, KC * TSMAX * 2]]).bitcast(BF16)
        FF = sub_ap(big_all, 0, [[1, HC * TSMAX * 2]]).bitcast(BF16)
        assert HC * TSMAX * 2 <= KC * TSMAX * 6

        def ca_ap(kc, a, b):
            return CA[:, kc * TSMAX + a: kc * TSMAX + b]

        def ca3_ap(a, b):
            return sub_ap(CA, a, [[TSMAX, KC], [1, b - a]])

        def caact_ap(kc, a, b):
            return CAACT[:, kc * TSMAX + a: kc * TSMAX + b]

        def merged_ap(kc, a, b):
            return MERGED[:, kc * TSMAX + a: kc * TSMAX + b]

        def f_ap(hc, a, b):
            return FF[:, hc * TSMAX + a: hc * TSMAX + b]

        UX = [sb(f"UX{i}", [128, UEXT], F32) for i in range(2)]
        UB = [sb(f"UB{i}", [128, UEXT], BF16) for i in range(2)]
        ZX = [sb(f"ZX{i}", [128, ZEXT], F32) for i in range(2)]
        BBUF = [sb(f"BBUF{i}", [128, TSMAX], F32) for i in range(2)]
        ZB = [sb(f"ZB{i}", [128, ZEXT], BF16) for i in range(2)]
        SQ = [sb(f"SQ{i}", [128, KC, NMAX], BF16) for i in range(2)]
        CBF = sb("CBF", [128, KC, NMAX], BF16)
        NST_ = 2
        SD = [sb(f"SD{i}", [128, NMAX], F32) for i in range(NST_)]
        RSTD = [sb(f"RSTD{i}", [128, NMAX], F32) for i in range(NST_)]
        MEAN = [sb(f"MEAN{i}", [128, NMAX], F32) for i in range(NST_)]
        VAR = [sb(f"VAR{i}", [128, NMAX], F32) for i in range(NST_)]
        NMR = [sb(f"NMR{i}", [128, NMAX], F32) for i in range(NST_)]
        NSG = NSG_CFG
        SG = [sb(f"SG{i}", [128, NMAX], F32) for i in range(NSG)]
        M1 = [sb(f"M1_{i}", [128, NMAX], F32) for i in range(2)]
        XROWS = [sb(f"XROW{i}", [128, TSMAX], F32) for i in range(N_XROWS)]
        DIAG = [sb(f"DIAG{i}", [128, max(NPE, 1), 128], BF16) for i in range(2)]
        DIAG3 = [sb(f"DIAG3_{i}", [128, KB, 128], BF16) for i in range(2)]
        WS = [sb(f"WS{i}", [128, 8, UW], BF16) for i in range(NSLOT)]
        PV = sb("PV", [128, NPV], F32)
        IDB = sb("IDB", [128, 128], BF16)
        IDF = sb("IDF", [128, 128], F32)
        EPS_T = sb("EPS_T", [128, 2], F32)
        WARM = sb("WARM", [128, 2], F32)
        EPSC[RMS_EPS] = EPS_T[:, 0:1]
        EPSC[LN_EPS] = EPS_T[:, 1:2]
        ONES = sb("ONES", [128, 128], BF16)
        SHA = sb("SHA", [128, DEPTH * NSMP * KC * HA], F32)
        SHB = sb("SHB", [128, DEPTH * NSMP * KC * HB], F32)
        HISTA = sb("HISTA", [128, DEPTH * KC * HA], F32)
        HISTZ = sb("HISTZ", [128, DEPTH * KC * HB], F32)
        OUTA = sb("OUTA", [128, DEPTH * KC * 3 * HA], F32)
        OUTB = sb("OUTB", [128, DEPTH * KC * 3 * HB], F32)
        PB = [es.enter_context(nc.psum_tensor(f"PB{i}", [128, 512], F32)) for i in range(8)]

        sems = {}

        def sem(name):
            if name not in sems:
                sems[name] = es.enter_context(nc.semaphore(name))
            return sems[name]

        for e_ in ENGS:
            sem("p_" + e_)

        t_pv = P.add("sp", lambda e: e.dma_start(out=PV[:], in_=pvec_d[:, :]), writes=[("pv",)], dma="d_pv", name="ld_pv")
        P.add("sp", lambda e: e.dma_start(out=SHA[:], in_=shA_d[:, :]), writes=[("sha",)], dma="d_sha", name="ld_sha")
        P.add("sp", lambda e: e.dma_start(out=SHB[:], in_=shB_d[:, :]), writes=[("shb",)], dma="d_shb", name="ld_shb")

        P.add("pool", lambda e: e.memset(IDF[:], 0.0), writes=[("idf",)], name="idf0")
        P.add("pool", lambda e: e.memset(ONES[:], 1.0), writes=[("ones",)], name="ones")
        P.add("pool", lambda e: e.memset(EPS_T[:, 0:1], RMS_EPS), writes=[("epsc", 0)], name="eps0")
        P.add("pool", lambda e: e.memset(EPS_T[:, 1:2], LN_EPS), writes=[("epsc", 1)], name="eps1")
        P.add("pool", lambda e: e.affine_select(out=IDF[:], in_=IDF[:], pattern=[[-1, 128]], compare_op=ALU.not_equal,
                                                fill=1.0, base=0, channel_multiplier=1),
              reads=[("idf",)], writes=[("idf",)], name="ident")
        P.add("pool", lambda e: e.tensor_copy(out=IDB[:], in_=IDF[:]), reads=[("idf",)], writes=[("idb",)], name="identb")

        def pvc(col):
            return PV[:, col:col + 1]

        wstate = dict(n=0)
        bank_state = dict(n=0)

        def load_unit(src_ap, nk, ncols):
            n = wstate["n"]
            wstate["n"] += 1
            slot = n % NSLOT
            key = ("ws", slot)
            src = src_ap.rearrange("(kc p) n -> p kc n", p=128)
            dst = WS[slot][:, 0:nk, 0:ncols]
            P.add("pool", lambda e, dst=dst, src=src: e.dma_start(out=dst, in_=src), writes=[key],
                  dma=f"d_ws{slot}", name=f"ldw{n}")
            return slot

        def next_banks(k, ring=6):
            out = []
            for _ in range(k):
                out.append(bank_state["n"] % ring)
                bank_state["n"] += 1
            return out

        nb_state = dict(cur=[], prev=[])
        fin_state = dict(cur=[], prev=[])

        def mm_job(accs, reads, name, fine_keys=None):
            t0_ = _mm_job(accs, reads, name, fine_keys)
            if any(str(k[0]).startswith(("xn", "bbcb")) for k in reads):
                nb_state["cur"].append(t0_)
            return t0_

        def _mm_job(accs, reads, name, fine_keys=None):
            def fn(e, accs=accs):
                last = None
                for (bank, n, mms) in accs:
                    for i, (l_ap, r_ap) in enumerate(mms):
                        last = e.matmul(PB[bank][:, 0:n], l_ap, r_ap, start=(i == 0), stop=(i == len(mms) - 1))
                return last
            if fine_keys is not None:
                K_ = len(accs[0][2])
                assert all(len(a[2]) == K_ for a in accs) and len(fine_keys) == K_
                base_reads = [r for r in reads if r not in set(fine_keys)]
                t_ = None
                for i in range(K_):
                    def fni(e, accs=accs, i=i, K_=K_):
                        last = None
                        for (bank, n, mms) in accs:
                            l_ap, r_ap = mms[i]
                            last = e.matmul(PB[bank][:, 0:n], l_ap, r_ap, start=(i == 0), stop=(i == K_ - 1))
                        return last
                    t_ = P.add("pe", fni, reads=base_reads + [fine_keys[i]], writes=[("pb", a[0]) for a in accs], name=f"{name}k{i}")
                return t_
            return P.add("pe", fn, reads=reads, writes=[("pb", a[0]) for a in accs], name=name)

        prev_H_tasks = []
        out_dma_tasks = []
        sgc = dict(n=0)
        abc = dict(n=0)
        stc = dict(n=0)

        for st in STS:
            g0, n_st, n_p, smp = st["g0"], st["n"], st["n_p"], st["smp"]
            tiles = st["tiles"]
            NT = len(tiles)
            sidx = st["idx"]

            par = sidx % 2
            X, XN, BBCB = st_views(par)
            XK, XNK, BBK = f"x{par}", f"xn{par}", f"bbcb{par}"
            if sidx > 0:
                nb_state["prev"], nb_state["cur"] = nb_state["cur"], []
                fin_state["prev"], fin_state["cur"] = fin_state["cur"], []

            def emit_xload(st2, deps):
                p2 = st2["idx"] % 2
                X2 = st_views(p2)[0]
                for kc in range(KC):
                    P.add("sp", lambda e, g2=st2["g0"], n2=st2["n"], kc=kc, X2=X2: e.dma_start(out=X2[:, kc, 0:n2], in_=xT[:, kc, g2:g2 + n2]),
                          writes=[(f"x{p2}", kc, tt) for tt in range(len(st2["tiles"]))], extra=deps, dma=f"d_x{kc}", name=f"ldx{st2['idx']}_{kc}")

            if sidx == 0:
                emit_xload(st, [])

            def ext_parts(a, b, H, shift):
                parts = []
                pa, pb_ = a, min(b, n_p)
                if pb_ > pa:
                    parts.append(("p", pa, pb_ - pa, pa + shift))
                if b > n_p:
                    assert smp and a <= n_p and b == n_p + NSMP * SL
                    parts.append(("s", n_p, NSMP * SL, H + n_p + shift))
                return parts

            def ext_ap(buf, part, H):
                kind, lo, n, off = part
                if kind == "p":
                    return buf[:, off:off + n]
                return sub_ap(buf[:], off, [[H + SL, NSMP], [1, SL]])

            def cmp_ap(ap2d_fn, part):
                kind, lo, n, off = part
                base = ap2d_fn(lo, lo + n)
                if kind == "p":
                    return base
                return sub_ap(base, 0, [[SL, NSMP], [1, SL]])

            def norm_accum(j, tt, tagn):
                ta, tn = tiles[tt]
                sq = SQ[tt]
                bank = 6 + tt
                t_sq = P.add("act", lambda e, sq=sq, j=j, ta=ta, tn=tn, X=X: e.activation(out=sq[:, j, 0:tn], in_=X[:, j, ta:ta + tn], func=AF.Square),
                                 reads=[(XK, j, tt)], writes=[("sq", tt, j)], name=f"sq_{tagn}{tt}_{j}")
                if tagn.startswith("fin"):
                    fin_state["cur"].append(t_sq)
                P.add("pe", lambda e, sq=sq, j=j, tn=tn, bank=bank: e.matmul(PB[bank][:, 0:tn], ONES[:], sq[:, j, 0:tn], start=(j == 0), stop=(j == KC - 1)),
                      reads=[("sq", tt, j), ("ones",)], writes=[("pb", bank)], name=f"ss_{tagn}{tt}_{j}")

            def norm_finish(tt, gcol, eps, out_fn, out_keyname, tagn, extra_w=()):
                ta, tn = tiles[tt]
                bank = 6 + tt
                si = tt
                P.add("act", lambda e, tn=tn, bank=bank, si=si, eps=eps: e.activation(out=SD[si][:, 0:tn], in_=PB[bank][:, 0:tn], func=AF.Sqrt, scale=1.0 / D, bias=EPSC[eps]),
                      reads=[("pb", bank), ("epsc", 0)], writes=[("sd", si)], name=f"sd_{tagn}{tt}", prio=0.1 * tt)
                P.add("dve", lambda e, tn=tn, si=si: e.reciprocal(out=RSTD[si][:, 0:tn], in_=SD[si][:, 0:tn]),
                      reads=[("sd", si)], writes=[("rstd", si)], name=f"rstd_{tagn}{tt}", prio=0.1 * tt)
                for kc in range(KC):
                    P.add("dve", lambda e, kc=kc, ta=ta, tn=tn, si=si, X=X: e.scalar_tensor_tensor(
                        out=out_fn(kc, ta, ta + tn), in0=X[:, kc, ta:ta + tn], scalar=pvc(gcol + kc),
                        in1=RSTD[si][:, 0:tn], op0=ALU.mult, op1=ALU.mult),
                        reads=[(XK, kc, tt), ("rstd", si), ("pv",)], writes=[(out_keyname, kc, tt)], extra=extra_w, name=f"nrm_{tagn}{tt}_{kc}", prio=0.1 * tt)
                    if tagn.startswith("fin"):
                        fin_state["cur"].append(P.tasks["dve"][-1])

            xn_out = lambda kc, a, b, XN=XN: XN[:, kc, a:b]
            for j in range(KC):
                for tt in range(NT):
                    norm_accum(j, tt, f"r1s{sidx}l0")
            for tt in range(NT):
                norm_finish(tt, PV_G1, RMS_EPS, xn_out, XNK, f"r1s{sidx}l0", extra_w=list(fin_state["prev"]))

            for l in range(DEPTH):
                tag = f"s{sidx}l{l}"

                units = {}
                silu_tasks = []
                xnk = lambda tt: [(XNK, kc, tt) for kc in range(KC)]

                def emit_A(j):
                    jb = j % 2
                    jd = j % 2
                    if j % UPC == 0:
                        c0 = UW * (j // UPC)
                        units["val"] = load_unit(w_in[l, :, OFF_AVAL + c0: OFF_AVAL + c0 + UW], 8, UW)
                        units["gate"] = load_unit(w_in[l, :, OFF_AGATE + c0: OFF_AGATE + c0 + UW], 8, UW)
                    uval, ugate = units["val"], units["gate"]
                    mc = (j % UPC) * 128
                    ux, ub = UX[jb], UB[jb]
                    if st["first"]:
                        P.add("act", lambda e, ux=ux: e.activation(out=ux[:, 0:HA], in_=PV[:, 0:HA], func=AF.Copy, scale=0.0),
                              reads=[("pv",)], writes=[("u", jb, "h")], name=f"uh0_{tag}{j}")
                    else:
                        hoff = (l * KC + j) * HA
                        P.add("act", lambda e, ux=ux, hoff=hoff: e.activation(out=ux[:, 0:HA], in_=HISTA[:, hoff:hoff + HA], func=AF.Copy),
                              reads=[("hista", l, j)], writes=[("u", jb, "h")], name=f"uh_{tag}{j}")
                    if smp:
                        src = sub_ap(SHA[:], (l * NSMP * KC + j) * HA, [[KC * HA, NSMP], [1, HA]])
                        dst = sub_ap(ux[:], HA + n_p, [[HA + SL, NSMP], [1, HA]])
                        P.add("act", lambda e, src=src, dst=dst: e.activation(out=dst, in_=src, func=AF.Copy),
                              reads=[("sha",)], writes=[("u", jb, "hs")], name=f"uhs_{tag}{j}")
                    wcol = PV_WA + (l * KC + j) * KA
                    bcol = PV_CAB + l * KC + j
                    if NPE > 0:
                        in0 = sub_ap(IDB[:], 0, [[0, NPE], [1, 128]])
                        in1 = sub_ap(PV[:], wcol, [[1, NPE], [0, 128]])
                        P.add(DIAG_ENG, lambda e, jd=jd, in0=in0, in1=in1: e.tensor_tensor(out=DIAG[jd][:, 0:NPE, :], in0=in0, in1=in1, op=ALU.mult),
                              reads=[("idb",), ("pv",)], writes=[("diag", jd)], name=f"diag_{tag}{j}")
                    def a_tile(tt):
                        ta, tn = tiles[tt]
                        bv, bg = next_banks(2)
                        accs = [(bv, tn, [(WS[uval][:, kc, mc:mc + 128], XN[:, kc, ta:ta + tn]) for kc in range(KC)]),
                                (bg, tn, [(WS[ugate][:, kc, mc:mc + 128], XN[:, kc, ta:ta + tn]) for kc in range(KC)])]
                        mm_job(accs, [("ws", uval), ("ws", ugate)] + xnk(tt), f"A_{tag}{j}_{tt}", fine_keys=(xnk(tt) if (j == 0 and FINE_A) else None))
                        si = sgc["n"] % NSG
                        sgc["n"] += 1
                        P.add("act", lambda e, si=si, bg=bg, tn=tn: e.activation(out=SG[si][:, 0:tn], in_=PB[bg][:, 0:tn], func=AF.Sigmoid),
                              reads=[("pb", bg)], writes=[("sg", si)], name=f"sig_{tag}{j}_{tt}")
                        parts = ext_parts(ta, ta + tn, HA, HA)
                        def fn(e, parts=parts, ux=ux, bv=bv, si=si, ta=ta):
                            last = None
                            for part in parts:
                                o = ext_ap(ux, part, HA)
                                i0 = cmp_ap(lambda a, b: PB[bv][:, a - ta:b - ta], part)
                                i1 = cmp_ap(lambda a, b: SG[si][:, a - ta:b - ta], part)
                                last = e.tensor_tensor(out=o, in0=i0, in1=i1, op=ALU.mult)
                            return last
                        P.add("dve", fn, reads=[("pb", bv), ("sg", si)], writes=[("u", jb, tt)], name=f"glu_{tag}{j}_{tt}")
                    def a_post():
                        ukeys = [("u", jb, "h")] + ([("u", jb, "hs")] if smp else []) + [("u", jb, tt) for tt in range(NT)]
                        LU = HA + n_p + (NSMP * (HA + SL) if smp else 0)
                        if not st["last"]:
                            hoff = (l * KC + j) * HA
                            P.add("act", lambda e, ux=ux, hoff=hoff, n_p=n_p: e.activation(out=HISTA[:, hoff:hoff + HA], in_=ux[:, n_p:n_p + HA], func=AF.Copy),
                                  reads=ukeys, writes=[("hista", l, j)], name=f"hst_{tag}{j}")
                        else:
                            ooff = (l * KC + j) * 3 * HA
                            def fn(e, ux=ux, ooff=ooff, n_p=n_p):
                                e.activation(out=OUTA[:, ooff:ooff + HA], in_=ux[:, n_p:n_p + HA], func=AF.Copy)
                                src = sub_ap(ux[:], HA + n_p + SL, [[HA + SL, NSMP], [1, HA]])
                                dst = sub_ap(OUTA[:], ooff + HA, [[HA, NSMP], [1, HA]])
                                return e.activation(out=dst, in_=src, func=AF.Copy)
                            P.add("act", fn, reads=ukeys, writes=[("outa", l, j)], name=f"outa_{tag}{j}")
                        cakeys = [("ca", j, tt) for tt in range(NT)]
                        if NPE > 0:
                            P.add("act", lambda e, ux=ux, ub=ub, LU=LU: e.activation(out=ub[:, 0:LU], in_=ux[:, 0:LU], func=AF.Copy),
                                  reads=ukeys, writes=[("ubf", jb)], name=f"ubf_{tag}{j}")
                            for tt, (ta, tn) in enumerate(tiles):
                                bank = 6 + (abc["n"] % 2)
                                abc["n"] += 1
                                parts = ext_parts(ta, ta + tn, HA, 0)
                                def fn(e, parts=parts, ub=ub, bank=bank, jd=jd, ta=ta):
                                    last = None
                                    for part in parts:
                                        o = cmp_ap(lambda a, b: PB[bank][:, a - ta:b - ta], part)
                                        for k in range(NPE):
                                            p2 = (part[0], part[1], part[2], part[3] + k)
                                            last = e.matmul(o, DIAG[jd][:, k, :], ext_ap(ub, p2, HA), start=(k == 0), stop=(k == NPE - 1))
                                    return last
                                P.add("pe", fn, reads=[("ubf", jb), ("diag", jd)], writes=[("pb", bank)], name=f"cvpe_{tag}{j}_{tt}")
                                P.add("act", lambda e, j=j, ta=ta, tn=tn, bank=bank, bcol=bcol: e.activation(
                                    out=ca_ap(j, ta, ta + tn), in_=PB[bank][:, 0:tn], func=AF.Identity, bias=pvc(bcol)),
                                    reads=[("pb", bank), ("pv",)], writes=[("ca", j, tt)], extra=prev_H_tasks, name=f"cvev_{tag}{j}_{tt}")
                        for k in range(NPE, KA):
                            partsr = ext_parts(0, n_st, HA, k)
                            def fn(e, partsr=partsr, ux=ux, j=j, k=k, first=(k == 0), bcol=bcol, wcol=wcol):
                                last = None
                                for part in partsr:
                                    i0 = ext_ap(ux, part, HA)
                                    o = cmp_ap(lambda a, b: ca_ap(j, a, b), part)
                                    if first:
                                        last = e.tensor_scalar(out=o, in0=i0, scalar1=pvc(wcol + k), scalar2=pvc(bcol), op0=ALU.mult, op1=ALU.add)
                                    else:
                                        last = e.scalar_tensor_tensor(out=o, in0=i0, scalar=pvc(wcol + k), in1=o, op0=ALU.mult, op1=ALU.add)
                                return last
                            rk = ukeys + [("pv",)] + ([] if k == 0 else cakeys)
                            P.add("dve", fn, reads=rk, writes=cakeys, extra=(prev_H_tasks if k == 0 else ()), name=f"tap_{tag}{j}_{k}")
                        if HOIST_LN:
                            ta0, tn0 = tiles[0]
                            P.add("act", lambda e, j=j, ta0=ta0, tn0=tn0: e.activation(out=SQ[0][:, j, 0:tn0], in_=ca_ap(j, ta0, ta0 + tn0), func=AF.Square),
                                  reads=[("ca", j, 0)], writes=[("sq", 0, j)], name=f"lnsqh_{tag}{j}")
                            P.add("act", lambda e, j=j, ta0=ta0, tn0=tn0: e.activation(out=CBF[:, j, 0:tn0], in_=ca_ap(j, ta0, ta0 + tn0), func=AF.Copy),
                                  reads=[("ca", j, 0)], writes=[("cbf", j)], name=f"lncbh_{tag}{j}")
                    return a_tile, a_post

                def emit_C(j):
                    jb = j % 2
                    jd = j % 2
                    if j % UPC == 0:
                        c0 = UW * (j // UPC)
                        units["bc"] = load_unit(w_in[l, :, OFF_BC + c0: OFF_BC + c0 + UW], 8, UW)
                        units["bx"] = load_unit(w_in[l, :, OFF_BX + c0: OFF_BX + c0 + UW], 8, UW)
                        units["bb"] = load_unit(w_in[l, :, OFF_BB + c0: OFF_BB + c0 + UW], 8, UW)
                    ubc, ubx, ubb = units["bc"], units["bx"], units["bb"]
                    mc = (j % UPC) * 128
                    zx, zb = ZX[jb], ZB[jb]
                    if st["first"]:
                        P.add("act", lambda e, zx=zx: e.activation(out=zx[:, 0:HB], in_=PV[:, 0:HB], func=AF.Copy, scale=0.0),
                              reads=[("pv",)], writes=[("z", jb, "h")], name=f"zh0_{tag}{j}")
                    else:
                        hoffz = (l * KC + j) * HB
                        P.add("act", lambda e, zx=zx, hoffz=hoffz: e.activation(out=zx[:, 0:HB], in_=HISTZ[:, hoffz:hoffz + HB], func=AF.Copy),
                              reads=[("histz", l, j)], writes=[("z", jb, "h")], name=f"zh_{tag}{j}")
                    if smp:
                        src = sub_ap(SHB[:], (l * NSMP * KC + j) * HB, [[KC * HB, NSMP], [1, HB]])
                        dst = sub_ap(zx[:], HB + n_p, [[HB + SL, NSMP], [1, HB]])
                        P.add("act", lambda e, src=src, dst=dst: e.activation(out=dst, in_=src, func=AF.Copy),
                              reads=[("shb",)], writes=[("z", jb, "hs")], name=f"zhs_{tag}{j}")
                    wbcol = PV_WB + (l * KC + j) * KB
                    in0 = sub_ap(IDB[:], 0, [[0, KB], [1, 128]])
                    in1 = sub_ap(PV[:], wbcol, [[1, KB], [0, 128]])
                    P.add(DIAG_ENG, lambda e, jd=jd, in0=in0, in1=in1: e.tensor_tensor(out=DIAG3[jd][:, :, :], in0=in0, in1=in1, op=ALU.mult),
                          reads=[("idb",), ("pv",)], writes=[("diag3", jd)], name=f"diag3_{tag}{j}")
                    def c_tile(tt):
                        ta, tn = tiles[tt]
                        b1, b2 = next_banks(2)
                        accs = [(b1, tn, [(WS[ubc][:, kc, mc:mc + 128], XN[:, kc, ta:ta + tn]) for kc in range(KC)]),
                                (b2, tn, [(WS[ubx][:, kc, mc:mc + 128], XN[:, kc, ta:ta + tn]) for kc in range(KC)])]
                        mm_job(accs, [("ws", ubc), ("ws", ubx)] + xnk(tt), f"C1_{tag}{j}_{tt}")
                        si = sgc["n"] % NSG
                        sgc["n"] += 1
                        P.add("act", lambda e, si=si, b2=b2, tn=tn: e.activation(out=SG[si][:, 0:tn], in_=PB[b2][:, 0:tn], func=AF.Copy),
                              reads=[("pb", b2)], writes=[("sg", si)], name=f"bx_{tag}{j}_{tt}")
                        parts = ext_parts(ta, ta + tn, HB, HB)
                        def fn(e, parts=parts, zx=zx, b1=b1, si=si, ta=ta):
                            last = None
                            for part in parts:
                                o = ext_ap(zx, part, HB)
                                i0 = cmp_ap(lambda a, b: PB[b1][:, a - ta:b - ta], part)
                                i1 = cmp_ap(lambda a, b: SG[si][:, a - ta:b - ta], part)
                                last = e.tensor_tensor(out=o, in0=i0, in1=i1, op=ALU.mult)
                            return last
                        P.add("dve", fn, reads=[("pb", b1), ("sg", si)], writes=[("z", jb, tt)], name=f"zb_{tag}{j}_{tt}")
                        (b3,) = next_banks(1)
                        accs = [(b3, tn, [(WS[ubb][:, kc, mc:mc + 128], XN[:, kc, ta:ta + tn]) for kc in range(KC)])]
                        mm_job(accs, [("ws", ubb)] + xnk(tt), f"C2_{tag}{j}_{tt}")
                        P.add("act", lambda e, jb=jb, b3=b3, ta=ta, tn=tn: e.activation(out=BBUF[jb][:, ta:ta + tn], in_=PB[b3][:, 0:tn], func=AF.Copy),
                              reads=[("pb", b3)], writes=[("bb", jb, tt)], name=f"bbv_{tag}{j}_{tt}")
                    def c_post():
                        zkeys = [("z", jb, "h")] + ([("z", jb, "hs")] if smp else []) + [("z", jb, tt) for tt in range(NT)]
                        LZ = HB + n_p + (NSMP * (HB + SL) if smp else 0)
                        if not st["last"]:
                            hoffz = (l * KC + j) * HB
                            P.add("act", lambda e, zx=zx, hoffz=hoffz, n_p=n_p: e.activation(out=HISTZ[:, hoffz:hoffz + HB], in_=zx[:, n_p:n_p + HB], func=AF.Copy),
                                  reads=zkeys, writes=[("histz", l, j)], name=f"hsz_{tag}{j}")
                        else:
                            ooffz = (l * KC + j) * 3 * HB
                            def fn(e, zx=zx, ooffz=ooffz, n_p=n_p):
                                e.activation(out=OUTB[:, ooffz:ooffz + HB], in_=zx[:, n_p:n_p + HB], func=AF.Copy)
                                src = sub_ap(zx[:], HB + n_p + SL, [[HB + SL, NSMP], [1, HB]])
                                dst = sub_ap(OUTB[:], ooffz + HB, [[HB, NSMP], [1, HB]])
                                return e.activation(out=dst, in_=src, func=AF.Copy)
                            P.add("act", fn, reads=zkeys, writes=[("outb", l, j)], name=f"outb_{tag}{j}")
                        P.add("act", lambda e, zx=zx, zb=zb, LZ=LZ: e.activation(out=zb[:, 0:LZ], in_=zx[:, 0:LZ], func=AF.Copy),
                              reads=zkeys, writes=[("zbf", jb)], name=f"zbf_{tag}{j}")
                        for tt, (ta, tn) in enumerate(tiles):
                            bank = 6 + (abc["n"] % 2)
                            abc["n"] += 1
                            parts = ext_parts(ta, ta + tn, HB, 0)
                            def fn(e, parts=parts, zb=zb, bank=bank, jd=jd, ta=ta):
                                last = None
                                for part in parts:
                                    o = cmp_ap(lambda a, b: PB[bank][:, a - ta:b - ta], part)
                                    for k in range(KB):
                                        p2 = (part[0], part[1], part[2], part[3] + k)
                                        last = e.matmul(o, DIAG3[jd][:, k, :], ext_ap(zb, p2, HB), start=(k == 0), stop=(k == KB - 1))
                                return last
                            P.add("pe", fn, reads=[("zbf", jb), ("diag3", jd)], writes=[("pb", bank)], name=f"cv3pe_{tag}{j}_{tt}")
                            P.add("dve", lambda e, j=j, jb=jb, ta=ta, tn=tn, bank=bank, BBCB=BBCB: e.tensor_tensor(
                                out=BBCB[:, j, ta:ta + tn], in0=PB[bank][:, 0:tn], in1=BBUF[jb][:, ta:ta + tn], op=ALU.mult),
                                reads=[("pb", bank), ("bb", jb, tt)], writes=[(BBK, j, tt)], name=f"bbcb_{tag}{j}_{tt}")
                    return c_tile, c_post

                for step in range(KC + CLAG):
                    def do_A():
                        if step < KC:
                            a_tile, a_post = emit_A(step)
                            for tt in range(NT):
                                a_tile(tt)
                            a_post()
                    def do_C():
                        jc = step - CLAG
                        if 0 <= jc < KC:
                            c_tile, c_post = emit_C(jc)
                            for tt in range(NT):
                                c_tile(tt)
                            c_post()
                    if AC_ORDER == 0:
                        do_A()
                        do_C()
                    else:
                        do_C()
                        do_A()

                for tt, (ta, tn) in enumerate(tiles):
                    si = tt
                    sq = SQ[tt]
                    sqk = [("sq", tt, kc) for kc in range(KC)]
                    cakt = [("ca", kc, tt) for kc in range(KC)]
                    cbk = [("cbf", kc) for kc in range(KC)]
                    if not (HOIST_LN and tt == 0):
                        P.add("act", lambda e, sq=sq, ta=ta, tn=tn: e.activation(out=sq[:, :, 0:tn], in_=ca3_ap(ta, ta + tn), func=AF.Square),
                              reads=cakt, writes=sqk, name=f"lnsq_{tag}{tt}")
                        P.add("act", lambda e, ta=ta, tn=tn: e.activation(out=CBF[:, :, 0:tn], in_=ca3_ap(ta, ta + tn), func=AF.Copy),
                              reads=cakt, writes=cbk, name=f"lncb_{tag}{tt}")
                    def fn(e, sq=sq, tn=tn):
                        last = None
                        for kc in range(KC):
                            last = e.matmul(PB[6][:, 0:tn], ONES[:], sq[:, kc, 0:tn], start=(kc == 0), stop=(kc == KC - 1))
                        return last
                    P.add("pe", fn, reads=sqk + [("ones",)], writes=[("pb", 6)], name=f"lns2_{tag}{tt}")
                    def fn(e, tn=tn):
                        last = None
                        for kc in range(KC):
                            last = e.matmul(PB[7][:, 0:tn], ONES[:], CBF[:, kc, 0:tn], start=(kc == 0), stop=(kc == KC - 1))
                        return last
                    P.add("pe", fn, reads=cbk + [("ones",)], writes=[("pb", 7)], name=f"lns1_{tag}{tt}")
                    P.add("dve", lambda e, si=si, tn=tn: e.tensor_scalar(out=MEAN[si][:, 0:tn], in0=PB[7][:, 0:tn], scalar1=1.0 / D, scalar2=None, op0=ALU.mult),
                          reads=[("pb", 7)], writes=[("mean", si)], name=f"lnm_{tag}{tt}")
                    P.add("dve", lambda e, si=si, tn=tn: e.tensor_tensor(out=VAR[si][:, 0:tn], in0=MEAN[si][:, 0:tn], in1=MEAN[si][:, 0:tn], op=ALU.mult),
                          reads=[("mean", si)], writes=[("var", si)], name=f"lnmsq_{tag}{tt}")
                    P.add("dve", lambda e, si=si, tn=tn: e.scalar_tensor_tensor(out=VAR[si][:, 0:tn], in0=PB[6][:, 0:tn], scalar=1.0 / D, in1=VAR[si][:, 0:tn], op0=ALU.mult, op1=ALU.subtract),
                          reads=[("pb", 6), ("var", si)], writes=[("var", si)], name=f"lnvar_{tag}{tt}")
                    P.add("act", lambda e, si=si, tn=tn: e.activation(out=SD[si][:, 0:tn], in_=VAR[si][:, 0:tn], func=AF.Sqrt, bias=EPSC[LN_EPS]),
                          reads=[("var", si), ("epsc", 1)], writes=[("sd", si)], name=f"lnsd_{tag}{tt}")
                    P.add("dve", lambda e, si=si, tn=tn: e.reciprocal(out=RSTD[si][:, 0:tn], in_=SD[si][:, 0:tn]),
                          reads=[("sd", si)], writes=[("rstd", si)], name=f"lnrs_{tag}{tt}")
                    P.add("dve", lambda e, si=si, tn=tn: e.scalar_tensor_tensor(out=NMR[si][:, 0:tn], in0=MEAN[si][:, 0:tn], scalar=-1.0, in1=RSTD[si][:, 0:tn], op0=ALU.mult, op1=ALU.mult),
                          reads=[("mean", si), ("rstd", si)], writes=[("nmr", si)], name=f"lnnm_{tag}{tt}")
                    for kc in range(KC):
                        P.add("dve", lambda e, kc=kc, ta=ta, tn=tn, si=si: e.tensor_tensor(out=ca_ap(kc, ta, ta + tn), in0=ca_ap(kc, ta, ta + tn), in1=RSTD[si][:, 0:tn], op=ALU.mult),
                              reads=[("ca", kc, tt), ("rstd", si)], writes=[("ca", kc, tt)], name=f"lnmul_{tag}{tt}_{kc}")
                        P.add("dve", lambda e, kc=kc, ta=ta, tn=tn, si=si: e.tensor_tensor(out=ca_ap(kc, ta, ta + tn), in0=ca_ap(kc, ta, ta + tn), in1=NMR[si][:, 0:tn], op=ALU.add),
                              reads=[("ca", kc, tt), ("nmr", si)], writes=[("ca", kc, tt)], name=f"lnadd_{tag}{tt}_{kc}")
                    for kc in range(KC):
                        gcol = PV_LNG + l * KC + kc
                        bcol = PV_LNB + l * KC + kc
                        t_ = P.add("act", lambda e, kc=kc, ta=ta, tn=tn, gcol=gcol, bcol=bcol: e.activation(
                            out=caact_ap(kc, ta, ta + tn), in_=ca_ap(kc, ta, ta + tn), func=AF.Silu, scale=pvc(gcol), bias=pvc(bcol)),
                            reads=[("ca", kc, tt), ("pv",)], writes=[("caact", kc, tt)], extra=prev_H_tasks, name=f"lnsilu_{tag}{tt}_{kc}")
                        silu_tasks.append(t_)

                D_pe = []
                dunit = {}

                def get_unit(name, q):
                    if (name, q) not in dunit:
                        c0 = UW * q
                        if name == "wa":
                            dunit[(name, q)] = load_unit(w_a_out[l, :, c0: c0 + UW], 8, UW)
                        elif name == "wb":
                            dunit[(name, q)] = load_unit(w_b_out[l, :, c0: c0 + UW], 8, UW)
                        elif name == "ga":
                            dunit[(name, q)] = load_unit(w_in[l, :, OFF_GA + c0: OFF_GA + c0 + UW], 8, UW)
                        else:
                            dunit[(name, q)] = load_unit(w_in[l, :, OFF_GB + c0: OFF_GB + c0 + UW], 8, UW)
                    return dunit[(name, q)]

                ROWS = [(BBUF[0], [("bb", 0, tt) for tt in range(NT)]),
                        (BBUF[1], [("bb", 1, tt) for tt in range(NT)]),
                        (UX[0], [("u", 0, "h"), ("u", 0, "hs")] + [("u", 0, tt) for tt in range(NT)]),
                        (UX[1], [("u", 1, "h"), ("u", 1, "hs")] + [("u", 1, tt) for tt in range(NT)]),
                        (ZX[0], [("z", 0, "h"), ("z", 0, "hs")] + [("z", 0, tt) for tt in range(NT)]),
                        (ZX[1], [("z", 1, "h"), ("z", 1, "hs")] + [("z", 1, tt) for tt in range(NT)])]
                for xi, xr in enumerate(XROWS):
                    ROWS.append((xr, [("xrow", xi, tt) for tt in range(NT)]))
                for qi in range(N_SQROWS):
                    ROWS.append((SQ[qi][:].rearrange("p k n -> p (k n)").bitcast(F32), [("sq", qi, kc) for kc in range(KC)]))
                NROW = len(ROWS)

                def emit_D2(j):
                    uwb, ugb = get_unit("wb", j // UPC), get_unit("gb", j // UPC)
                    mc = (j % UPC) * 128
                    row, rkeys = ROWS[j % NROW]
                    for tt, (ta, tn) in enumerate(tiles):
                        b3, b4 = next_banks(2, RING_D)
                        accs = [(b3, tn, [(WS[uwb][:, kc, mc:mc + 128], BBCB[:, kc, ta:ta + tn]) for kc in range(KC)]),
                                (b4, tn, [(WS[ugb][:, kc, mc:mc + 128], XN[:, kc, ta:ta + tn]) for kc in range(KC)])]
                        D_pe.append(mm_job(accs, [("ws", uwb), ("ws", ugb)] + [(XNK, kc, tt) for kc in range(KC)] + [(BBK, kc, tt) for kc in range(KC)], f"D2_{tag}{j}_{tt}"))
                        si2 = sgc["n"] % NSG
                        sgc["n"] += 1
                        P.add("act", lambda e, si2=si2, b4=b4, tn=tn: e.activation(out=SG[si2][:, 0:tn], in_=PB[b4][:, 0:tn], func=AF.Sigmoid),
                              reads=[("pb", b4)], writes=[("sg", si2)], name=f"sgb_{tag}{j}_{tt}")
                        P.add("dve", lambda e, row=row, b3=b3, si2=si2, ta=ta, tn=tn: e.tensor_tensor(out=row[:, ta:ta + tn], in0=PB[b3][:, 0:tn], in1=SG[si2][:, 0:tn], op=ALU.mult),
                              reads=[("pb", b3), ("sg", si2)], writes=rkeys, name=f"m2_{tag}{j}_{tt}")

                def emit_D1(j):
                    uwa, uga = get_unit("wa", j // UPC), get_unit("ga", j // UPC)
                    mc = (j % UPC) * 128
                    row, rkeys = ROWS[j % NROW]
                    for tt, (ta, tn) in enumerate(tiles):
                        b1, b2 = next_banks(2, RING_D)
                        accs = [(b1, tn, [(WS[uwa][:, kc, mc:mc + 128], caact_ap(kc, ta, ta + tn)) for kc in range(KC)]),
                                (b2, tn, [(WS[uga][:, kc, mc:mc + 128], XN[:, kc, ta:ta + tn]) for kc in range(KC)])]
                        D_pe.append(mm_job(accs, [("ws", uwa), ("ws", uga)] + [(XNK, kc, tt) for kc in range(KC)] + [("caact", kc, tt) for kc in range(KC)], f"D1_{tag}{j}_{tt}",
                                           fine_keys=([("caact", kc, tt) for kc in range(KC)] if (j == 0 and FINE_D) else None)))
                        si = sgc["n"] % NSG
                        sgc["n"] += 1
                        P.add("act", lambda e, si=si, b2=b2, tn=tn: e.activation(out=SG[si][:, 0:tn], in_=PB[b2][:, 0:tn], func=AF.Sigmoid),
                              reads=[("pb", b2)], writes=[("sg", si)], name=f"sga_{tag}{j}_{tt}")
                        mi = (j * NT + tt) % 2
                        P.add("dve", lambda e, mi=mi, b1=b1, si=si, tn=tn: e.tensor_tensor(out=M1[mi][:, 0:tn], in0=PB[b1][:, 0:tn], in1=SG[si][:, 0:tn], op=ALU.mult),
                              reads=[("pb", b1), ("sg", si)], writes=[("m1", mi)], name=f"m1_{tag}{j}_{tt}")
                        P.add("dve", lambda e, mi=mi, j=j, row=row, ta=ta, tn=tn: e.tensor_tensor(out=merged_ap(j, ta, ta + tn), in0=row[:, ta:ta + tn], in1=M1[mi][:, 0:tn], op=ALU.add),
                              reads=rkeys + [("m1", mi)], writes=[("merged", j, tt)], extra=silu_tasks, name=f"mrg_{tag}{j}_{tt}")

                for j in range(min(NROW, KC)):
                    emit_D2(j)
                for j in range(KC):
                    emit_D1(j)
                    if j + NROW < KC:
                        emit_D2(j + NROW)

                P.add("act", lambda e: e.activation(out=WARM[:, 0:1], in_=EPS_T[:, 0:1], func=AF.Sqrt), reads=[("epsc", 0)], writes=[("warm",)], name=f"warmE_{tag}", prio=2)
                E_pe = []
                wo_units = [load_unit(w_o[l, :, UW * q: UW * q + UW], 8, UW) for q in range(KC // UPC)]
                for tt, (ta, tn) in enumerate(tiles):
                    for j in range(KC):
                        uo = wo_units[j // UPC]
                        mc = (j % UPC) * 128
                        (b1,) = next_banks(1)
                        accs = [(b1, tn, [(WS[uo][:, kc, mc:mc + 128], merged_ap(kc, ta, ta + tn)) for kc in range(KC)])]
                        E_pe.append(mm_job(accs, [("ws", uo)] + [("merged", kc, tt) for kc in range(KC)], f"E_{tag}{j}_{tt}"))
                        P.add("dve", lambda e, j=j, b1=b1, ta=ta, tn=tn, X=X: e.tensor_tensor(out=X[:, j, ta:ta + tn], in0=PB[b1][:, 0:tn], in1=X[:, j, ta:ta + tn], op=ALU.add),
                              reads=[("pb", b1), (XK, j, tt)], writes=[(XK, j, tt)], name=f"res1_{tag}{j}_{tt}")
                        norm_accum(j, tt, "r2" + tag)
                    norm_finish(tt, PV_G2 + l * KC, RMS_EPS, xn_out, XNK, "r2" + tag)

                alias_deps = D_pe + E_pe
                gunit = {}

                def g_units(q):
                    if q not in gunit:
                        c0 = UW * q
                        ncols = min(UW, DH - c0)
                        gunit[q] = (load_unit(w_gate[l, :, c0: c0 + ncols], 8, ncols), load_unit(w_up[l, :, c0: c0 + ncols], 8, ncols))
                    return gunit[q]

                def emit_G(hc, tt):
                    ufg, ufu = g_units(hc // UPC)
                    mc = (hc % UPC) * 128
                    ta, tn = tiles[tt]
                    b1, b2 = next_banks(2, RING_G)
                    accs = [(b1, tn, [(WS[ufg][:, kc, mc:mc + 128], XN[:, kc, ta:ta + tn]) for kc in range(KC)]),
                            (b2, tn, [(WS[ufu][:, kc, mc:mc + 128], XN[:, kc, ta:ta + tn]) for kc in range(KC)])]
                    mm_job(accs, [("ws", ufg), ("ws", ufu)] + [(XNK, kc, tt) for kc in range(KC)], f"G_{tag}{hc}_{tt}",
                           fine_keys=([(XNK, kc, tt) for kc in range(KC)] if (hc == 0 and FINE_G) else None))
                    si = sgc["n"] % NSG
                    sgc["n"] += 1
                    P.add("act", lambda e, si=si, b1=b1, tn=tn: e.activation(out=SG[si][:, 0:tn], in_=PB[b1][:, 0:tn], func=AF.Silu),
                          reads=[("pb", b1)], writes=[("sg", si)], name=f"fsilu_{tag}{hc}_{tt}")
                    P.add("dve", lambda e, si=si, b2=b2, hc=hc, ta=ta, tn=tn: e.tensor_tensor(out=f_ap(hc, ta, ta + tn), in0=PB[b2][:, 0:tn], in1=SG[si][:, 0:tn], op=ALU.mult),
                          reads=[("pb", b2), ("sg", si)], writes=[("f", hc, tt)], extra=alias_deps, name=f"f_{tag}{hc}_{tt}")

                for step in range(HC + GLAG):
                    if step < HC:
                        emit_G(step, 0)
                    if step - GLAG >= 0:
                        for tt in range(1, NT):
                            emit_G(step - GLAG, tt)

                if l == DEPTH - 1 and not st["last"]:
                    emit_xload(STS[sidx + 1], list(nb_state["cur"]))

                P.add("act", lambda e: e.activation(out=WARM[:, 0:1], in_=EPS_T[:, 0:1], func=AF.Sqrt), reads=[("epsc", 0)], writes=[("warm",)], name=f"warmH_{tag}", prio=2)
                H_pe = []
                for j in range(KC):
                    if j % UPC == 0:
                        c0 = UW * (j // UPC)
                        dunits = []
                        for r0 in range(0, HC, 8):
                            nk = min(8, HC - r0)
                            dunits.append((load_unit(w_down[l, r0 * 128:(r0 + nk) * 128, c0: c0 + UW], nk, UW), r0, nk))
                    mc = (j % UPC) * 128
                    for tt, (ta, tn) in enumerate(tiles):
                        (b1,) = next_banks(1)
                        mms = []
                        for (slot, r0, nk) in dunits:
                            for kk in range(nk):
                                mms.append((WS[slot][:, kk, mc:mc + 128], f_ap(r0 + kk, ta, ta + tn)))
                        accs = [(b1, tn, mms)]
                        H_pe.append(mm_job(accs, [("ws", u[0]) for u in dunits] + [("f", hc, tt) for hc in range(HC)], f"H_{tag}{j}_{tt}"))
                        P.add("dve", lambda e, j=j, b1=b1, ta=ta, tn=tn, X=X: e.tensor_tensor(out=X[:, j, ta:ta + tn], in0=PB[b1][:, 0:tn], in1=X[:, j, ta:ta + tn], op=ALU.add),
                              reads=[("pb", b1), (XK, j, tt)], writes=[(XK, j, tt)], name=f"res2_{tag}{j}_{tt}")
                        norm_accum(j, tt, (f"r1s{sidx}l{l + 1}" if l + 1 < DEPTH else f"fin{sidx}"))
                prev_H_tasks = H_pe
                for tt in range(NT):
                    if l + 1 < DEPTH:
                        norm_finish(tt, PV_G1 + (l + 1) * KC, RMS_EPS, xn_out, XNK, f"r1s{sidx}l{l + 1}")
                    else:
                        norm_finish(tt, PV_GF, RMS_EPS, lambda kc, a, b: ca_ap(kc, a, b), "yfm", f"fin{sidx}", extra_w=prev_H_tasks)

            t_ = P.add("sp", lambda e, g0=g0, n_st=n_st: e.dma_start(out=yT[:, :, g0:g0 + n_st], in_=sub_ap(CA, 0, [[TSMAX, KC], [1, n_st]])),
                       reads=[("yfm", kc, tt) for kc in range(KC) for tt in range(NT)], dma="d_y", name=f"sty{sidx}")
            out_dma_tasks.append(t_)
            prev_H_tasks = prev_H_tasks + [t_]

        t_ = P.add("sp", lambda e: e.dma_start(out=oA_d[:, :], in_=OUTA[:]), reads=[("outa", l, j) for l in range(DEPTH) for j in range(KC)], dma="d_oa", name="st_oa")
        out_dma_tasks.append(t_)
        t_ = P.add("sp", lambda e: e.dma_start(out=oB_d[:, :], in_=OUTB[:]), reads=[("outb", l, j) for l in range(DEPTH) for j in range(KC)], dma="d_ob", name="st_ob")
        out_dma_tasks.append(t_)
        P.add("sp", lambda e: None, extra=out_dma_tasks, name="final_wait")

        if SCHED:
            est_total = schedule(P, window=WINDOW, lat=SCHED_LAT)
            if DEBUG_SCHED:
                print(f"[kernel] scheduled estimate: {est_total / 1e3:.1f} us")
        dependents = set()
        for e_ in ENGS:
            for t in P.tasks[e_]:
                dependents.update(t.deps)
        dma_cnt = {}
        for e_ in ENGS:
            n = 0
            for t in P.tasks[e_]:
                if t.dma is not None:
                    dma_cnt[t.dma] = dma_cnt.get(t.dma, 0) + 1
                    t.ev_sem = sem(t.dma)
                    t.ev_val = 16 * dma_cnt[t.dma]
                    t.signal = True
                elif t in dependents:
                    n += 1
                    t.ev_sem = sem("p_" + e_)
                    t.ev_val = n
                    t.signal = True

        def emit(eng_name, e):
            waited = {}
            for t in P.tasks[eng_name]:
                need = {}
                for d in t.deps:
                    if d.dma is None and d.eng == eng_name and eng_name == "pe":
                        continue
                    assert d.signal, (t.name, d.name)
                    k = d.ev_sem.name
                    if need.get(k, (None, 0))[1] < d.ev_val:
                        need[k] = (d.ev_sem, d.ev_val)
                for k, (s_, v) in need.items():
                    if waited.get(k, 0) < v:
                        e.wait_ge(s_, v)
                        waited[k] = v
                inst = t.fn(e)
                if t.signal:
                    assert inst is not None, t.name
                    inst.then_inc(t.ev_sem, 16 if t.dma is not None else 1)

        block = es.enter_context(nc.Block())

        @block.tensor
        def _(e):
            emit("pe", e)

        @block.scalar
        def _(e):
            emit("act", e)

        @block.vector
        def _(e):
            emit("dve", e)

        @block.gpsimd
        def _(e):
            emit("pool", e)

        @block.sync
        def _(e):
            emit("sp", e)
    return nc


EPSC = {}


def _prep_core(c, x_prompt, x_sample, state_conv_a, state_conv_b, meta_tokens):
    toks = np.concatenate([meta_tokens, x_prompt[c], x_sample[2 * c], x_sample[2 * c + 1]], axis=0)
    xT = np.ascontiguousarray(toks.reshape(TT, KC, 128).transpose(2, 1, 0))
    sa = state_conv_a[:, 2 * c:2 * c + 2]
    shA = np.ascontiguousarray(sa.reshape(DEPTH, NSMP, HA, KC, 128).transpose(4, 0, 1, 3, 2)).reshape(128, -1)
    sbb = state_conv_b[:, 2 * c:2 * c + 2]
    shB = np.ascontiguousarray(sbb.reshape(DEPTH, NSMP, HB, KC, 128).transpose(4, 0, 1, 3, 2)).reshape(128, -1)
    return xT, shA, shB


def _vec_cols(v):
    lead = v.shape[:-1]
    a = v.reshape(-1, KC, 128)
    return np.ascontiguousarray(a.transpose(2, 0, 1)).reshape(128, -1)


_CACHE = {}


def kernel(x_prompt, x_sample, state_conv_a, state_conv_b, meta_tokens, norm1_g, w_in,
           conv_a_w, conv_a_b, ln_a_g, ln_a_b, w_a_out, conv_b_w, w_b_out, w_o, norm2_g,
           w_ffn_gate, w_ffn_up, w_ffn_down, final_norm_g):
    f32 = np.float32
    A = lambda a: np.ascontiguousarray(np.asarray(a, dtype=f32))
    x_prompt, x_sample, state_conv_a, state_conv_b, meta_tokens = map(A, (x_prompt, x_sample, state_conv_a, state_conv_b, meta_tokens))
    wa = np.asarray(conv_a_w, f32).reshape(DEPTH, KA, KC, 128).transpose(3, 0, 2, 1).reshape(128, -1)
    wb = np.asarray(conv_b_w, f32).reshape(DEPTH, KB, KC, 128).transpose(3, 0, 2, 1).reshape(128, -1)
    pvec = np.concatenate([
        _vec_cols(np.asarray(norm1_g, f32)), _vec_cols(np.asarray(conv_a_b, f32)), _vec_cols(np.asarray(ln_a_g, f32)),
        _vec_cols(np.asarray(ln_a_b, f32)), _vec_cols(np.asarray(norm2_g, f32)), _vec_cols(np.asarray(final_norm_g, f32)[None]),
        wa, wb], axis=1)
    assert pvec.shape == (128, NPV), pvec.shape
    pvec = np.ascontiguousarray(pvec)

    if "nc" not in _CACHE:
        _CACHE["nc"] = build()
    nc = _CACHE["nc"]
    shared = dict(pvec=pvec, w_in=A(w_in), w_a_out=A(w_a_out), w_b_out=A(w_b_out), w_o=A(w_o),
                  w_ffn_gate=A(w_ffn_gate), w_ffn_up=A(w_ffn_up), w_ffn_down=A(w_ffn_down))
    in_maps = []
    for c in range(NCORES):
        xT, shA, shB = _prep_core(c, x_prompt, x_sample, state_conv_a, state_conv_b, meta_tokens)
        m = dict(shared)
        m.update(xT=xT, shA=shA, shB=shB)
        in_maps.append(m)
    res = run_bass_kernel_spmd(nc, in_maps, core_ids=list(range(NCORES)))
    B = x_prompt.shape[0]
    y_prompt = np.empty((B, SEQ, D), f32)
    y_sample = np.empty((2 * NCORES, SL, D), f32)
    nap = np.empty((DEPTH, B, HA, D), f32)
    nbp = np.empty((DEPTH, B, HB, D), f32)
    nas = np.empty((DEPTH, 2 * NCORES, HA, D), f32)
    nbs = np.empty((DEPTH, 2 * NCORES, HB, D), f32)
    for c in range(NCORES):
        r = res.results[c]
        y = np.asarray(r["yT"]).reshape(128, KC, TT).transpose(2, 1, 0).reshape(TT, D)
        y_prompt[c] = y[NMETA:TP]
        y_sample[2 * c] = y[TP:TP + SL]
        y_sample[2 * c + 1] = y[TP + SL:TT]
        oa = np.asarray(r["oA"]).reshape(128, DEPTH, KC, 3, HA).transpose(1, 3, 4, 2, 0).reshape(DEPTH, 3, HA, D)
        ob = np.asarray(r["oB"]).reshape(128, DEPTH, KC, 3, HB).transpose(1, 3, 4, 2, 0).reshape(DEPTH, 3, HB, D)
        nap[:, c] = oa[:, 0]
        nas[:, 2 * c] = oa[:, 1]
        nas[:, 2 * c + 1] = oa[:, 2]
        nbp[:, c] = ob[:, 0]
        nbs[:, 2 * c] = ob[:, 1]
        nbs[:, 2 * c + 1] = ob[:, 2]
    return (y_prompt, y_sample, nap, nbp, nas, nbs)
```

```python
import numpy as np
from contextlib import ExitStack
import concourse.bass as bass
import concourse.mybir as mybir
from concourse.ap import AP
from concourse.bass_utils import run_bass_kernel_spmd

F32 = mybir.dt.float32
BF16 = mybir.dt.bfloat16
AF = mybir.ActivationFunctionType
ALU = mybir.AluOpType

NCORES = 8
D = 1024
KC = 8
NIN = 7168
DH = 2816
HC = 22
NMETA = 16
SEQ = 2048
TP = NMETA + SEQ
NSMP = 2
SL = 16
TT = TP + NSMP * SL
KA = 31
HA = 30
KB = 3
HB = 2
DEPTH = 2
OFF_AVAL, OFF_AGATE, OFF_BB, OFF_BC, OFF_BX, OFF_GA, OFF_GB = 0, 1024, 2048, 3072, 4096, 5120, 6144
RMS_EPS = 1e-6
LN_EPS = 1e-5

NPE = 19
NSLOT = 10
UW = 256
UPC = UW // 128
DIAG_ENG = "dve"
TLAG = 0
GLAG = 4
CLAG = 0
NMAX = 350
SCHED = True
PE_GHZ = 2.25
SCHED_Q = 1.0
SCHED_SLACK = 0.0
SCHED_LAT = 80.0
READY_TIE = True
RING_D = 6
FINE_A = 1
FINE_G = 1
FINE_D = 1
HOIST_LN = 1
N_XROWS = 0
N_SQROWS = 2
AC_ORDER = 0
RING_G = 8
NSG_CFG = 3
TL_LO, TL_HI = 1.0, 0.0
WINDOW = 128
DEBUG_SCHED = False
WINDOW = 128

PV_G1 = 0
PV_CAB = PV_G1 + DEPTH * KC
PV_LNG = PV_CAB + DEPTH * KC
PV_LNB = PV_LNG + DEPTH * KC
PV_G2 = PV_LNB + DEPTH * KC
PV_GF = PV_G2 + DEPTH * KC
PV_WA = PV_GF + KC
PV_WB = PV_WA + DEPTH * KC * KA
NPV = PV_WB + DEPTH * KC * KB


def make_sts():
    sizes = [699, 699, 698]
    sts = []
    g = 0
    for i, n in enumerate(sizes):
        last = i == len(sizes) - 1
        n_p = n - (NSMP * SL if last else 0)
        n0 = (n + 1) // 2
        tiles = [(0, n0), (n0, n - n0)]
        sts.append(dict(idx=i, g0=g, n=n, n_p=n_p, smp=last, tiles=tiles, first=(i == 0), last=last))
        g += n
    assert g == TT
    return sts


STS = make_sts()
TSMAX = max(s["n"] for s in STS)
UEXT = max(HA + s["n_p"] + (NSMP * (HA + SL) if s["smp"] else 0) for s in STS)
ZEXT = max(HB + s["n_p"] + (NSMP * (HB + SL) if s["smp"] else 0) for s in STS)

ENGS = ("pe", "act", "dve", "pool", "sp")


class Task:
    __slots__ = ("eng", "fn", "deps", "dma", "signal", "ev_sem", "ev_val", "name", "seq", "prio")

    def __init__(self, eng, fn, name):
        self.eng = eng
        self.fn = fn
        self.deps = set()
        self.dma = None
        self.signal = False
        self.ev_sem = None
        self.ev_val = 0
        self.name = name


class Prog:
    HI = ("sgb_", "m2_", "sq_", "ss_", "sd_", "rstd_", "nrm_", "lnsq", "lncb", "lns2", "lns1", "lnm_", "lnmsq", "lnvar", "lnsd", "lnrs", "lnnm", "lnmul", "lnadd", "lnsilu")

    def default_prio(self, name):
        return 0 if name.startswith(self.HI) else 1

    def __init__(self):
        self.tasks = {e: [] for e in ENGS}
        self.writers = {}
        self.readers = {}
        self.nseq = 0
        self._partial = {}

    def add(self, eng, fn, reads=(), writes=(), extra=(), dma=None, name="", prio=None):
        t = Task(eng, fn, name)
        t.prio = self.default_prio(name) if prio is None else prio
        t.dma = dma
        t.seq = self.nseq
        self.nseq += 1
        deps = set(x for x in extra if x is not None)
        for k in reads:
            deps.update(self.writers.get(k, ()))
        for k in writes:
            deps.update(self.readers.get(k, ()))
            deps.update(self.writers.get(k, ()))
        for k in reads:
            self.readers.setdefault(k, []).append(t)
        for k in writes:
            self.writers[k] = [t]
            self.readers[k] = []
        deps.discard(t)
        t.deps = deps
        self.tasks[eng].append(t)
        return t


def sub_ap(base, extra_off, dims):
    return AP(base.tensor, base.offset + extra_off, [list(base.ap[0])] + [list(d) for d in dims])


class Est:
    def __init__(self, eng):
        self.eng = eng
        self.ns = 0.0
        self.act_set = None
        self.dma_bytes = 0

    @staticmethod
    def _el(ap):
        n = 1
        for d in ap.shape[1:]:
            n *= d
        return n

    def matmul(self, out, lhsT, rhs, **k):
        self.ns += max(self._el(out), 64) / PE_GHZ + 2
        return self

    def activation(self, out, in_, func, bias=None, scale=None, **k):
        self.ns += 200 + 0.833 * self._el(out)
        if isinstance(scale, AP):
            self.ns += 90
        if func == AF.Sigmoid:
            self.act_set = "sig"
        elif func == AF.Silu:
            self.act_set = "silu"
        elif func == AF.Sqrt:
            self.act_set = "sqrt"
        return self

    def tensor_tensor(self, out, in0, in1, op, **k):
        self.ns += 140 + self._el(out) / 0.96
        return self

    def scalar_tensor_tensor(self, out, in0, scalar, in1, op0, op1, **k):
        self.ns += 140 + self._el(out) / 0.96
        return self

    def tensor_scalar(self, out, in0, scalar1, scalar2, op0, op1=None, **k):
        self.ns += 140 + self._el(out) / 0.96
        return self

    def reciprocal(self, out, in_):
        self.ns += 100 + 6.4 * self._el(out)
        return self

    def tensor_copy(self, out, in_, **k):
        self.ns += 150 + self._el(out)
        return self

    def memset(self, ap, c):
        self.ns += 100 + self._el(ap) * 0.5
        return self

    def affine_select(self, out, in_, **k):
        self.ns += 300
        return self

    def dma_start(self, out, in_, **k):
        self.dma_bytes += self._el(in_) * 4 * in_.shape[0]
        self.ns += 1050 if self.eng == "pool" else 150
        return self

    def then_inc(self, *a, **k):
        return self


def schedule(P, window=64, lat=150.0, dma_rate=300.0, dma_lat=2000.0):
    idx = 0
    for e in ENGS:
        pass
    allt = []
    for e in ENGS:
        allt.extend(P.tasks[e])
    allt.sort(key=lambda t: t.seq)
    est = {}
    for t in allt:
        st_ = Est(t.eng)
        t.fn(st_)
        est[t] = st_
    ndeps = {t: len(t.deps) for t in allt}
    users = {t: [] for t in allt}
    for t in allt:
        for d in t.deps:
            users[d].append(t)
    ready = {t: 0.0 for t in allt if ndeps[t] == 0}
    pending = {e: list(P.tasks[e]) for e in ENGS}
    free = {e: 0.0 for e in ENGS}
    dma_free = 0.0
    act_cur = None
    partial = {}
    pe_tl = []
    order = {e: [] for e in ENGS}
    finish = {}
    remaining = len(allt)
    while remaining:
        best = None
        for e in ENGS:
            pl = pending[e]
            cands = []
            mn = None
            for t in pl[:window]:
                r = ready.get(t)
                if r is None:
                    continue
                start = max(r, free[e])
                pen = 0.0
                if e == "act" and est[t].act_set is not None and est[t].act_set != act_cur:
                    pen = 1300.0
                cmp_start = start + (pen if t.prio >= 1 else 0.0)
                cands.append((cmp_start, start + pen, t, r))
                if mn is None or cmp_start < mn:
                    mn = cmp_start
            if not cands:
                continue
            pick = None
            for (cs, start, t, r) in cands:
                if cs <= mn + SCHED_SLACK:
                    k2 = (t.prio, cs, (r if READY_TIE else 0), t.seq)
                    if pick is None or k2 < pick[0]:
                        pick = (k2, cs, start, t)
            key = (pick[1], pick[3].prio, pick[3].seq, pick[2])
            if best is None or key < best[0]:
                best = (key, e, pick[3])
        assert best is not None, "scheduler deadlock"
        start = best[0][3]
        e, t = best[1], best[2]
        es_ = est[t]
        if t.dma is not None:
            issue_end = start + es_.ns
            x0 = max(issue_end, dma_free)
            dma_free = x0 + es_.dma_bytes / dma_rate
            fin = dma_free + dma_lat
            free[e] = issue_end
        else:
            fin = start + es_.ns
            free[e] = fin
            if e == "act" and es_.act_set is not None:
                act_cur = es_.act_set
        finish[t] = fin
        if e == "pe":
            pe_tl.append((start, fin, t.name))
        if DEBUG_SCHED and TL_LO <= start / 1e3 <= TL_HI:
            print(f"  [s] {e:4s} {t.name:22s} start={start/1e3:8.2f} fin={fin/1e3:8.2f}")
        order[e].append(t)
        pending[e].remove(t)
        remaining -= 1
        for u in users[t]:
            partial[u] = max(partial.get(u, 0.0), fin + lat)
            ndeps[u] -= 1
            if ndeps[u] == 0:
                ready[u] = partial[u]
    if DEBUG_SCHED:
        gaps = {}
        prev = 0.0
        for (st0, fn0, nm) in pe_tl:
            ph = nm.split("_")[0]
            if st0 > prev:
                gaps[ph] = gaps.get(ph, 0.0) + (st0 - prev)
            prev = max(prev, fn0)
        print("[sched] PE gaps before (us):", {k: round(v / 1e3, 1) for k, v in sorted(gaps.items(), key=lambda kv: -kv[1])})
        busy = {e: sum(est[t].ns for t in order[e]) for e in ENGS}
        print("[sched] busy us:", {e: round(busy[e] / 1e3, 1) for e in ENGS}, "free:", {e: round(free[e] / 1e3, 1) for e in ENGS}, "dma_free", round(dma_free / 1e3, 1))
    P.tasks = order
    return max(finish.values())


def build():
    nc = bass.Bass("TRN2", target_bir_lowering=False)

    def din(name, shape):
        return nc.dram_tensor(name, list(shape), F32, kind="ExternalInput").ap()

    def dout(name, shape):
        return nc.dram_tensor(name, list(shape), F32, kind="ExternalOutput").ap()

    xT = din("xT", [128, KC, TT])
    pvec_d = din("pvec", [128, NPV])
    shA_d = din("shA", [128, DEPTH * NSMP * KC * HA])
    shB_d = din("shB", [128, DEPTH * NSMP * KC * HB])
    w_in = din("w_in", [DEPTH, D, NIN])
    w_a_out = din("w_a_out", [DEPTH, D, D])
    w_b_out = din("w_b_out", [DEPTH, D, D])
    w_o = din("w_o", [DEPTH, D, D])
    w_gate = din("w_ffn_gate", [DEPTH, D, DH])
    w_up = din("w_ffn_up", [DEPTH, D, DH])
    w_down = din("w_ffn_down", [DEPTH, DH, D])
    yT = dout("yT", [128, KC, TT])
    oA_d = dout("oA", [128, DEPTH * KC * 3 * HA])
    oB_d = dout("oB", [128, DEPTH * KC * 3 * HB])

    P = Prog()
    es = ExitStack()
    with es:
        def sb(name, shape, dt):
            return es.enter_context(nc.sbuf_tensor(name, list(shape), dt))

        BUF2 = [sb(f"XBUF{i}", [128, KC * TSMAX * 4], mybir.dt.uint8) for i in range(2)]

        def st_views(par):
            Xv = BUF2[par][:].bitcast(F32).rearrange("p (k t) -> p k t", k=KC)
            o = BUF2[1 - par][:]
            XNv = sub_ap(o, 0, [[1, KC * TSMAX * 2]]).bitcast(BF16).rearrange("p (k t) -> p k t", k=KC)
            BBv = sub_ap(o, KC * TSMAX * 2, [[1, KC * TSMAX * 2]]).bitcast(BF16).rearrange("p (k t) -> p k t", k=KC)
            return Xv, XNv, BBv
        BIG = sb("BIG", [128, KC * TSMAX * 4 + KC * TSMAX * 2], mybir.dt.uint8)
        big_all = BIG[:]
        CA = sub_ap(big_all, 0, [[1, KC * TSMAX * 4]]).bitcast(F32)
        CAACT = sub_ap(big_all, KC * TSMAX * 4, [[1, KC * TSMAX * 2]]).bitcast(BF16)
        MERGED = sub_ap(big_all, 0, [[1, KC * TSMAX * 2]]).bitcast(BF16)
        FF = sub_ap(big_all, 0, [[1, HC * TSMAX * 2]]).bitcast(BF16)
        assert HC * TSMAX * 2 <= KC * TSMAX * 6

        def ca_ap(kc, a, b):
            return CA[:, kc * TSMAX + a: kc * TSMAX + b]

        def ca3_ap(a, b):
            return sub_ap(CA, a, [[TSMAX, KC], [1, b - a]])

        def caact_ap(kc, a, b):
            return CAACT[:, kc * TSMAX + a: kc * TSMAX + b]

        def merged_ap(kc, a, b):
            return MERGED[:, kc * TSMAX + a: kc * TSMAX + b]

        def f_ap(hc, a, b):
            return FF[:, hc * TSMAX + a: hc * TSMAX + b]

        UX = [sb(f"UX{i}", [128, UEXT], F32) for i in range(2)]
        UB = [sb(f"UB{i}", [128, UEXT], BF16) for i in range(2)]
        ZX = [sb(f"ZX{i}", [128, ZEXT], F32) for i in range(2)]
        BBUF = [sb(f"BBUF{i}", [128, TSMAX], F32) for i in range(2)]
        ZB = [sb(f"ZB{i}", [128, ZEXT], BF16) for i in range(2)]
        SQ = [sb(f"SQ{i}", [128, KC, NMAX], BF16) for i in range(2)]
        CBF = sb("CBF", [128, KC, NMAX], BF16)
        NST_ = 2
        SD = [sb(f"SD{i}", [128, NMAX], F32) for i in range(NST_)]
        RSTD = [sb(f"RSTD{i}", [128, NMAX], F32) for i in range(NST_)]
        MEAN = [sb(f"MEAN{i}", [128, NMAX], F32) for i in range(NST_)]
        VAR = [sb(f"VAR{i}", [128, NMAX], F32) for i in range(NST_)]
        NMR = [sb(f"NMR{i}", [128, NMAX], F32) for i in range(NST_)]
        NSG = NSG_CFG
        SG = [sb(f"SG{i}", [128, NMAX], F32) for i in range(NSG)]
        M1 = [sb(f"M1_{i}", [128, NMAX], F32) for i in range(2)]
        XROWS = [sb(f"XROW{i}", [128, TSMAX], F32) for i in range(N_XROWS)]
        DIAG = [sb(f"DIAG{i}", [128, max(NPE, 1), 128], BF16) for i in range(2)]
        DIAG3 = [sb(f"DIAG3_{i}", [128, KB, 128], BF16) for i in range(2)]
        WS = [sb(f"WS{i}", [128, 8, UW], BF16) for i in range(NSLOT)]
        PV = sb("PV", [128, NPV], F32)
        IDB = sb("IDB", [128, 128], BF16)
        IDF = sb("IDF", [128, 128], F32)
        EPS_T = sb("EPS_T", [128, 2], F32)
        WARM = sb("WARM", [128, 2], F32)
        EPSC[RMS_EPS] = EPS_T[:, 0:1]
        EPSC[LN_EPS] = EPS_T[:, 1:2]
        ONES = sb("ONES", [128, 128], BF16)
        SHA = sb("SHA", [128, DEPTH * NSMP * KC * HA], F32)
        SHB = sb("SHB", [128, DEPTH * NSMP * KC * HB], F32)
        HISTA = sb("HISTA", [128, DEPTH * KC * HA], F32)
        HISTZ = sb("HISTZ", [128, DEPTH * KC * HB], F32)
        OUTA = sb("OUTA", [128, DEPTH * KC * 3 * HA], F32)
        OUTB = sb("OUTB", [128, DEPTH * KC * 3 * HB], F32)
        PB = [es.enter_context(nc.psum_tensor(f"PB{i}", [128, 512], F32)) for i in range(8)]

        sems = {}

        def sem(name):
            if name not in sems:
                sems[name] = es.enter_context(nc.semaphore(name))
            return sems[name]

        for e_ in ENGS:
            sem("p_" + e_)

        t_pv = P.add("sp", lambda e: e.dma_start(out=PV[:], in_=pvec_d[:, :]), writes=[("pv",)], dma="d_pv", name="ld_pv")
        P.add("sp", lambda e: e.dma_start(out=SHA[:], in_=shA_d[:, :]), writes=[("sha",)], dma="d_sha", name="ld_sha")
        P.add("sp", lambda e: e.dma_start(out=SHB[:], in_=shB_d[:, :]), writes=[("shb",)], dma="d_shb", name="ld_shb")

        P.add("pool", lambda e: e.memset(IDF[:], 0.0), writes=[("idf",)], name="idf0")
        P.add("pool", lambda e: e.memset(ONES[:], 1.0), writes=[("ones",)], name="ones")
        P.add("pool", lambda e: e.memset(EPS_T[:, 0:1], RMS_EPS), writes=[("epsc", 0)], name="eps0")
        P.add("pool", lambda e: e.memset(EPS_T[:, 1:2], LN_EPS), writes=[("epsc", 1)], name="eps1")
        P.add("pool", lambda e: e.affine_select(out=IDF[:], in_=IDF[:], pattern=[[-1, 128]], compare_op=ALU.not_equal,
                                                fill=1.0, base=0, channel_multiplier=1),
              reads=[("idf",)], writes=[("idf",)], name="ident")
        P.add("pool", lambda e: e.tensor_copy(out=IDB[:], in_=IDF[:]), reads=[("idf",)], writes=[("idb",)], name="identb")

        def pvc(col):
            return PV[:, col:col + 1]

        wstate = dict(n=0)
        bank_state = dict(n=0)

        def load_unit(src_ap, nk, ncols):
            n = wstate["n"]
            wstate["n"] += 1
            slot = n % NSLOT
            key = ("ws", slot)
            src = src_ap.rearrange("(kc p) n -> p kc n", p=128)
            dst = WS[slot][:, 0:nk, 0:ncols]
            P.add("pool", lambda e, dst=dst, src=src: e.dma_start(out=dst, in_=src), writes=[key],
                  dma=f"d_ws{slot}", name=f"ldw{n}")
            return slot

        def next_banks(k, ring=6):
            out = []
            for _ in range(k):
                out.append(bank_state["n"] % ring)
                bank_state["n"] += 1
            return out

        nb_state = dict(cur=[], prev=[])
        fin_state = dict(cur=[], prev=[])

        def mm_job(accs, reads, name, fine_keys=None):
            t0_ = _mm_job(accs, reads, name, fine_keys)
            if any(str(k[0]).startswith(("xn", "bbcb")) for k in reads):
                nb_state["cur"].append(t0_)
            return t0_

        def _mm_job(accs, reads, name, fine_keys=None):
            def fn(e, accs=accs):
                last = None
                for (bank, n, mms) in accs:
                    for i, (l_ap, r_ap) in enumerate(mms):
                        last = e.matmul(PB[bank][:, 0:n], l_ap, r_ap, start=(i == 0), stop=(i == len(mms) - 1))
                return last
            if fine_keys is not None:
                K_ = len(accs[0][2])
                assert all(len(a[2]) == K_ for a in accs) and len(fine_keys) == K_
                base_reads = [r for r in reads if r not in set(fine_keys)]
                t_ = None
                for i in range(K_):
                    def fni(e, accs=accs, i=i, K_=K_):
                        last = None
                        for (bank, n, mms) in accs:
                            l_ap, r_ap = mms[i]
                            last = e.matmul(PB[bank][:, 0:n], l_ap, r_ap, start=(i == 0), stop=(i == K_ - 1))
                        return last
                    t_ = P.add("pe", fni, reads=base_reads + [fine_keys[i]], writes=[("pb", a[0]) for a in accs], name=f"{name}k{i}")
                return t_
            return P.add("pe", fn, reads=reads, writes=[("pb", a[0]) for a in accs], name=name)

        prev_H_tasks = []
        out_dma_tasks = []
        sgc = dict(n=0)
        abc = dict(n=0)
        stc = dict(n=0)

        for st in STS:
            g0, n_st, n_p, smp = st["g0"], st["n"], st["n_p"], st["smp"]
            tiles = st["tiles"]
            NT = len(tiles)
            sidx = st["idx"]

            par = sidx % 2
            X, XN, BBCB = st_views(par)
            XK, XNK, BBK = f"x{par}", f"xn{par}", f"bbcb{par}"
            if sidx > 0:
                nb_state["prev"], nb_state["cur"] = nb_state["cur"], []
                fin_state["prev"], fin_state["cur"] = fin_state["cur"], []

            def emit_xload(st2, deps):
                p2 = st2["idx"] % 2
                X2 = st_views(p2)[0]
                for kc in range(KC):
                    P.add("sp", lambda e, g2=st2["g0"], n2=st2["n"], kc=kc, X2=X2: e.dma_start(out=X2[:, kc, 0:n2], in_=xT[:, kc, g2:g2 + n2]),
                          writes=[(f"x{p2}", kc, tt) for tt in range(len(st2["tiles"]))], extra=deps, dma=f"d_x{kc}", name=f"ldx{st2['idx']}_{kc}")

            if sidx == 0:
                emit_xload(st, [])

            def ext_parts(a, b, H, shift):
                parts = []
                pa, pb_ = a, min(b, n_p)
                if pb_ > pa:
                    parts.append(("p", pa, pb_ - pa, pa + shift))
                if b > n_p:
                    assert smp and a <= n_p and b == n_p + NSMP * SL
                    parts.append(("s", n_p, NSMP * SL, H + n_p + shift))
                return parts

            def ext_ap(buf, part, H):
                kind, lo, n, off = part
                if kind == "p":
                    return buf[:, off:off + n]
                return sub_ap(buf[:], off, [[H + SL, NSMP], [1, SL]])

            def cmp_ap(ap2d_fn, part):
                kind, lo, n, off = part
                base = ap2d_fn(lo, lo + n)
                if kind == "p":
                    return base
                return sub_ap(base, 0, [[SL, NSMP], [1, SL]])

            def norm_accum(j, tt, tagn):
                ta, tn = tiles[tt]
                sq = SQ[tt]
                bank = 6 + tt
                t_sq = P.add("act", lambda e, sq=sq, j=j, ta=ta, tn=tn, X=X: e.activation(out=sq[:, j, 0:tn], in_=X[:, j, ta:ta + tn], func=AF.Square),
                                 reads=[(XK, j, tt)], writes=[("sq", tt, j)], name=f"sq_{tagn}{tt}_{j}")
                if tagn.startswith("fin"):
                    fin_state["cur"].append(t_sq)
                P.add("pe", lambda e, sq=sq, j=j, tn=tn, bank=bank: e.matmul(PB[bank][:, 0:tn], ONES[:], sq[:, j, 0:tn], start=(j == 0), stop=(j == KC - 1)),
                      reads=[("sq", tt, j), ("ones",)], writes=[("pb", bank)], name=f"ss_{tagn}{tt}_{j}")

            def norm_finish(tt, gcol, eps, out_fn, out_keyname, tagn, extra_w=()):
                ta, tn = tiles[tt]
                bank = 6 + tt
                si = tt
                P.add("act", lambda e, tn=tn, bank=bank, si=si, eps=eps: e.activation(out=SD[si][:, 0:tn], in_=PB[bank][:, 0:tn], func=AF.Sqrt, scale=1.0 / D, bias=EPSC[eps]),
                      reads=[("pb", bank), ("epsc", 0)], writes=[("sd", si)], name=f"sd_{tagn}{tt}", prio=0.1 * tt)
                P.add("dve", lambda e, tn=tn, si=si: e.reciprocal(out=RSTD[si][:, 0:tn], in_=SD[si][:, 0:tn]),
                      reads=[("sd", si)], writes=[("rstd", si)], name=f"rstd_{tagn}{tt}", prio=0.1 * tt)
                for kc in range(KC):
                    P.add("dve", lambda e, kc=kc, ta=ta, tn=tn, si=si, X=X: e.scalar_tensor_tensor(
                        out=out_fn(kc, ta, ta + tn), in0=X[:, kc, ta:ta + tn], scalar=pvc(gcol + kc),
                        in1=RSTD[si][:, 0:tn], op0=ALU.mult, op1=ALU.mult),
                        reads=[(XK, kc, tt), ("rstd", si), ("pv",)], writes=[(out_keyname, kc, tt)], extra=extra_w, name=f"nrm_{tagn}{tt}_{kc}", prio=0.1 * tt)
                    if tagn.startswith("fin"):
                        fin_state["cur"].append(P.tasks["dve"][-1])

            xn_out = lambda kc, a, b, XN=XN: XN[:, kc, a:b]
            for j in range(KC):
                for tt in range(NT):
                    norm_accum(j, tt, f"r1s{sidx}l0")
            for tt in range(NT):
                norm_finish(tt, PV_G1, RMS_EPS, xn_out, XNK, f"r1s{sidx}l0", extra_w=list(fin_state["prev"]))

            for l in range(DEPTH):
                tag = f"s{sidx}l{l}"

                units = {}
                silu_tasks = []
                xnk = lambda tt: [(XNK, kc, tt) for kc in range(KC)]

                def emit_A(j):
                    jb = j % 2
                    jd = j % 2
                    if j % UPC == 0:
                        c0 = UW * (j // UPC)
                        units["val"] = load_unit(w_in[l, :, OFF_AVAL + c0: OFF_AVAL + c0 + UW], 8, UW)
                        units["gate"] = load_unit(w_in[l, :, OFF_AGATE + c0: OFF_AGATE + c0 + UW], 8, UW)
                    uval, ugate = units["val"], units["gate"]
                    mc = (j % UPC) * 128
                    ux, ub = UX[jb], UB[jb]
                    if st["first"]:
                        P.add("act", lambda e, ux=ux: e.activation(out=ux[:, 0:HA], in_=PV[:, 0:HA], func=AF.Copy, scale=0.0),
                              reads=[("pv",)], writes=[("u", jb, "h")], name=f"uh0_{tag}{j}")
                    else:
                        hoff = (l * KC + j) * HA
                        P.add("act", lambda e, ux=ux, hoff=hoff: e.activation(out=ux[:, 0:HA], in_=HISTA[:, hoff:hoff + HA], func=AF.Copy),
                              reads=[("hista", l, j)], writes=[("u", jb, "h")], name=f"uh_{tag}{j}")
                    if smp:
                        src = sub_ap(SHA[:], (l * NSMP * KC + j) * HA, [[KC * HA, NSMP], [1, HA]])
                        dst = sub_ap(ux[:], HA + n_p, [[HA + SL, NSMP], [1, HA]])
                        P.add("act", lambda e, src=src, dst=dst: e.activation(out=dst, in_=src, func=AF.Copy),
                              reads=[("sha",)], writes=[("u", jb, "hs")], name=f"uhs_{tag}{j}")
                    wcol = PV_WA + (l * KC + j) * KA
                    bcol = PV_CAB + l * KC + j
                    if NPE > 0:
                        in0 = sub_ap(IDB[:], 0, [[0, NPE], [1, 128]])
                        in1 = sub_ap(PV[:], wcol, [[1, NPE], [0, 128]])
                        P.add(DIAG_ENG, lambda e, jd=jd, in0=in0, in1=in1: e.tensor_tensor(out=DIAG[jd][:, 0:NPE, :], in0=in0, in1=in1, op=ALU.mult),
                              reads=[("idb",), ("pv",)], writes=[("diag", jd)], name=f"diag_{tag}{j}")
                    def a_tile(tt):
                        ta, tn = tiles[tt]
                        bv, bg = next_banks(2)
                        accs = [(bv, tn, [(WS[uval][:, kc, mc:mc + 128], XN[:, kc, ta:ta + tn]) for kc in range(KC)]),
                                (bg, tn, [(WS[ugate][:, kc, mc:mc + 128], XN[:, kc, ta:ta + tn]) for kc in range(KC)])]
                        mm_job(accs, [("ws", uval), ("ws", ugate)] + xnk(tt), f"A_{tag}{j}_{tt}", fine_keys=(xnk(tt) if (j == 0 and FINE_A) else None))
                        si = sgc["n"] % NSG
                        sgc["n"] += 1
                        P.add("act", lambda e, si=si, bg=bg, tn=tn: e.activation(out=SG[si][:, 0:tn], in_=PB[bg][:, 0:tn], func=AF.Sigmoid),
                              reads=[("pb", bg)], writes=[("sg", si)], name=f"sig_{tag}{j}_{tt}")
                        parts = ext_parts(ta, ta + tn, HA, HA)
                        def fn(e, parts=parts, ux=ux, bv=bv, si=si, ta=ta):
                            last = None
                            for part in parts:
                                o = ext_ap(ux, part, HA)
                                i0 = cmp_ap(lambda a, b: PB[bv][:, a - ta:b - ta], part)
                                i1 = cmp_ap(lambda a, b: SG[si][:, a - ta:b - ta], part)
                                last = e.tensor_tensor(out=o, in0=i0, in1=i1, op=ALU.mult)
                            return last
                        P.add("dve", fn, reads=[("pb", bv), ("sg", si)], writes=[("u", jb, tt)], name=f"glu_{tag}{j}_{tt}")
                    def a_post():
                        ukeys = [("u", jb, "h")] + ([("u", jb, "hs")] if smp else []) + [("u", jb, tt) for tt in range(NT)]
                        LU = HA + n_p + (NSMP * (HA + SL) if smp else 0)
                        if not st["last"]:
                            hoff = (l * KC + j) * HA
                            P.add("act", lambda e, ux=ux, hoff=hoff, n_p=n_p: e.activation(out=HISTA[:, hoff:hoff + HA], in_=ux[:, n_p:n_p + HA], func=AF.Copy),
                                  reads=ukeys, writes=[("hista", l, j)], name=f"hst_{tag}{j}")
                        else:
                            ooff = (l * KC + j) * 3 * HA
                            def fn(e, ux=ux, ooff=ooff, n_p=n_p):
                                e.activation(out=OUTA[:, ooff:ooff + HA], in_=ux[:, n_p:n_p + HA], func=AF.Copy)
                                src = sub_ap(ux[:], HA + n_p + SL, [[HA + SL, NSMP], [1, HA]])
                                dst = sub_ap(OUTA[:], ooff + HA, [[HA, NSMP], [1, HA]])
                                return e.activation(out=dst, in_=src, func=AF.Copy)
                            P.add("act", fn, reads=ukeys, writes=[("outa", l, j)], name=f"outa_{tag}{j}")
                        cakeys = [("ca", j, tt) for tt in range(NT)]
                        if NPE > 0:
                            P.add("act", lambda e, ux=ux, ub=ub, LU=LU: e.activation(out=ub[:, 0:LU], in_=ux[:, 0:LU], func=AF.Copy),
                                  reads=ukeys, writes=[("ubf", jb)], name=f"ubf_{tag}{j}")
                            for tt, (ta, tn) in enumerate(tiles):
                                bank = 6 + (abc["n"] % 2)
                                abc["n"] += 1
                                parts = ext_parts(ta, ta + tn, HA, 0)
                                def fn(e, parts=parts, ub=ub, bank=bank, jd=jd, ta=ta):
                                    last = None
                                    for part in parts:
                                        o = cmp_ap(lambda a, b: PB[bank][:, a - ta:b - ta], part)
                                        for k in range(NPE):
                                            p2 = (part[0], part[1], part[2], part[3] + k)
                                            last = e.matmul(o, DIAG[jd][:, k, :], ext_ap(ub, p2, HA), start=(k == 0), stop=(k == NPE - 1))
                                    return last
                                P.add("pe", fn, reads=[("ubf", jb), ("diag", jd)], writes=[("pb", bank)], name=f"cvpe_{tag}{j}_{tt}")
                                P.add("act", lambda e, j=j, ta=ta, tn=tn, bank=bank, bcol=bcol: e.activation(
                                    out=ca_ap(j, ta, ta + tn), in_=PB[bank][:, 0:tn], func=AF.Identity, bias=pvc(bcol)),
                                    reads=[("pb", bank), ("pv",)], writes=[("ca", j, tt)], extra=prev_H_tasks, name=f"cvev_{tag}{j}_{tt}")
                        for k in range(NPE, KA):
                            partsr = ext_parts(0, n_st, HA, k)
                            def fn(e, partsr=partsr, ux=ux, j=j, k=k, first=(k == 0), bcol=bcol, wcol=wcol):
                                last = None
                                for part in partsr:
                                    i0 = ext_ap(ux, part, HA)
                                    o = cmp_ap(lambda a, b: ca_ap(j, a, b), part)
                                    if first:
                                        last = e.tensor_scalar(out=o, in0=i0, scalar1=pvc(wcol + k), scalar2=pvc(bcol), op0=ALU.mult, op1=ALU.add)
                                    else:
                                        last = e.scalar_tensor_tensor(out=o, in0=i0, scalar=pvc(wcol + k), in1=o, op0=ALU.mult, op1=ALU.add)
                                return last
                            rk = ukeys + [("pv",)] + ([] if k == 0 else cakeys)
                            P.add("dve", fn, reads=rk, writes=cakeys, extra=(prev_H_tasks if k == 0 else ()), name=f"tap_{tag}{j}_{k}")
                        if HOIST_LN:
                            ta0, tn0 = tiles[0]
                            P.add("act", lambda e, j=j, ta0=ta0, tn0=tn0: e.activation(out=SQ[0][:, j, 0:tn0], in_=ca_ap(j, ta0, ta0 + tn0), func=AF.Square),
                                  reads=[("ca", j, 0)], writes=[("sq", 0, j)], name=f"lnsqh_{tag}{j}")
                            P.add("act", lambda e, j=j, ta0=ta0, tn0=tn0: e.activation(out=CBF[:, j, 0:tn0], in_=ca_ap(j, ta0, ta0 + tn0), func=AF.Copy),
                                  reads=[("ca", j, 0)], writes=[("cbf", j)], name=f"lncbh_{tag}{j}")
                    return a_tile, a_post

                def emit_C(j):
                    jb = j % 2
                    jd = j % 2
                    if j % UPC == 0:
                        c0 = UW * (j // UPC)
                        units["bc"] = load_unit(w_in[l, :, OFF_BC + c0: OFF_BC + c0 + UW], 8, UW)
                        units["bx"] = load_unit(w_in[l, :, OFF_BX + c0: OFF_BX + c0 + UW], 8, UW)
                        units["bb"] = load_unit(w_in[l, :, OFF_BB + c0: OFF_BB + c0 + UW], 8, UW)
                    ubc, ubx, ubb = units["bc"], units["bx"], units["bb"]
                    mc = (j % UPC) * 128
                    zx, zb = ZX[jb], ZB[jb]
                    if st["first"]:
                        P.add("act", lambda e, zx=zx: e.activation(out=zx[:, 0:HB], in_=PV[:, 0:HB], func=AF.Copy, scale=0.0),
                              reads=[("pv",)], writes=[("z", jb, "h")], name=f"zh0_{tag}{j}")
                    else:
                        hoffz = (l * KC + j) * HB
                        P.add("act", lambda e, zx=zx, hoffz=hoffz: e.activation(out=zx[:, 0:HB], in_=HISTZ[:, hoffz:hoffz + HB], func=AF.Copy),
                              reads=[("histz", l, j)], writes=[("z", jb, "h")], name=f"zh_{tag}{j}")
                    if smp:
                        src = sub_ap(SHB[:], (l * NSMP * KC + j) * HB, [[KC * HB, NSMP], [1, HB]])
                        dst = sub_ap(zx[:], HB + n_p, [[HB + SL, NSMP], [1, HB]])
                        P.add("act", lambda e, src=src, dst=dst: e.activation(out=dst, in_=src, func=AF.Copy),
                              reads=[("shb",)], writes=[("z", jb, "hs")], name=f"zhs_{tag}{j}")
                    wbcol = PV_WB + (l * KC + j) * KB
                    in0 = sub_ap(IDB[:], 0, [[0, KB], [1, 128]])
                    in1 = sub_ap(PV[:], wbcol, [[1, KB], [0, 128]])
                    P.add(DIAG_ENG, lambda e, jd=jd, in0=in0, in1=in1: e.tensor_tensor(out=DIAG3[jd][:, :, :], in0=in0, in1=in1, op=ALU.mult),
                          reads=[("idb",), ("pv",)], writes=[("diag3", jd)], name=f"diag3_{tag}{j}")
                    def c_tile(tt):
                        ta, tn = tiles[tt]
                        b1, b2 = next_banks(2)
                        accs = [(b1, tn, [(WS[ubc][:, kc, mc:mc + 128], XN[:, kc, ta:ta + tn]) for kc in range(KC)]),
                                (b2, tn, [(WS[ubx][:, kc, mc:mc + 128], XN[:, kc, ta:ta + tn]) for kc in range(KC)])]
                        mm_job(accs, [("ws", ubc), ("ws", ubx)] + xnk(tt), f"C1_{tag}{j}_{tt}")
                        si = sgc["n"] % NSG
                        sgc["n"] += 1
                        P.add("act", lambda e, si=si, b2=b2, tn=tn: e.activation(out=SG[si][:, 0:tn], in_=PB[b2][:, 0:tn], func=AF.Copy),
                              reads=[("pb", b2)], writes=[("sg", si)], name=f"bx_{tag}{j}_{tt}")
                        parts = ext_parts(ta, ta + tn, HB, HB)
                        def fn(e, parts=parts, zx=zx, b1=b1, si=si, ta=ta):
                            last = None
                            for part in parts:
                                o = ext_ap(zx, part, HB)
                                i0 = cmp_ap(lambda a, b: PB[b1][:, a - ta:b - ta], part)
                                i1 = cmp_ap(lambda a, b: SG[si][:, a - ta:b - ta], part)
                                last = e.tensor_tensor(out=o, in0=i0, in1=i1, op=ALU.mult)
                            return last
                        P.add("dve", fn, reads=[("pb", b1), ("sg", si)], writes=[("z", jb, tt)], name=f"zb_{tag}{j}_{tt}")
                        (b3,) = next_banks(1)
                        accs = [(b3, tn, [(WS[ubb][:, kc, mc:mc + 128], XN[:, kc, ta:ta + tn]) for kc in range(KC)])]
                        mm_job(accs, [("ws", ubb)] + xnk(tt), f"C2_{tag}{j}_{tt}")
                        P.add("act", lambda e, jb=jb, b3=b3, ta=ta, tn=tn: e.activation(out=BBUF[jb][:, ta:ta + tn], in_=PB[b3][:, 0:tn], func=AF.Copy),
                              reads=[("pb", b3)], writes=[("bb", jb, tt)], name=f"bbv_{tag}{j}_{tt}")
                    def c_post():
                        zkeys = [("z", jb, "h")] + ([("z", jb, "hs")] if smp else []) + [("z", jb, tt) for tt in range(NT)]
                        LZ = HB + n_p + (NSMP * (HB + SL) if smp else 0)
                        if not st["last"]:
                            hoffz = (l * KC + j) * HB
                            P.add("act", lambda e, zx=zx, hoffz=hoffz, n_p=n_p: e.activation(out=HISTZ[:, hoffz:hoffz + HB], in_=zx[:, n_p:n_p + HB], func=AF.Copy),
                                  reads=zkeys, writes=[("histz", l, j)], name=f"hsz_{tag}{j}")
                        else:
                            ooffz = (l * KC + j) * 3 * HB
                            def fn(e, zx=zx, ooffz=ooffz, n_p=n_p):
                                e.activation(out=OUTB[:, ooffz:ooffz + HB], in_=zx[:, n_p:n_p + HB], func=AF.Copy)
                                src = sub_ap(zx[:], HB + n_p + SL, [[HB + SL, NSMP], [1, HB]])
                                dst = sub_ap(OUTB[:], ooffz + HB, [[HB, NSMP], [1, HB]])
                                return e.activation(out=dst, in_=src, func=AF.Copy)
                            P.add("act", fn, reads=zkeys, writes=[("outb", l, j)], name=f"outb_{tag}{j}")
                        P.add("act", lambda e, zx=zx, zb=zb, LZ=LZ: e.activation(out=zb[:, 0:LZ], in_=zx[:, 0:LZ], func=AF.Copy),
                              reads=zkeys, writes=[("zbf", jb)], name=f"zbf_{tag}{j}")
                        for tt, (ta, tn) in enumerate(tiles):
                            bank = 6 + (abc["n"] % 2)
                            abc["n"] += 1
                            parts = ext_parts(ta, ta + tn, HB, 0)
                            def fn(e, parts=parts, zb=zb, bank=bank, jd=jd, ta=ta):
                                last = None
                                for part in parts:
                                    o = cmp_ap(lambda a, b: PB[bank][:, a - ta:b - ta], part)
                                    for k in range(KB):
                                        p2 = (part[0], part[1], part[2], part[3] + k)
                                        last = e.matmul(o, DIAG3[jd][:, k, :], ext_ap(zb, p2, HB), start=(k == 0), stop=(k == KB - 1))
                                return last
                            P.add("pe", fn, reads=[("zbf", jb), ("diag3", jd)], writes=[("pb", bank)], name=f"cv3pe_{tag}{j}_{tt}")
                            P.add("dve", lambda e, j=j, jb=jb, ta=ta, tn=tn, bank=bank, BBCB=BBCB: e.tensor_tensor(
                                out=BBCB[:, j, ta:ta + tn], in0=PB[bank][:, 0:tn], in1=BBUF[jb][:, ta:ta + tn], op=ALU.mult),
                                reads=[("pb", bank), ("bb", jb, tt)], writes=[(BBK, j, tt)], name=f"bbcb_{tag}{j}_{tt}")
                    return c_tile, c_post

                for step in range(KC + CLAG):
                    def do_A():
                        if step < KC:
                            a_tile, a_post = emit_A(step)
                            for tt in range(NT):
                                a_tile(tt)
                            a_post()
                    def do_C():
                        jc = step - CLAG
                        if 0 <= jc < KC:
                            c_tile, c_post = emit_C(jc)
                            for tt in range(NT):
                                c_tile(tt)
                            c_post()
                    if AC_ORDER == 0:
                        do_A()
                        do_C()
                    else:
                        do_C()
                        do_A()

                for tt, (ta, tn) in enumerate(tiles):
                    si = tt
                    sq = SQ[tt]
                    sqk = [("sq", tt, kc) for kc in range(KC)]
                    cakt = [("ca", kc, tt) for kc in range(KC)]
                    cbk = [("cbf", kc) for kc in range(KC)]
                    if not (HOIST_LN and tt == 0):
                        P.add("act", lambda e, sq=sq, ta=ta, tn=tn: e.activation(out=sq[:, :, 0:tn], in_=ca3_ap(ta, ta + tn), func=AF.Square),
                              reads=cakt, writes=sqk, name=f"lnsq_{tag}{tt}")
                        P.add("act", lambda e, ta=ta, tn=tn: e.activation(out=CBF[:, :, 0:tn], in_=ca3_ap(ta, ta + tn), func=AF.Copy),
                              reads=cakt, writes=cbk, name=f"lncb_{tag}{tt}")
                    def fn(e, sq=sq, tn=tn):
                        last = None
                        for kc in range(KC):
                            last = e.matmul(PB[6][:, 0:tn], ONES[:], sq[:, kc, 0:tn], start=(kc == 0), stop=(kc == KC - 1))
                        return last
                    P.add("pe", fn, reads=sqk + [("ones",)], writes=[("pb", 6)], name=f"lns2_{tag}{tt}")
                    def fn(e, tn=tn):
                        last = None
                        for kc in range(KC):
                            last = e.matmul(PB[7][:, 0:tn], ONES[:], CBF[:, kc, 0:tn], start=(kc == 0), stop=(kc == KC - 1))
                        return last
                    P.add("pe", fn, reads=cbk + [("ones",)], writes=[("pb", 7)], name=f"lns1_{tag}{tt}")
                    P.add("dve", lambda e, si=si, tn=tn: e.tensor_scalar(out=MEAN[si][:, 0:tn], in0=PB[7][:, 0:tn], scalar1=1.0 / D, scalar2=None, op0=ALU.mult),
                          reads=[("pb", 7)], writes=[("mean", si)], name=f"lnm_{tag}{tt}")
                    P.add("dve", lambda e, si=si, tn=tn: e.tensor_tensor(out=VAR[si][:, 0:tn], in0=MEAN[si][:, 0:tn], in1=MEAN[si][:, 0:tn], op=ALU.mult),
                          reads=[("mean", si)], writes=[("var", si)], name=f"lnmsq_{tag}{tt}")
                    P.add("dve", lambda e, si=si, tn=tn: e.scalar_tensor_tensor(out=VAR[si][:, 0:tn], in0=PB[6][:, 0:tn], scalar=1.0 / D, in1=VAR[si][:, 0:tn], op0=ALU.mult, op1=ALU.subtract),
                          reads=[("pb", 6), ("var", si)], writes=[("var", si)], name=f"lnvar_{tag}{tt}")
                    P.add("act", lambda e, si=si, tn=tn: e.activation(out=SD[si][:, 0:tn], in_=VAR[si][:, 0:tn], func=AF.Sqrt, bias=EPSC[LN_EPS]),
                          reads=[("var", si), ("epsc", 1)], writes=[("sd", si)], name=f"lnsd_{tag}{tt}")
                    P.add("dve", lambda e, si=si, tn=tn: e.reciprocal(out=RSTD[si][:, 0:tn], in_=SD[si][:, 0:tn]),
                          reads=[("sd", si)], writes=[("rstd", si)], name=f"lnrs_{tag}{tt}")
                    P.add("dve", lambda e, si=si, tn=tn: e.scalar_tensor_tensor(out=NMR[si][:, 0:tn], in0=MEAN[si][:, 0:tn], scalar=-1.0, in1=RSTD[si][:, 0:tn], op0=ALU.mult, op1=ALU.mult),
                          reads=[("mean", si), ("rstd", si)], writes=[("nmr", si)], name=f"lnnm_{tag}{tt}")
                    for kc in range(KC):
                        P.add("dve", lambda e, kc=kc, ta=ta, tn=tn, si=si: e.tensor_tensor(out=ca_ap(kc, ta, ta + tn), in0=ca_ap(kc, ta, ta + tn), in1=RSTD[si][:, 0:tn], op=ALU.mult),
                              reads=[("ca", kc, tt), ("rstd", si)], writes=[("ca", kc, tt)], name=f"lnmul_{tag}{tt}_{kc}")
                        P.add("dve", lambda e, kc=kc, ta=ta, tn=tn, si=si: e.tensor_tensor(out=ca_ap(kc, ta, ta + tn), in0=ca_ap(kc, ta, ta + tn), in1=NMR[si][:, 0:tn], op=ALU.add),
                              reads=[("ca", kc, tt), ("nmr", si)], writes=[("ca", kc, tt)], name=f"lnadd_{tag}{tt}_{kc}")
                    for kc in range(KC):
                        gcol = PV_LNG + l * KC + kc
                        bcol = PV_LNB + l * KC + kc
                        t_ = P.add("act", lambda e, kc=kc, ta=ta, tn=tn, gcol=gcol, bcol=bcol: e.activation(
                            out=caact_ap(kc, ta, ta + tn), in_=ca_ap(kc, ta, ta + tn), func=AF.Silu, scale=pvc(gcol), bias=pvc(bcol)),
                            reads=[("ca", kc, tt), ("pv",)], writes=[("caact", kc, tt)], extra=prev_H_tasks, name=f"lnsilu_{tag}{tt}_{kc}")
                        silu_tasks.append(t_)

                D_pe = []
                dunit = {}

                def get_unit(name, q):
                    if (name, q) not in dunit:
                        c0 = UW * q
                        if name == "wa":
                            dunit[(name, q)] = load_unit(w_a_out[l, :, c0: c0 + UW], 8, UW)
                        elif name == "wb":
                            dunit[(name, q)] = load_unit(w_b_out[l, :, c0: c0 + UW], 8, UW)
                        elif name == "ga":
                            dunit[(name, q)] = load_unit(w_in[l, :, OFF_GA + c0: OFF_GA + c0 + UW], 8, UW)
                        else:
                            dunit[(name, q)] = load_unit(w_in[l, :, OFF_GB + c0: OFF_GB + c0 + UW], 8, UW)
                    return dunit[(name, q)]

                ROWS = [(BBUF[0], [("bb", 0, tt) for tt in range(NT)]),
                        (BBUF[1], [("bb", 1, tt) for tt in range(NT)]),
                        (UX[0], [("u", 0, "h"), ("u", 0, "hs")] + [("u", 0, tt) for tt in range(NT)]),
                        (UX[1], [("u", 1, "h"), ("u", 1, "hs")] + [("u", 1, tt) for tt in range(NT)]),
                        (ZX[0], [("z", 0, "h"), ("z", 0, "hs")] + [("z", 0, tt) for tt in range(NT)]),
                        (ZX[1], [("z", 1, "h"), ("z", 1, "hs")] + [("z", 1, tt) for tt in range(NT)])]
                for xi, xr in enumerate(XROWS):
                    ROWS.append((xr, [("xrow", xi, tt) for tt in range(NT)]))
                for qi in range(N_SQROWS):
                    ROWS.append((SQ[qi][:].rearrange("p k n -> p (k n)").bitcast(F32), [("sq", qi, kc) for kc in range(KC)]))
                NROW = len(ROWS)

                def emit_D2(j):
                    uwb, ugb = get_unit("wb", j // UPC), get_unit("gb", j // UPC)
                    mc = (j % UPC) * 128
                    row, rkeys = ROWS[j % NROW]
                    for tt, (ta, tn) in enumerate(tiles):
                        b3, b4 = next_banks(2, RING_D)
                        accs = [(b3, tn, [(WS[uwb][:, kc, mc:mc + 128], BBCB[:, kc, ta:ta + tn]) for kc in range(KC)]),
                                (b4, tn, [(WS[ugb][:, kc, mc:mc + 128], XN[:, kc, ta:ta + tn]) for kc in range(KC)])]
                        D_pe.append(mm_job(accs, [("ws", uwb), ("ws", ugb)] + [(XNK, kc, tt) for kc in range(KC)] + [(BBK, kc, tt) for kc in range(KC)], f"D2_{tag}{j}_{tt}"))
                        si2 = sgc["n"] % NSG
                        sgc["n"] += 1
                        P.add("act", lambda e, si2=si2, b4=b4, tn=tn: e.activation(out=SG[si2][:, 0:tn], in_=PB[b4][:, 0:tn], func=AF.Sigmoid),
                              reads=[("pb", b4)], writes=[("sg", si2)], name=f"sgb_{tag}{j}_{tt}")
                        P.add("dve", lambda e, row=row, b3=b3, si2=si2, ta=ta, tn=tn: e.tensor_tensor(out=row[:, ta:ta + tn], in0=PB[b3][:, 0:tn], in1=SG[si2][:, 0:tn], op=ALU.mult),
                              reads=[("pb", b3), ("sg", si2)], writes=rkeys, name=f"m2_{tag}{j}_{tt}")

                def emit_D1(j):
                    uwa, uga = get_unit("wa", j // UPC), get_unit("ga", j // UPC)
                    mc = (j % UPC) * 128
                    row, rkeys = ROWS[j % NROW]
                    for tt, (ta, tn) in enumerate(tiles):
                        b1, b2 = next_banks(2, RING_D)
                        accs = [(b1, tn, [(WS[uwa][:, kc, mc:mc + 128], caact_ap(kc, ta, ta + tn)) for kc in range(KC)]),
                                (b2, tn, [(WS[uga][:, kc, mc:mc + 128], XN[:, kc, ta:ta + tn]) for kc in range(KC)])]
                        D_pe.append(mm_job(accs, [("ws", uwa), ("ws", uga)] + [(XNK, kc, tt) for kc in range(KC)] + [("caact", kc, tt) for kc in range(KC)], f"D1_{tag}{j}_{tt}",
                                           fine_keys=([("caact", kc, tt) for kc in range(KC)] if (j == 0 and FINE_D) else None)))
                        si = sgc["n"] % NSG
                        sgc["n"] += 1
                        P.add("act", lambda e, si=si, b2=b2, tn=tn: e.activation(out=SG[si][:, 0:tn], in_=PB[b2][:, 0:tn], func=AF.Sigmoid),
                              reads=[("pb", b2)], writes=[("sg", si)], name=f"sga_{tag}{j}_{tt}")
                        mi = (j * NT + tt) % 2
                        P.add("dve", lambda e, mi=mi, b1=b1, si=si, tn=tn: e.tensor_tensor(out=M1[mi][:, 0:tn], in0=PB[b1][:, 0:tn], in1=SG[si][:, 0:tn], op=ALU.mult),
                              reads=[("pb", b1), ("sg", si)], writes=[("m1", mi)], name=f"m1_{tag}{j}_{tt}")
                        P.add("dve", lambda e, mi=mi, j=j, row=row, ta=ta, tn=tn: e.tensor_tensor(out=merged_ap(j, ta, ta + tn), in0=row[:, ta:ta + tn], in1=M1[mi][:, 0:tn], op=ALU.add),
                              reads=rkeys + [("m1", mi)], writes=[("merged", j, tt)], extra=silu_tasks, name=f"mrg_{tag}{j}_{tt}")

                for j in range(min(NROW, KC)):
                    emit_D2(j)
                for j in range(KC):
                    emit_D1(j)
                    if j + NROW < KC:
                        emit_D2(j + NROW)

                P.add("act", lambda e: e.activation(out=WARM[:, 0:1], in_=EPS_T[:, 0:1], func=AF.Sqrt), reads=[("epsc", 0)], writes=[("warm",)], name=f"warmE_{tag}", prio=2)
                E_pe = []
                wo_units = [load_unit(w_o[l, :, UW * q: UW * q + UW], 8, UW) for q in range(KC // UPC)]
                for tt, (ta, tn) in enumerate(tiles):
                    for j in range(KC):
                        uo = wo_units[j // UPC]
                        mc = (j % UPC) * 128
                        (b1,) = next_banks(1)
                        accs = [(b1, tn, [(WS[uo][:, kc, mc:mc + 128], merged_ap(kc, ta, ta + tn)) for kc in range(KC)])]
                        E_pe.append(mm_job(accs, [("ws", uo)] + [("merged", kc, tt) for kc in range(KC)], f"E_{tag}{j}_{tt}"))
                        P.add("dve", lambda e, j=j, b1=b1, ta=ta, tn=tn, X=X: e.tensor_tensor(out=X[:, j, ta:ta + tn], in0=PB[b1][:, 0:tn], in1=X[:, j, ta:ta + tn], op=ALU.add),
                              reads=[("pb", b1), (XK, j, tt)], writes=[(XK, j, tt)], name=f"res1_{tag}{j}_{tt}")
                        norm_accum(j, tt, "r2" + tag)
                    norm_finish(tt, PV_G2 + l * KC, RMS_EPS, xn_out, XNK, "r2" + tag)

                alias_deps = D_pe + E_pe
                gunit = {}

                def g_units(q):
                    if q not in gunit:
                        c0 = UW * q
                        ncols = min(UW, DH - c0)
                        gunit[q] = (load_unit(w_gate[l, :, c0: c0 + ncols], 8, ncols), load_unit(w_up[l, :, c0: c0 + ncols], 8, ncols))
                    return gunit[q]

                def emit_G(hc, tt):
                    ufg, ufu = g_units(hc // UPC)
                    mc = (hc % UPC) * 128
                    ta, tn = tiles[tt]
                    b1, b2 = next_banks(2, RING_G)
                    accs = [(b1, tn, [(WS[ufg][:, kc, mc:mc + 128], XN[:, kc, ta:ta + tn]) for kc in range(KC)]),
                            (b2, tn, [(WS[ufu][:, kc, mc:mc + 128], XN[:, kc, ta:ta + tn]) for kc in range(KC)])]
                    mm_job(accs, [("ws", ufg), ("ws", ufu)] + [(XNK, kc, tt) for kc in range(KC)], f"G_{tag}{hc}_{tt}",
                           fine_keys=([(XNK, kc, tt) for kc in range(KC)] if (hc == 0 and FINE_G) else None))
                    si = sgc["n"] % NSG
                    sgc["n"] += 1
                    P.add("act", lambda e, si=si, b1=b1, tn=tn: e.activation(out=SG[si][:, 0:tn], in_=PB[b1][:, 0:tn], func=AF.Silu),
                          reads=[("pb", b1)], writes=[("sg", si)], name=f"fsilu_{tag}{hc}_{tt}")
                    P.add("dve", lambda e, si=si, b2=b2, hc=hc, ta=ta, tn=tn: e.tensor_tensor(out=f_ap(hc, ta, ta + tn), in0=PB[b2][:, 0:tn], in1=SG[si][:, 0:tn], op=ALU.mult),
                          reads=[("pb", b2), ("sg", si)], writes=[("f", hc, tt)], extra=alias_deps, name=f"f_{tag}{hc}_{tt}")

                for step in range(HC + GLAG):
                    if step < HC:
                        emit_G(step, 0)
                    if step - GLAG >= 0:
                        for tt in range(1, NT):
                            emit_G(step - GLAG, tt)

                if l == DEPTH - 1 and not st["last"]:
                    emit_xload(STS[sidx + 1], list(nb_state["cur"]))

                P.add("act", lambda e: e.activation(out=WARM[:, 0:1], in_=EPS_T[:, 0:1], func=AF.Sqrt), reads=[("epsc", 0)], writes=[("warm",)], name=f"warmH_{tag}", prio=2)
                H_pe = []
                for j in range(KC):
                    if j % UPC == 0:
                        c0 = UW * (j // UPC)
                        dunits = []
                        for r0 in range(0, HC, 8):
                            nk = min(8, HC - r0)
                            dunits.append((load_unit(w_down[l, r0 * 128:(r0 + nk) * 128, c0: c0 + UW], nk, UW), r0, nk))
                    mc = (j % UPC) * 128
                    for tt, (ta, tn) in enumerate(tiles):
                        (b1,) = next_banks(1)
                        mms = []
                        for (slot, r0, nk) in dunits:
                            for kk in range(nk):
                                mms.append((WS[slot][:, kk, mc:mc + 128], f_ap(r0 + kk, ta, ta + tn)))
                        accs = [(b1, tn, mms)]
                        H_pe.append(mm_job(accs, [("ws", u[0]) for u in dunits] + [("f", hc, tt) for hc in range(HC)], f"H_{tag}{j}_{tt}"))
                        P.add("dve", lambda e, j=j, b1=b1, ta=ta, tn=tn, X=X: e.tensor_tensor(out=X[:, j, ta:ta + tn], in0=PB[b1][:, 0:tn], in1=X[:, j, ta:ta + tn], op=ALU.add),
                              reads=[("pb", b1), (XK, j, tt)], writes=[(XK, j, tt)], name=f"res2_{tag}{j}_{tt}")
                        norm_accum(j, tt, (f"r1s{sidx}l{l + 1}" if l + 1 < DEPTH else f"fin{sidx}"))
                prev_H_tasks = H_pe
                for tt in range(NT):
                    if l + 1 < DEPTH:
                        norm_finish(tt, PV_G1 + (l + 1) * KC, RMS_EPS, xn_out, XNK, f"r1s{sidx}l{l + 1}")
                    else:
                        norm_finish(tt, PV_GF, RMS_EPS, lambda kc, a, b: ca_ap(kc, a, b), "yfm", f"fin{sidx}", extra_w=prev_H_tasks)

            t_ = P.add("sp", lambda e, g0=g0, n_st=n_st: e.dma_start(out=yT[:, :, g0:g0 + n_st], in_=sub_ap(CA, 0, [[TSMAX, KC], [1, n_st]])),
                       reads=[("yfm", kc, tt) for kc in range(KC) for tt in range(NT)], dma="d_y", name=f"sty{sidx}")
            out_dma_tasks.append(t_)
            prev_H_tasks = prev_H_tasks + [t_]

        t_ = P.add("sp", lambda e: e.dma_start(out=oA_d[:, :], in_=OUTA[:]), reads=[("outa", l, j) for l in range(DEPTH) for j in range(KC)], dma="d_oa", name="st_oa")
        out_dma_tasks.append(t_)
        t_ = P.add("sp", lambda e: e.dma_start(out=oB_d[:, :], in_=OUTB[:]), reads=[("outb", l, j) for l in range(DEPTH) for j in range(KC)], dma="d_ob", name="st_ob")
        out_dma_tasks.append(t_)
        P.add("sp", lambda e: None, extra=out_dma_tasks, name="final_wait")

        if SCHED:
            est_total = schedule(P, window=WINDOW, lat=SCHED_LAT)
            if DEBUG_SCHED:
                print(f"[kernel] scheduled estimate: {est_total / 1e3:.1f} us")
        dependents = set()
        for e_ in ENGS:
            for t in P.tasks[e_]:
                dependents.update(t.deps)
        dma_cnt = {}
        for e_ in ENGS:
            n = 0
            for t in P.tasks[e_]:
                if t.dma is not None:
                    dma_cnt[t.dma] = dma_cnt.get(t.dma, 0) + 1
                    t.ev_sem = sem(t.dma)
                    t.ev_val = 16 * dma_cnt[t.dma]
                    t.signal = True
                elif t in dependents:
                    n += 1
                    t.ev_sem = sem("p_" + e_)
                    t.ev_val = n
                    t.signal = True

        def emit(eng_name, e):
            waited = {}
            for t in P.tasks[eng_name]:
                need = {}
                for d in t.deps:
                    if d.dma is None and d.eng == eng_name and eng_name == "pe":
                        continue
                    assert d.signal, (t.name, d.name)
                    k = d.ev_sem.name
                    if need.get(k, (None, 0))[1] < d.ev_val:
                        need[k] = (d.ev_sem, d.ev_val)
                for k, (s_, v) in need.items():
                    if waited.get(k, 0) < v:
                        e.wait_ge(s_, v)
                        waited[k] = v
                inst = t.fn(e)
                if t.signal:
                    assert inst is not None, t.name
                    inst.then_inc(t.ev_sem, 16 if t.dma is not None else 1)

        block = es.enter_context(nc.Block())

        @block.tensor
        def _(e):
            emit("pe", e)

        @block.scalar
        def _(e):
            emit("act", e)

        @block.vector
        def _(e):
            emit("dve", e)

        @block.gpsimd
        def _(e):
            emit("pool", e)

        @block.sync
        def _(e):
            emit("sp", e)
    return nc


EPSC = {}


def _prep_core(c, x_prompt, x_sample, state_conv_a, state_conv_b, meta_tokens):
    toks = np.concatenate([meta_tokens, x_prompt[c], x_sample[2 * c], x_sample[2 * c + 1]], axis=0)
    xT = np.ascontiguousarray(toks.reshape(TT, KC, 128).transpose(2, 1, 0))
    sa = state_conv_a[:, 2 * c:2 * c + 2]
    shA = np.ascontiguousarray(sa.reshape(DEPTH, NSMP, HA, KC, 128).transpose(4, 0, 1, 3, 2)).reshape(128, -1)
    sbb = state_conv_b[:, 2 * c:2 * c + 2]
    shB = np.ascontiguousarray(sbb.reshape(DEPTH, NSMP, HB, KC, 128).transpose(4, 0, 1, 3, 2)).reshape(128, -1)
    return xT, shA, shB


def _vec_cols(v):
    lead = v.shape[:-1]
    a = v.reshape(-1, KC, 128)
    return np.ascontiguousarray(a.transpose(2, 0, 1)).reshape(128, -1)


_CACHE = {}


def kernel(x_prompt, x_sample, state_conv_a, state_conv_b, meta_tokens, norm1_g, w_in,
           conv_a_w, conv_a_b, ln_a_g, ln_a_b, w_a_out, conv_b_w, w_b_out, w_o, norm2_g,
           w_ffn_gate, w_ffn_up, w_ffn_down, final_norm_g):
    f32 = np.float32
    A = lambda a: np.ascontiguousarray(np.asarray(a, dtype=f32))
    x_prompt, x_sample, state_conv_a, state_conv_b, meta_tokens = map(A, (x_prompt, x_sample, state_conv_a, state_conv_b, meta_tokens))
    wa = np.asarray(conv_a_w, f32).reshape(DEPTH, KA, KC, 128).transpose(3, 0, 2, 1).reshape(128, -1)
    wb = np.asarray(conv_b_w, f32).reshape(DEPTH, KB, KC, 128).transpose(3, 0, 2, 1).reshape(128, -1)
    pvec = np.concatenate([
        _vec_cols(np.asarray(norm1_g, f32)), _vec_cols(np.asarray(conv_a_b, f32)), _vec_cols(np.asarray(ln_a_g, f32)),
        _vec_cols(np.asarray(ln_a_b, f32)), _vec_cols(np.asarray(norm2_g, f32)), _vec_cols(np.asarray(final_norm_g, f32)[None]),
        wa, wb], axis=1)
    assert pvec.shape == (128, NPV), pvec.shape
    pvec = np.ascontiguousarray(pvec)

    if "nc" not in _CACHE:
        _CACHE["nc"] = build()
    nc = _CACHE["nc"]
    shared = dict(pvec=pvec, w_in=A(w_in), w_a_out=A(w_a_out), w_b_out=A(w_b_out), w_o=A(w_o),
                  w_ffn_gate=A(w_ffn_gate), w_ffn_up=A(w_ffn_up), w_ffn_down=A(w_ffn_down))
    in_maps = []
    for c in range(NCORES):
        xT, shA, shB = _prep_core(c, x_prompt, x_sample, state_conv_a, state_conv_b, meta_tokens)
        m = dict(shared)
        m.update(xT=xT, shA=shA, shB=shB)
        in_maps.append(m)
    res = run_bass_kernel_spmd(nc, in_maps, core_ids=list(range(NCORES)))
    B = x_prompt.shape[0]
    y_prompt = np.empty((B, SEQ, D), f32)
    y_sample = np.empty((2 * NCORES, SL, D), f32)
    nap = np.empty((DEPTH, B, HA, D), f32)
    nbp = np.empty((DEPTH, B, HB, D), f32)
    nas = np.empty((DEPTH, 2 * NCORES, HA, D), f32)
    nbs = np.empty((DEPTH, 2 * NCORES, HB, D), f32)
    for c in range(NCORES):
        r = res.results[c]
        y = np.asarray(r["yT"]).reshape(128, KC, TT).transpose(2, 1, 0).reshape(TT, D)
        y_prompt[c] = y[NMETA:TP]
        y_sample[2 * c] = y[TP:TP + SL]
        y_sample[2 * c + 1] = y[TP + SL:TT]
        oa = np.asarray(r["oA"]).reshape(128, DEPTH, KC, 3, HA).transpose(1, 3, 4, 2, 0).reshape(DEPTH, 3, HA, D)
        ob = np.asarray(r["oB"]).reshape(128, DEPTH, KC, 3, HB).transpose(1, 3, 4, 2, 0).reshape(DEPTH, 3, HB, D)
        nap[:, c] = oa[:, 0]
        nas[:, 2 * c] = oa[:, 1]
        nas[:, 2 * c + 1] = oa[:, 2]
        nbp[:, c] = ob[:, 0]
        nbs[:, 2 * c] = ob[:, 1]
        nbs[:, 2 * c + 1] = ob[:, 2]
    return (y_prompt, y_sample, nap, nbp, nas, nbs)
```

```python
import numpy as np
from contextlib import ExitStack
import concourse.bass as bass
import concourse.mybir as mybir
from concourse.ap import AP
from concourse.bass_utils import run_bass_kernel_spmd

F32 = mybir.dt.float32
BF16 = mybir.dt.bfloat16
AF = mybir.ActivationFunctionType
ALU = mybir.AluOpType

NCORES = 8
D = 1024
KC = 8
NIN = 7168
DH = 2816
HC = 22
NMETA = 16
SEQ = 2048
TP = NMETA + SEQ
NSMP = 2
SL = 16
TT = TP + NSMP * SL
KA = 31
HA = 30
KB = 3
HB = 2
DEPTH = 2
OFF_AVAL, OFF_AGATE, OFF_BB, OFF_BC, OFF_BX, OFF_GA, OFF_GB = 0, 1024, 2048, 3072, 4096, 5120, 6144
RMS_EPS = 1e-6
LN_EPS = 1e-5

NPE = 19
NSLOT = 10
UW = 256
UPC = UW // 128
DIAG_ENG = "dve"
TLAG = 0
GLAG = 4
CLAG = 0
NMAX = 350
SCHED = True
PE_GHZ = 2.25
SCHED_Q = 1.0
SCHED_SLACK = 0.0
READY_TIE = True
RING_D = 6
FINE_A = 1
FINE_G = 1
FINE_D = 1
HOIST_LN = 1
Y_SPLIT = 1
N_XROWS = 0
N_SQROWS = 2
AC_ORDER = 0
RING_G = 8
NSG_CFG = 3
TL_LO, TL_HI = 1.0, 0.0
WINDOW = 128
DEBUG_SCHED = False
WINDOW = 128

PV_G1 = 0
PV_CAB = PV_G1 + DEPTH * KC
PV_LNG = PV_CAB + DEPTH * KC
PV_LNB = PV_LNG + DEPTH * KC
PV_G2 = PV_LNB + DEPTH * KC
PV_GF = PV_G2 + DEPTH * KC
PV_WA = PV_GF + KC
PV_WB = PV_WA + DEPTH * KC * KA
NPV = PV_WB + DEPTH * KC * KB


def make_sts():
    sizes = [699, 699, 698]
    sts = []
    g = 0
    for i, n in enumerate(sizes):
        last = i == len(sizes) - 1
        n_p = n - (NSMP * SL if last else 0)
        n0 = (n + 1) // 2
        tiles = [(0, n0), (n0, n - n0)]
        sts.append(dict(idx=i, g0=g, n=n, n_p=n_p, smp=last, tiles=tiles, first=(i == 0), last=last))
        g += n
    assert g == TT
    return sts


STS = make_sts()
TSMAX = max(s["n"] for s in STS)
UEXT = max(HA + s["n_p"] + (NSMP * (HA + SL) if s["smp"] else 0) for s in STS)
ZEXT = max(HB + s["n_p"] + (NSMP * (HB + SL) if s["smp"] else 0) for s in STS)

ENGS = ("pe", "act", "dve", "pool", "sp")


class Task:
    __slots__ = ("eng", "fn", "deps", "dma", "signal", "ev_sem", "ev_val", "name", "seq", "prio")

    def __init__(self, eng, fn, name):
        self.eng = eng
        self.fn = fn
        self.deps = set()
        self.dma = None
        self.signal = False
        self.ev_sem = None
        self.ev_val = 0
        self.name = name


class Prog:
    HI = ("sgb_", "m2_", "sq_", "ss_", "sd_", "rstd_", "nrm_", "lnsq", "lncb", "lns2", "lns1", "lnm_", "lnmsq", "lnvar", "lnsd", "lnrs", "lnnm", "lnmul", "lnadd", "lnsilu")

    def default_prio(self, name):
        return 0 if name.startswith(self.HI) else 1

    def __init__(self):
        self.tasks = {e: [] for e in ENGS}
        self.writers = {}
        self.readers = {}
        self.nseq = 0
        self._partial = {}

    def add(self, eng, fn, reads=(), writes=(), extra=(), dma=None, name="", prio=None):
        t = Task(eng, fn, name)
        t.prio = self.default_prio(name) if prio is None else prio
        t.dma = dma
        t.seq = self.nseq
        self.nseq += 1
        deps = set(x for x in extra if x is not None)
        for k in reads:
            deps.update(self.writers.get(k, ()))
        for k in writes:
            deps.update(self.readers.get(k, ()))
            deps.update(self.writers.get(k, ()))
        for k in reads:
            self.readers.setdefault(k, []).append(t)
        for k in writes:
            self.writers[k] = [t]
            self.readers[k] = []
        deps.discard(t)
        t.deps = deps
        self.tasks[eng].append(t)
        return t


def sub_ap(base, extra_off, dims):
    return AP(base.tensor, base.offset + extra_off, [list(base.ap[0])] + [list(d) for d in dims])


class Est:
    def __init__(self, eng):
        self.eng = eng
        self.ns = 0.0
        self.act_set = None
        self.dma_bytes = 0

    @staticmethod
    def _el(ap):
        n = 1
        for d in ap.shape[1:]:
            n *= d
        return n

    def matmul(self, out, lhsT, rhs, **k):
        self.ns += max(self._el(out), 64) / PE_GHZ + 2
        return self

    def activation(self, out, in_, func, bias=None, scale=None, **k):
        self.ns += 200 + 0.833 * self._el(out)
        if isinstance(scale, AP):
            self.ns += 90
        if func == AF.Sigmoid:
            self.act_set = "sig"
        elif func == AF.Silu:
            self.act_set = "silu"
        elif func == AF.Sqrt:
            self.act_set = "sqrt"
        return self

    def tensor_tensor(self, out, in0, in1, op, **k):
        self.ns += 140 + self._el(out) / 0.96
        return self

    def scalar_tensor_tensor(self, out, in0, scalar, in1, op0, op1, **k):
        self.ns += 140 + self._el(out) / 0.96
        return self

    def tensor_scalar(self, out, in0, scalar1, scalar2, op0, op1=None, **k):
        self.ns += 140 + self._el(out) / 0.96
        return self

    def reciprocal(self, out, in_):
        self.ns += 100 + 6.4 * self._el(out)
        return self

    def tensor_copy(self, out, in_, **k):
        self.ns += 150 + self._el(out)
        return self

    def memset(self, ap, c):
        self.ns += 100 + self._el(ap) * 0.5
        return self

    def affine_select(self, out, in_, **k):
        self.ns += 300
        return self

    def dma_start(self, out, in_, **k):
        self.dma_bytes += self._el(in_) * 4 * in_.shape[0]
        self.ns += 1050 if self.eng == "pool" else 150
        return self

    def then_inc(self, *a, **k):
        return self


def schedule(P, window=64, lat=150.0, dma_rate=300.0, dma_lat=2000.0):
    idx = 0
    for e in ENGS:
        pass
    allt = []
    for e in ENGS:
        allt.extend(P.tasks[e])
    allt.sort(key=lambda t: t.seq)
    est = {}
    for t in allt:
        st_ = Est(t.eng)
        t.fn(st_)
        est[t] = st_
    ndeps = {t: len(t.deps) for t in allt}
    users = {t: [] for t in allt}
    for t in allt:
        for d in t.deps:
            users[d].append(t)
    ready = {t: 0.0 for t in allt if ndeps[t] == 0}
    pending = {e: list(P.tasks[e]) for e in ENGS}
    free = {e: 0.0 for e in ENGS}
    dma_free = 0.0
    act_cur = None
    partial = {}
    pe_tl = []
    order = {e: [] for e in ENGS}
    finish = {}
    remaining = len(allt)
    while remaining:
        best = None
        for e in ENGS:
            pl = pending[e]
            cands = []
            mn = None
            for t in pl[:window]:
                r = ready.get(t)
                if r is None:
                    continue
                start = max(r, free[e])
                pen = 0.0
                if e == "act" and est[t].act_set is not None and est[t].act_set != act_cur:
                    pen = 1300.0
                cmp_start = start + (pen if t.prio >= 1 else 0.0)
                cands.append((cmp_start, start + pen, t, r))
                if mn is None or cmp_start < mn:
                    mn = cmp_start
            if not cands:
                continue
            pick = None
            for (cs, start, t, r) in cands:
                if cs <= mn + SCHED_SLACK:
                    k2 = (t.prio, cs, (r if READY_TIE else 0), t.seq)
                    if pick is None or k2 < pick[0]:
                        pick = (k2, cs, start, t)
            key = (pick[1], pick[3].prio, pick[3].seq, pick[2])
            if best is None or key < best[0]:
                best = (key, e, pick[3])
        assert best is not None, "scheduler deadlock"
        start = best[0][3]
        e, t = best[1], best[2]
        es_ = est[t]
        if t.dma is not None:
            issue_end = start + es_.ns
            x0 = max(issue_end, dma_free)
            dma_free = x0 + es_.dma_bytes / dma_rate
            fin = dma_free + dma_lat
            free[e] = issue_end
        else:
            fin = start + es_.ns
            free[e] = fin
            if e == "act" and es_.act_set is not None:
                act_cur = es_.act_set
        finish[t] = fin
        if e == "pe":
            pe_tl.append((start, fin, t.name))
        if DEBUG_SCHED and TL_LO <= start / 1e3 <= TL_HI:
            print(f"  [s] {e:4s} {t.name:22s} start={start/1e3:8.2f} fin={fin/1e3:8.2f}")
        order[e].append(t)
        pending[e].remove(t)
        remaining -= 1
        for u in users[t]:
            partial[u] = max(partial.get(u, 0.0), fin + lat)
            ndeps[u] -= 1
            if ndeps[u] == 0:
                ready[u] = partial[u]
    if DEBUG_SCHED:
        gaps = {}
        prev = 0.0
        for (st0, fn0, nm) in pe_tl:
            ph = nm.split("_")[0]
            if st0 > prev:
                gaps[ph] = gaps.get(ph, 0.0) + (st0 - prev)
            prev = max(prev, fn0)
        print("[sched] PE gaps before (us):", {k: round(v / 1e3, 1) for k, v in sorted(gaps.items(), key=lambda kv: -kv[1])})
        busy = {e: sum(est[t].ns for t in order[e]) for e in ENGS}
        print("[sched] busy us:", {e: round(busy[e] / 1e3, 1) for e in ENGS}, "free:", {e: round(free[e] / 1e3, 1) for e in ENGS}, "dma_free", round(dma_free / 1e3, 1))
    P.tasks = order
    return max(finish.values())


def build():
    nc = bass.Bass("TRN2", target_bir_lowering=False)

    def din(name, shape):
        return nc.dram_tensor(name, list(shape), F32, kind="ExternalInput").ap()

    def dout(name, shape):
        return nc.dram_tensor(name, list(shape), F32, kind="ExternalOutput").ap()

    xT = din("xT", [128, KC, TT])
    pvec_d = din("pvec", [128, NPV])
    shA_d = din("shA", [128, DEPTH * NSMP * KC * HA])
    shB_d = din("shB", [128, DEPTH * NSMP * KC * HB])
    w_in = din("w_in", [DEPTH, D, NIN])
    w_a_out = din("w_a_out", [DEPTH, D, D])
    w_b_out = din("w_b_out", [DEPTH, D, D])
    w_o = din("w_o", [DEPTH, D, D])
    w_gate = din("w_ffn_gate", [DEPTH, D, DH])
    w_up = din("w_ffn_up", [DEPTH, D, DH])
    w_down = din("w_ffn_down", [DEPTH, DH, D])
    yT = dout("yT", [128, KC, TT])
    oA_d = dout("oA", [128, DEPTH * KC * 3 * HA])
    oB_d = dout("oB", [128, DEPTH * KC * 3 * HB])

    P = Prog()
    es = ExitStack()
    with es:
        def sb(name, shape, dt):
            return es.enter_context(nc.sbuf_tensor(name, list(shape), dt))

        BUF2 = [sb(f"XBUF{i}", [128, KC * TSMAX * 4], mybir.dt.uint8) for i in range(2)]

        def st_views(par):
            Xv = BUF2[par][:].bitcast(F32).rearrange("p (k t) -> p k t", k=KC)
            o = BUF2[1 - par][:]
            XNv = sub_ap(o, 0, [[1, KC * TSMAX * 2]]).bitcast(BF16).rearrange("p (k t) -> p k t", k=KC)
            BBv = sub_ap(o, KC * TSMAX * 2, [[1, KC * TSMAX * 2]]).bitcast(BF16).rearrange("p (k t) -> p k t", k=KC)
            return Xv, XNv, BBv
        BIG = sb("BIG", [128, KC * TSMAX * 4 + KC * TSMAX * 2], mybir.dt.uint8)
        big_all = BIG[:]
        CA = sub_ap(big_all, 0, [[1, KC * TSMAX * 4]]).bitcast(F32)
        CAACT = sub_ap(big_all, KC * TSMAX * 4, [[1, KC * TSMAX * 2]]).bitcast(BF16)
        MERGED = sub_ap(big_all, 0, [[1, KC * TSMAX * 2]]).bitcast(BF16)
        FF = sub_ap(big_all, 0, [[1, HC * TSMAX * 2]]).bitcast(BF16)
        assert HC * TSMAX * 2 <= KC * TSMAX * 6

        def ca_ap(kc, a, b):
            return CA[:, kc * TSMAX + a: kc * TSMAX + b]

        def ca3_ap(a, b):
            return sub_ap(CA, a, [[TSMAX, KC], [1, b - a]])

        def caact_ap(kc, a, b):
            return CAACT[:, kc * TSMAX + a: kc * TSMAX + b]

        def merged_ap(kc, a, b):
            return MERGED[:, kc * TSMAX + a: kc * TSMAX + b]

        def f_ap(hc, a, b):
            return FF[:, hc * TSMAX + a: hc * TSMAX + b]

        UX = [sb(f"UX{i}", [128, UEXT], F32) for i in range(2)]
        UB = [sb(f"UB{i}", [128, UEXT], BF16) for i in range(2)]
        ZX = [sb(f"ZX{i}", [128, ZEXT], F32) for i in range(2)]
        BBUF = [sb(f"BBUF{i}", [128, TSMAX], F32) for i in range(2)]
        ZB = [sb(f"ZB{i}", [128, ZEXT], BF16) for i in range(2)]
        SQ = [sb(f"SQ{i}", [128, KC, NMAX], BF16) for i in range(2)]
        CBF = sb("CBF", [128, KC, NMAX], BF16)
        NST_ = 2
        SD = [sb(f"SD{i}", [128, NMAX], F32) for i in range(NST_)]
        RSTD = [sb(f"RSTD{i}", [128, NMAX], F32) for i in range(NST_)]
        MEAN = [sb(f"MEAN{i}", [128, NMAX], F32) for i in range(NST_)]
        VAR = [sb(f"VAR{i}", [128, NMAX], F32) for i in range(NST_)]
        NMR = [sb(f"NMR{i}", [128, NMAX], F32) for i in range(NST_)]
        NSG = NSG_CFG
        SG = [sb(f"SG{i}", [128, NMAX], F32) for i in range(NSG)]
        M1 = [sb(f"M1_{i}", [128, NMAX], F32) for i in range(2)]
        XROWS = [sb(f"XROW{i}", [128, TSMAX], F32) for i in range(N_XROWS)]
        DIAG = [sb(f"DIAG{i}", [128, max(NPE, 1), 128], BF16) for i in range(2)]
        DIAG3 = [sb(f"DIAG3_{i}", [128, KB, 128], BF16) for i in range(2)]
        WS = [sb(f"WS{i}", [128, 8, UW], BF16) for i in range(NSLOT)]
        PV = sb("PV", [128, NPV], F32)
        IDB = sb("IDB", [128, 128], BF16)
        IDF = sb("IDF", [128, 128], F32)
        EPS_T = sb("EPS_T", [128, 2], F32)
        WARM = sb("WARM", [128, 2], F32)
        EPSC[RMS_EPS] = EPS_T[:, 0:1]
        EPSC[LN_EPS] = EPS_T[:, 1:2]
        ONES = sb("ONES", [128, 128], BF16)
        SHA = sb("SHA", [128, DEPTH * NSMP * KC * HA], F32)
        SHB = sb("SHB", [128, DEPTH * NSMP * KC * HB], F32)
        HISTA = sb("HISTA", [128, DEPTH * KC * HA], F32)
        HISTZ = sb("HISTZ", [128, DEPTH * KC * HB], F32)
        OUTA = sb("OUTA", [128, DEPTH * KC * 3 * HA], F32)
        OUTB = sb("OUTB", [128, DEPTH * KC * 3 * HB], F32)
        PB = [es.enter_context(nc.psum_tensor(f"PB{i}", [128, 512], F32)) for i in range(8)]

        sems = {}

        def sem(name):
            if name not in sems:
                sems[name] = es.enter_context(nc.semaphore(name))
            return sems[name]

        for e_ in ENGS:
            sem("p_" + e_)

        t_pv = P.add("sp", lambda e: e.dma_start(out=PV[:], in_=pvec_d[:, :]), writes=[("pv",)], dma="d_pv", name="ld_pv")
        P.add("sp", lambda e: e.dma_start(out=SHA[:], in_=shA_d[:, :]), writes=[("sha",)], dma="d_sha", name="ld_sha")
        P.add("sp", lambda e: e.dma_start(out=SHB[:], in_=shB_d[:, :]), writes=[("shb",)], dma="d_shb", name="ld_shb")

        P.add("pool", lambda e: e.memset(IDF[:], 0.0), writes=[("idf",)], name="idf0")
        P.add("pool", lambda e: e.memset(ONES[:], 1.0), writes=[("ones",)], name="ones")
        P.add("pool", lambda e: e.memset(EPS_T[:, 0:1], RMS_EPS), writes=[("epsc", 0)], name="eps0")
        P.add("pool", lambda e: e.memset(EPS_T[:, 1:2], LN_EPS), writes=[("epsc", 1)], name="eps1")
        P.add("pool", lambda e: e.affine_select(out=IDF[:], in_=IDF[:], pattern=[[-1, 128]], compare_op=ALU.not_equal,
                                                fill=1.0, base=0, channel_multiplier=1),
              reads=[("idf",)], writes=[("idf",)], name="ident")
        P.add("pool", lambda e: e.tensor_copy(out=IDB[:], in_=IDF[:]), reads=[("idf",)], writes=[("idb",)], name="identb")

        def pvc(col):
            return PV[:, col:col + 1]

        wstate = dict(n=0)
        bank_state = dict(n=0)

        def load_unit(src_ap, nk, ncols):
            n = wstate["n"]
            wstate["n"] += 1
            slot = n % NSLOT
            key = ("ws", slot)
            src = src_ap.rearrange("(kc p) n -> p kc n", p=128)
            dst = WS[slot][:, 0:nk, 0:ncols]
            P.add("pool", lambda e, dst=dst, src=src: e.dma_start(out=dst, in_=src), writes=[key],
                  dma=f"d_ws{slot}", name=f"ldw{n}")
            return slot

        def next_banks(k, ring=6):
            out = []
            for _ in range(k):
                out.append(bank_state["n"] % ring)
                bank_state["n"] += 1
            return out

        nb_state = dict(cur=[], prev=[])
        fin_state = dict(cur=[], prev=[])

        def mm_job(accs, reads, name, fine_keys=None):
            t0_ = _mm_job(accs, reads, name, fine_keys)
            if any(str(k[0]).startswith(("xn", "bbcb")) for k in reads):
                nb_state["cur"].append(t0_)
            return t0_

        def _mm_job(accs, reads, name, fine_keys=None):
            def fn(e, accs=accs):
                last = None
                for (bank, n, mms) in accs:
                    for i, (l_ap, r_ap) in enumerate(mms):
                        last = e.matmul(PB[bank][:, 0:n], l_ap, r_ap, start=(i == 0), stop=(i == len(mms) - 1))
                return last
            if fine_keys is not None:
                K_ = len(accs[0][2])
                assert all(len(a[2]) == K_ for a in accs) and len(fine_keys) == K_
                base_reads = [r for r in reads if r not in set(fine_keys)]
                t_ = None
                for i in range(K_):
                    def fni(e, accs=accs, i=i, K_=K_):
                        last = None
                        for (bank, n, mms) in accs:
                            l_ap, r_ap = mms[i]
                            last = e.matmul(PB[bank][:, 0:n], l_ap, r_ap, start=(i == 0), stop=(i == K_ - 1))
                        return last
                    t_ = P.add("pe", fni, reads=base_reads + [fine_keys[i]], writes=[("pb", a[0]) for a in accs], name=f"{name}k{i}")
                return t_
            return P.add("pe", fn, reads=reads, writes=[("pb", a[0]) for a in accs], name=name)

        prev_H_tasks = []
        out_dma_tasks = []
        sgc = dict(n=0)
        abc = dict(n=0)
        stc = dict(n=0)

        for st in STS:
            g0, n_st, n_p, smp = st["g0"], st["n"], st["n_p"], st["smp"]
            tiles = st["tiles"]
            NT = len(tiles)
            sidx = st["idx"]

            par = sidx % 2
            X, XN, BBCB = st_views(par)
            XK, XNK, BBK = f"x{par}", f"xn{par}", f"bbcb{par}"
            if sidx > 0:
                nb_state["prev"], nb_state["cur"] = nb_state["cur"], []
                fin_state["prev"], fin_state["cur"] = fin_state["cur"], []

            def emit_xload(st2, deps):
                p2 = st2["idx"] % 2
                X2 = st_views(p2)[0]
                for kc in range(KC):
                    P.add("sp", lambda e, g2=st2["g0"], n2=st2["n"], kc=kc, X2=X2: e.dma_start(out=X2[:, kc, 0:n2], in_=xT[:, kc, g2:g2 + n2]),
                          writes=[(f"x{p2}", kc, tt) for tt in range(len(st2["tiles"]))], extra=deps, dma=f"d_x{kc}", name=f"ldx{st2['idx']}_{kc}")

            if sidx == 0:
                emit_xload(st, [])

            def ext_parts(a, b, H, shift):
                parts = []
                pa, pb_ = a, min(b, n_p)
                if pb_ > pa:
                    parts.append(("p", pa, pb_ - pa, pa + shift))
                if b > n_p:
                    assert smp and a <= n_p and b == n_p + NSMP * SL
                    parts.append(("s", n_p, NSMP * SL, H + n_p + shift))
                return parts

            def ext_ap(buf, part, H):
                kind, lo, n, off = part
                if kind == "p":
                    return buf[:, off:off + n]
                return sub_ap(buf[:], off, [[H + SL, NSMP], [1, SL]])

            def cmp_ap(ap2d_fn, part):
                kind, lo, n, off = part
                base = ap2d_fn(lo, lo + n)
                if kind == "p":
                    return base
                return sub_ap(base, 0, [[SL, NSMP], [1, SL]])

            def norm_accum(j, tt, tagn):
                ta, tn = tiles[tt]
                sq = SQ[tt]
                bank = 6 + tt
                t_sq = P.add("act", lambda e, sq=sq, j=j, ta=ta, tn=tn, X=X: e.activation(out=sq[:, j, 0:tn], in_=X[:, j, ta:ta + tn], func=AF.Square),
                                 reads=[(XK, j, tt)], writes=[("sq", tt, j)], name=f"sq_{tagn}{tt}_{j}")
                if tagn.startswith("fin"):
                    fin_state["cur"].append(t_sq)
                P.add("pe", lambda e, sq=sq, j=j, tn=tn, bank=bank: e.matmul(PB[bank][:, 0:tn], ONES[:], sq[:, j, 0:tn], start=(j == 0), stop=(j == KC - 1)),
                      reads=[("sq", tt, j), ("ones",)], writes=[("pb", bank)], name=f"ss_{tagn}{tt}_{j}")

            def norm_finish(tt, gcol, eps, out_fn, out_keyname, tagn, extra_w=()):
                ta, tn = tiles[tt]
                bank = 6 + tt
                si = tt
                P.add("act", lambda e, tn=tn, bank=bank, si=si, eps=eps: e.activation(out=SD[si][:, 0:tn], in_=PB[bank][:, 0:tn], func=AF.Sqrt, scale=1.0 / D, bias=EPSC[eps]),
                      reads=[("pb", bank), ("epsc", 0)], writes=[("sd", si)], name=f"sd_{tagn}{tt}", prio=0.1 * tt)
                P.add("dve", lambda e, tn=tn, si=si: e.reciprocal(out=RSTD[si][:, 0:tn], in_=SD[si][:, 0:tn]),
                      reads=[("sd", si)], writes=[("rstd", si)], name=f"rstd_{tagn}{tt}", prio=0.1 * tt)
                for kc in range(KC):
                    P.add("dve", lambda e, kc=kc, ta=ta, tn=tn, si=si, X=X: e.scalar_tensor_tensor(
                        out=out_fn(kc, ta, ta + tn), in0=X[:, kc, ta:ta + tn], scalar=pvc(gcol + kc),
                        in1=RSTD[si][:, 0:tn], op0=ALU.mult, op1=ALU.mult),
                        reads=[(XK, kc, tt), ("rstd", si), ("pv",)], writes=[(out_keyname, kc, tt)], extra=extra_w, name=f"nrm_{tagn}{tt}_{kc}", prio=((0.01 * kc + 0.001 * tt) if (tagn.startswith("fin") and Y_SPLIT) else 0.1 * tt))
                    if tagn.startswith("fin"):
                        fin_state["cur"].append(P.tasks["dve"][-1])

            xn_out = lambda kc, a, b, XN=XN: XN[:, kc, a:b]
            for j in range(KC):
                for tt in range(NT):
                    norm_accum(j, tt, f"r1s{sidx}l0")
            for tt in range(NT):
                norm_finish(tt, PV_G1, RMS_EPS, xn_out, XNK, f"r1s{sidx}l0", extra_w=list(fin_state["prev"]))

            for l in range(DEPTH):
                tag = f"s{sidx}l{l}"

                units = {}
                silu_tasks = []
                xnk = lambda tt: [(XNK, kc, tt) for kc in range(KC)]

                def emit_A(j):
                    jb = j % 2
                    jd = j % 2
                    if j % UPC == 0:
                        c0 = UW * (j // UPC)
                        units["val"] = load_unit(w_in[l, :, OFF_AVAL + c0: OFF_AVAL + c0 + UW], 8, UW)
                        units["gate"] = load_unit(w_in[l, :, OFF_AGATE + c0: OFF_AGATE + c0 + UW], 8, UW)
                    uval, ugate = units["val"], units["gate"]
                    mc = (j % UPC) * 128
                    ux, ub = UX[jb], UB[jb]
                    if st["first"]:
                        P.add("act", lambda e, ux=ux: e.activation(out=ux[:, 0:HA], in_=PV[:, 0:HA], func=AF.Copy, scale=0.0),
                              reads=[("pv",)], writes=[("u", jb, "h")], name=f"uh0_{tag}{j}")
                    else:
                        hoff = (l * KC + j) * HA
                        P.add("act", lambda e, ux=ux, hoff=hoff: e.activation(out=ux[:, 0:HA], in_=HISTA[:, hoff:hoff + HA], func=AF.Copy),
                              reads=[("hista", l, j)], writes=[("u", jb, "h")], name=f"uh_{tag}{j}")
                    if smp:
                        src = sub_ap(SHA[:], (l * NSMP * KC + j) * HA, [[KC * HA, NSMP], [1, HA]])
                        dst = sub_ap(ux[:], HA + n_p, [[HA + SL, NSMP], [1, HA]])
                        P.add("act", lambda e, src=src, dst=dst: e.activation(out=dst, in_=src, func=AF.Copy),
                              reads=[("sha",)], writes=[("u", jb, "hs")], name=f"uhs_{tag}{j}")
                    wcol = PV_WA + (l * KC + j) * KA
                    bcol = PV_CAB + l * KC + j
                    if NPE > 0:
                        in0 = sub_ap(IDB[:], 0, [[0, NPE], [1, 128]])
                        in1 = sub_ap(PV[:], wcol, [[1, NPE], [0, 128]])
                        P.add(DIAG_ENG, lambda e, jd=jd, in0=in0, in1=in1: e.tensor_tensor(out=DIAG[jd][:, 0:NPE, :], in0=in0, in1=in1, op=ALU.mult),
                              reads=[("idb",), ("pv",)], writes=[("diag", jd)], name=f"diag_{tag}{j}")
                    def a_tile(tt):
                        ta, tn = tiles[tt]
                        bv, bg = next_banks(2)
                        accs = [(bv, tn, [(WS[uval][:, kc, mc:mc + 128], XN[:, kc, ta:ta + tn]) for kc in range(KC)]),
                                (bg, tn, [(WS[ugate][:, kc, mc:mc + 128], XN[:, kc, ta:ta + tn]) for kc in range(KC)])]
                        mm_job(accs, [("ws", uval), ("ws", ugate)] + xnk(tt), f"A_{tag}{j}_{tt}", fine_keys=(xnk(tt) if (j == 0 and FINE_A) else None))
                        si = sgc["n"] % NSG
                        sgc["n"] += 1
                        P.add("act", lambda e, si=si, bg=bg, tn=tn: e.activation(out=SG[si][:, 0:tn], in_=PB[bg][:, 0:tn], func=AF.Sigmoid),
                              reads=[("pb", bg)], writes=[("sg", si)], name=f"sig_{tag}{j}_{tt}")
                        parts = ext_parts(ta, ta + tn, HA, HA)
                        def fn(e, parts=parts, ux=ux, bv=bv, si=si, ta=ta):
                            last = None
                            for part in parts:
                                o = ext_ap(ux, part, HA)
                                i0 = cmp_ap(lambda a, b: PB[bv][:, a - ta:b - ta], part)
                                i1 = cmp_ap(lambda a, b: SG[si][:, a - ta:b - ta], part)
                                last = e.tensor_tensor(out=o, in0=i0, in1=i1, op=ALU.mult)
                            return last
                        P.add("dve", fn, reads=[("pb", bv), ("sg", si)], writes=[("u", jb, tt)], name=f"glu_{tag}{j}_{tt}")
                    def a_post():
                        ukeys = [("u", jb, "h")] + ([("u", jb, "hs")] if smp else []) + [("u", jb, tt) for tt in range(NT)]
                        LU = HA + n_p + (NSMP * (HA + SL) if smp else 0)
                        if not st["last"]:
                            hoff = (l * KC + j) * HA
                            P.add("act", lambda e, ux=ux, hoff=hoff, n_p=n_p: e.activation(out=HISTA[:, hoff:hoff + HA], in_=ux[:, n_p:n_p + HA], func=AF.Copy),
                                  reads=ukeys, writes=[("hista", l, j)], name=f"hst_{tag}{j}")
                        else:
                            ooff = (l * KC + j) * 3 * HA
                            def fn(e, ux=ux, ooff=ooff, n_p=n_p):
                                e.activation(out=OUTA[:, ooff:ooff + HA], in_=ux[:, n_p:n_p + HA], func=AF.Copy)
                                src = sub_ap(ux[:], HA + n_p + SL, [[HA + SL, NSMP], [1, HA]])
                                dst = sub_ap(OUTA[:], ooff + HA, [[HA, NSMP], [1, HA]])
                                return e.activation(out=dst, in_=src, func=AF.Copy)
                            P.add("act", fn, reads=ukeys, writes=[("outa", l, j)], name=f"outa_{tag}{j}")
                        cakeys = [("ca", j, tt) for tt in range(NT)]
                        if NPE > 0:
                            P.add("act", lambda e, ux=ux, ub=ub, LU=LU: e.activation(out=ub[:, 0:LU], in_=ux[:, 0:LU], func=AF.Copy),
                                  reads=ukeys, writes=[("ubf", jb)], name=f"ubf_{tag}{j}")
                            for tt, (ta, tn) in enumerate(tiles):
                                bank = 6 + (abc["n"] % 2)
                                abc["n"] += 1
                                parts = ext_parts(ta, ta + tn, HA, 0)
                                def fn(e, parts=parts, ub=ub, bank=bank, jd=jd, ta=ta):
                                    last = None
                                    for part in parts:
                                        o = cmp_ap(lambda a, b: PB[bank][:, a - ta:b - ta], part)
                                        for k in range(NPE):
                                            p2 = (part[0], part[1], part[2], part[3] + k)
                                            last = e.matmul(o, DIAG[jd][:, k, :], ext_ap(ub, p2, HA), start=(k == 0), stop=(k == NPE - 1))
                                    return last
                                P.add("pe", fn, reads=[("ubf", jb), ("diag", jd)], writes=[("pb", bank)], name=f"cvpe_{tag}{j}_{tt}")
                                P.add("act", lambda e, j=j, ta=ta, tn=tn, bank=bank, bcol=bcol: e.activation(
                                    out=ca_ap(j, ta, ta + tn), in_=PB[bank][:, 0:tn], func=AF.Identity, bias=pvc(bcol)),
                                    reads=[("pb", bank), ("pv",)], writes=[("ca", j, tt)], extra=prev_H_tasks, name=f"cvev_{tag}{j}_{tt}")
                        for k in range(NPE, KA):
                            partsr = ext_parts(0, n_st, HA, k)
                            def fn(e, partsr=partsr, ux=ux, j=j, k=k, first=(k == 0), bcol=bcol, wcol=wcol):
                                last = None
                                for part in partsr:
                                    i0 = ext_ap(ux, part, HA)
                                    o = cmp_ap(lambda a, b: ca_ap(j, a, b), part)
                                    if first:
                                        last = e.tensor_scalar(out=o, in0=i0, scalar1=pvc(wcol + k), scalar2=pvc(bcol), op0=ALU.mult, op1=ALU.add)
                                    else:
                                        last = e.scalar_tensor_tensor(out=o, in0=i0, scalar=pvc(wcol + k), in1=o, op0=ALU.mult, op1=ALU.add)
                                return last
                            rk = ukeys + [("pv",)] + ([] if k == 0 else cakeys)
                            P.add("dve", fn, reads=rk, writes=cakeys, extra=(prev_H_tasks if k == 0 else ()), name=f"tap_{tag}{j}_{k}")
                        if HOIST_LN:
                            ta0, tn0 = tiles[0]
                            P.add("act", lambda e, j=j, ta0=ta0, tn0=tn0: e.activation(out=SQ[0][:, j, 0:tn0], in_=ca_ap(j, ta0, ta0 + tn0), func=AF.Square),
                                  reads=[("ca", j, 0)], writes=[("sq", 0, j)], name=f"lnsqh_{tag}{j}")
                            P.add("act", lambda e, j=j, ta0=ta0, tn0=tn0: e.activation(out=CBF[:, j, 0:tn0], in_=ca_ap(j, ta0, ta0 + tn0), func=AF.Copy),
                                  reads=[("ca", j, 0)], writes=[("cbf", j)], name=f"lncbh_{tag}{j}")
                    return a_tile, a_post

                def emit_C(j):
                    jb = j % 2
                    jd = j % 2
                    if j % UPC == 0:
                        c0 = UW * (j // UPC)
                        units["bc"] = load_unit(w_in[l, :, OFF_BC + c0: OFF_BC + c0 + UW], 8, UW)
                        units["bx"] = load_unit(w_in[l, :, OFF_BX + c0: OFF_BX + c0 + UW], 8, UW)
                        units["bb"] = load_unit(w_in[l, :, OFF_BB + c0: OFF_BB + c0 + UW], 8, UW)
                    ubc, ubx, ubb = units["bc"], units["bx"], units["bb"]
                    mc = (j % UPC) * 128
                    zx, zb = ZX[jb], ZB[jb]
                    if st["first"]:
                        P.add("act", lambda e, zx=zx: e.activation(out=zx[:, 0:HB], in_=PV[:, 0:HB], func=AF.Copy, scale=0.0),
                              reads=[("pv",)], writes=[("z", jb, "h")], name=f"zh0_{tag}{j}")
                    else:
                        hoffz = (l * KC + j) * HB
                        P.add("act", lambda e, zx=zx, hoffz=hoffz: e.activation(out=zx[:, 0:HB], in_=HISTZ[:, hoffz:hoffz + HB], func=AF.Copy),
                              reads=[("histz", l, j)], writes=[("z", jb, "h")], name=f"zh_{tag}{j}")
                    if smp:
                        src = sub_ap(SHB[:], (l * NSMP * KC + j) * HB, [[KC * HB, NSMP], [1, HB]])
                        dst = sub_ap(zx[:], HB + n_p, [[HB + SL, NSMP], [1, HB]])
                        P.add("act", lambda e, src=src, dst=dst: e.activation(out=dst, in_=src, func=AF.Copy),
                              reads=[("shb",)], writes=[("z", jb, "hs")], name=f"zhs_{tag}{j}")
                    wbcol = PV_WB + (l * KC + j) * KB
                    in0 = sub_ap(IDB[:], 0, [[0, KB], [1, 128]])
                    in1 = sub_ap(PV[:], wbcol, [[1, KB], [0, 128]])
                    P.add(DIAG_ENG, lambda e, jd=jd, in0=in0, in1=in1: e.tensor_tensor(out=DIAG3[jd][:, :, :], in0=in0, in1=in1, op=ALU.mult),
                          reads=[("idb",), ("pv",)], writes=[("diag3", jd)], name=f"diag3_{tag}{j}")
                    def c_tile(tt):
                        ta, tn = tiles[tt]
                        b1, b2 = next_banks(2)
                        accs = [(b1, tn, [(WS[ubc][:, kc, mc:mc + 128], XN[:, kc, ta:ta + tn]) for kc in range(KC)]),
                                (b2, tn, [(WS[ubx][:, kc, mc:mc + 128], XN[:, kc, ta:ta + tn]) for kc in range(KC)])]
                        mm_job(accs, [("ws", ubc), ("ws", ubx)] + xnk(tt), f"C1_{tag}{j}_{tt}")
                        si = sgc["n"] % NSG
                        sgc["n"] += 1
                        P.add("act", lambda e, si=si, b2=b2, tn=tn: e.activation(out=SG[si][:, 0:tn], in_=PB[b2][:, 0:tn], func=AF.Copy),
                              reads=[("pb", b2)], writes=[("sg", si)], name=f"bx_{tag}{j}_{tt}")
                        parts = ext_parts(ta, ta + tn, HB, HB)
                        def fn(e, parts=parts, zx=zx, b1=b1, si=si, ta=ta):
                            last = None
                            for part in parts:
                                o = ext_ap(zx, part, HB)
                                i0 = cmp_ap(lambda a, b: PB[b1][:, a - ta:b - ta], part)
                                i1 = cmp_ap(lambda a, b: SG[si][:, a - ta:b - ta], part)
                                last = e.tensor_tensor(out=o, in0=i0, in1=i1, op=ALU.mult)
                            return last
                        P.add("dve", fn, reads=[("pb", b1), ("sg", si)], writes=[("z", jb, tt)], name=f"zb_{tag}{j}_{tt}")
                        (b3,) = next_banks(1)
                        accs = [(b3, tn, [(WS[ubb][:, kc, mc:mc + 128], XN[:, kc, ta:ta + tn]) for kc in range(KC)])]
                        mm_job(accs, [("ws", ubb)] + xnk(tt), f"C2_{tag}{j}_{tt}")
                        P.add("act", lambda e, jb=jb, b3=b3, ta=ta, tn=tn: e.activation(out=BBUF[jb][:, ta:ta + tn], in_=PB[b3][:, 0:tn], func=AF.Copy),
                              reads=[("pb", b3)], writes=[("bb", jb, tt)], name=f"bbv_{tag}{j}_{tt}")
                    def c_post():
                        zkeys = [("z", jb, "h")] + ([("z", jb, "hs")] if smp else []) + [("z", jb, tt) for tt in range(NT)]
                        LZ = HB + n_p + (NSMP * (HB + SL) if smp else 0)
                        if not st["last"]:
                            hoffz = (l * KC + j) * HB
                            P.add("act", lambda e, zx=zx, hoffz=hoffz, n_p=n_p: e.activation(out=HISTZ[:, hoffz:hoffz + HB], in_=zx[:, n_p:n_p + HB], func=AF.Copy),
                                  reads=zkeys, writes=[("histz", l, j)], name=f"hsz_{tag}{j}")
                        else:
                            ooffz = (l * KC + j) * 3 * HB
                            def fn(e, zx=zx, ooffz=ooffz, n_p=n_p):
                                e.activation(out=OUTB[:, ooffz:ooffz + HB], in_=zx[:, n_p:n_p + HB], func=AF.Copy)
                                src = sub_ap(zx[:], HB + n_p + SL, [[HB + SL, NSMP], [1, HB]])
                                dst = sub_ap(OUTB[:], ooffz + HB, [[HB, NSMP], [1, HB]])
                                return e.activation(out=dst, in_=src, func=AF.Copy)
                            P.add("act", fn, reads=zkeys, writes=[("outb", l, j)], name=f"outb_{tag}{j}")
                        P.add("act", lambda e, zx=zx, zb=zb, LZ=LZ: e.activation(out=zb[:, 0:LZ], in_=zx[:, 0:LZ], func=AF.Copy),
                              reads=zkeys, writes=[("zbf", jb)], name=f"zbf_{tag}{j}")
                        for tt, (ta, tn) in enumerate(tiles):
                            bank = 6 + (abc["n"] % 2)
                            abc["n"] += 1
                            parts = ext_parts(ta, ta + tn, HB, 0)
                            def fn(e, parts=parts, zb=zb, bank=bank, jd=jd, ta=ta):
                                last = None
                                for part in parts:
                                    o = cmp_ap(lambda a, b: PB[bank][:, a - ta:b - ta], part)
                                    for k in range(KB):
                                        p2 = (part[0], part[1], part[2], part[3] + k)
                                        last = e.matmul(o, DIAG3[jd][:, k, :], ext_ap(zb, p2, HB), start=(k == 0), stop=(k == KB - 1))
                                return last
                            P.add("pe", fn, reads=[("zbf", jb), ("diag3", jd)], writes=[("pb", bank)], name=f"cv3pe_{tag}{j}_{tt}")
                            P.add("dve", lambda e, j=j, jb=jb, ta=ta, tn=tn, bank=bank, BBCB=BBCB: e.tensor_tensor(
                                out=BBCB[:, j, ta:ta + tn], in0=PB[bank][:, 0:tn], in1=BBUF[jb][:, ta:ta + tn], op=ALU.mult),
                                reads=[("pb", bank), ("bb", jb, tt)], writes=[(BBK, j, tt)], name=f"bbcb_{tag}{j}_{tt}")
                    return c_tile, c_post

                for step in range(KC + CLAG):
                    def do_A():
                        if step < KC:
                            a_tile, a_post = emit_A(step)
                            for tt in range(NT):
                                a_tile(tt)
                            a_post()
                    def do_C():
                        jc = step - CLAG
                        if 0 <= jc < KC:
                            c_tile, c_post = emit_C(jc)
                            for tt in range(NT):
                                c_tile(tt)
                            c_post()
                    if AC_ORDER == 0:
                        do_A()
                        do_C()
                    else:
                        do_C()
                        do_A()

                for tt, (ta, tn) in enumerate(tiles):
                    si = tt
                    sq = SQ[tt]
                    sqk = [("sq", tt, kc) for kc in range(KC)]
                    cakt = [("ca", kc, tt) for kc in range(KC)]
                    cbk = [("cbf", kc) for kc in range(KC)]
                    if not (HOIST_LN and tt == 0):
                        P.add("act", lambda e, sq=sq, ta=ta, tn=tn: e.activation(out=sq[:, :, 0:tn], in_=ca3_ap(ta, ta + tn), func=AF.Square),
                              reads=cakt, writes=sqk, name=f"lnsq_{tag}{tt}")
                        P.add("act", lambda e, ta=ta, tn=tn: e.activation(out=CBF[:, :, 0:tn], in_=ca3_ap(ta, ta + tn), func=AF.Copy),
                              reads=cakt, writes=cbk, name=f"lncb_{tag}{tt}")
                    def fn(e, sq=sq, tn=tn):
                        last = None
                        for kc in range(KC):
                            last = e.matmul(PB[6][:, 0:tn], ONES[:], sq[:, kc, 0:tn], start=(kc == 0), stop=(kc == KC - 1))
                        return last
                    P.add("pe", fn, reads=sqk + [("ones",)], writes=[("pb", 6)], name=f"lns2_{tag}{tt}")
                    def fn(e, tn=tn):
                        last = None
                        for kc in range(KC):
                            last = e.matmul(PB[7][:, 0:tn], ONES[:], CBF[:, kc, 0:tn], start=(kc == 0), stop=(kc == KC - 1))
                        return last
                    P.add("pe", fn, reads=cbk + [("ones",)], writes=[("pb", 7)], name=f"lns1_{tag}{tt}")
                    P.add("dve", lambda e, si=si, tn=tn: e.tensor_scalar(out=MEAN[si][:, 0:tn], in0=PB[7][:, 0:tn], scalar1=1.0 / D, scalar2=None, op0=ALU.mult),
                          reads=[("pb", 7)], writes=[("mean", si)], name=f"lnm_{tag}{tt}")
                    P.add("dve", lambda e, si=si, tn=tn: e.tensor_tensor(out=VAR[si][:, 0:tn], in0=MEAN[si][:, 0:tn], in1=MEAN[si][:, 0:tn], op=ALU.mult),
                          reads=[("mean", si)], writes=[("var", si)], name=f"lnmsq_{tag}{tt}")
                    P.add("dve", lambda e, si=si, tn=tn: e.scalar_tensor_tensor(out=VAR[si][:, 0:tn], in0=PB[6][:, 0:tn], scalar=1.0 / D, in1=VAR[si][:, 0:tn], op0=ALU.mult, op1=ALU.subtract),
                          reads=[("pb", 6), ("var", si)], writes=[("var", si)], name=f"lnvar_{tag}{tt}")
                    P.add("act", lambda e, si=si, tn=tn: e.activation(out=SD[si][:, 0:tn], in_=VAR[si][:, 0:tn], func=AF.Sqrt, bias=EPSC[LN_EPS]),
                          reads=[("var", si), ("epsc", 1)], writes=[("sd", si)], name=f"lnsd_{tag}{tt}")
                    P.add("dve", lambda e, si=si, tn=tn: e.reciprocal(out=RSTD[si][:, 0:tn], in_=SD[si][:, 0:tn]),
                          reads=[("sd", si)], writes=[("rstd", si)], name=f"lnrs_{tag}{tt}")
                    P.add("dve", lambda e, si=si, tn=tn: e.scalar_tensor_tensor(out=NMR[si][:, 0:tn], in0=MEAN[si][:, 0:tn], scalar=-1.0, in1=RSTD[si][:, 0:tn], op0=ALU.mult, op1=ALU.mult),
                          reads=[("mean", si), ("rstd", si)], writes=[("nmr", si)], name=f"lnnm_{tag}{tt}")
                    for kc in range(KC):
                        P.add("dve", lambda e, kc=kc, ta=ta, tn=tn, si=si: e.tensor_tensor(out=ca_ap(kc, ta, ta + tn), in0=ca_ap(kc, ta, ta + tn), in1=RSTD[si][:, 0:tn], op=ALU.mult),
                              reads=[("ca", kc, tt), ("rstd", si)], writes=[("ca", kc, tt)], name=f"lnmul_{tag}{tt}_{kc}")
                        P.add("dve", lambda e, kc=kc, ta=ta, tn=tn, si=si: e.tensor_tensor(out=ca_ap(kc, ta, ta + tn), in0=ca_ap(kc, ta, ta + tn), in1=NMR[si][:, 0:tn], op=ALU.add),
                              reads=[("ca", kc, tt), ("nmr", si)], writes=[("ca", kc, tt)], name=f"lnadd_{tag}{tt}_{kc}")
                    for kc in range(KC):
                        gcol = PV_LNG + l * KC + kc
                        bcol = PV_LNB + l * KC + kc
                        t_ = P.add("act", lambda e, kc=kc, ta=ta, tn=tn, gcol=gcol, bcol=bcol: e.activation(
                            out=caact_ap(kc, ta, ta + tn), in_=ca_ap(kc, ta, ta + tn), func=AF.Silu, scale=pvc(gcol), bias=pvc(bcol)),
                            reads=[("ca", kc, tt), ("pv",)], writes=[("caact", kc, tt)], extra=prev_H_tasks, name=f"lnsilu_{tag}{tt}_{kc}")
                        silu_tasks.append(t_)

                D_pe = []
                dunit = {}

                def get_unit(name, q):
                    if (name, q) not in dunit:
                        c0 = UW * q
                        if name == "wa":
                            dunit[(name, q)] = load_unit(w_a_out[l, :, c0: c0 + UW], 8, UW)
                        elif name == "wb":
                            dunit[(name, q)] = load_unit(w_b_out[l, :, c0: c0 + UW], 8, UW)
                        elif name == "ga":
                            dunit[(name, q)] = load_unit(w_in[l, :, OFF_GA + c0: OFF_GA + c0 + UW], 8, UW)
                        else:
                            dunit[(name, q)] = load_unit(w_in[l, :, OFF_GB + c0: OFF_GB + c0 + UW], 8, UW)
                    return dunit[(name, q)]

                ROWS = [(BBUF[0], [("bb", 0, tt) for tt in range(NT)]),
                        (BBUF[1], [("bb", 1, tt) for tt in range(NT)]),
                        (UX[0], [("u", 0, "h"), ("u", 0, "hs")] + [("u", 0, tt) for tt in range(NT)]),
                        (UX[1], [("u", 1, "h"), ("u", 1, "hs")] + [("u", 1, tt) for tt in range(NT)]),
                        (ZX[0], [("z", 0, "h"), ("z", 0, "hs")] + [("z", 0, tt) for tt in range(NT)]),
                        (ZX[1], [("z", 1, "h"), ("z", 1, "hs")] + [("z", 1, tt) for tt in range(NT)])]
                for xi, xr in enumerate(XROWS):
                    ROWS.append((xr, [("xrow", xi, tt) for tt in range(NT)]))
                for qi in range(N_SQROWS):
                    ROWS.append((SQ[qi][:].rearrange("p k n -> p (k n)").bitcast(F32), [("sq", qi, kc) for kc in range(KC)]))
                NROW = len(ROWS)

                def emit_D2(j):
                    uwb, ugb = get_unit("wb", j // UPC), get_unit("gb", j // UPC)
                    mc = (j % UPC) * 128
                    row, rkeys = ROWS[j % NROW]
                    for tt, (ta, tn) in enumerate(tiles):
                        b3, b4 = next_banks(2, RING_D)
                        accs = [(b3, tn, [(WS[uwb][:, kc, mc:mc + 128], BBCB[:, kc, ta:ta + tn]) for kc in range(KC)]),
                                (b4, tn, [(WS[ugb][:, kc, mc:mc + 128], XN[:, kc, ta:ta + tn]) for kc in range(KC)])]
                        D_pe.append(mm_job(accs, [("ws", uwb), ("ws", ugb)] + [(XNK, kc, tt) for kc in range(KC)] + [(BBK, kc, tt) for kc in range(KC)], f"D2_{tag}{j}_{tt}"))
                        si2 = sgc["n"] % NSG
                        sgc["n"] += 1
                        P.add("act", lambda e, si2=si2, b4=b4, tn=tn: e.activation(out=SG[si2][:, 0:tn], in_=PB[b4][:, 0:tn], func=AF.Sigmoid),
                              reads=[("pb", b4)], writes=[("sg", si2)], name=f"sgb_{tag}{j}_{tt}")
                        P.add("dve", lambda e, row=row, b3=b3, si2=si2, ta=ta, tn=tn: e.tensor_tensor(out=row[:, ta:ta + tn], in0=PB[b3][:, 0:tn], in1=SG[si2][:, 0:tn], op=ALU.mult),
                              reads=[("pb", b3), ("sg", si2)], writes=rkeys, name=f"m2_{tag}{j}_{tt}")

                def emit_D1(j):
                    uwa, uga = get_unit("wa", j // UPC), get_unit("ga", j // UPC)
                    mc = (j % UPC) * 128
                    row, rkeys = ROWS[j % NROW]
                    for tt, (ta, tn) in enumerate(tiles):
                        b1, b2 = next_banks(2, RING_D)
                        accs = [(b1, tn, [(WS[uwa][:, kc, mc:mc + 128], caact_ap(kc, ta, ta + tn)) for kc in range(KC)]),
                                (b2, tn, [(WS[uga][:, kc, mc:mc + 128], XN[:, kc, ta:ta + tn]) for kc in range(KC)])]
                        D_pe.append(mm_job(accs, [("ws", uwa), ("ws", uga)] + [(XNK, kc, tt) for kc in range(KC)] + [("caact", kc, tt) for kc in range(KC)], f"D1_{tag}{j}_{tt}",
                                           fine_keys=([("caact", kc, tt) for kc in range(KC)] if (j == 0 and FINE_D) else None)))
                        si = sgc["n"] % NSG
                        sgc["n"] += 1
                        P.add("act", lambda e, si=si, b2=b2, tn=tn: e.activation(out=SG[si][:, 0:tn], in_=PB[b2][:, 0:tn], func=AF.Sigmoid),
                              reads=[("pb", b2)], writes=[("sg", si)], name=f"sga_{tag}{j}_{tt}")
                        mi = (j * NT + tt) % 2
                        P.add("dve", lambda e, mi=mi, b1=b1, si=si, tn=tn: e.tensor_tensor(out=M1[mi][:, 0:tn], in0=PB[b1][:, 0:tn], in1=SG[si][:, 0:tn], op=ALU.mult),
                              reads=[("pb", b1), ("sg", si)], writes=[("m1", mi)], name=f"m1_{tag}{j}_{tt}")
                        P.add("dve", lambda e, mi=mi, j=j, row=row, ta=ta, tn=tn: e.tensor_tensor(out=merged_ap(j, ta, ta + tn), in0=row[:, ta:ta + tn], in1=M1[mi][:, 0:tn], op=ALU.add),
                              reads=rkeys + [("m1", mi)], writes=[("merged", j, tt)], extra=silu_tasks, name=f"mrg_{tag}{j}_{tt}")

                for j in range(min(NROW, KC)):
                    emit_D2(j)
                for j in range(KC):
                    emit_D1(j)
                    if j + NROW < KC:
                        emit_D2(j + NROW)

                P.add("act", lambda e: e.activation(out=WARM[:, 0:1], in_=EPS_T[:, 0:1], func=AF.Sqrt), reads=[("epsc", 0)], writes=[("warm",)], name=f"warmE_{tag}", prio=2)
                E_pe = []
                wo_units = [load_unit(w_o[l, :, UW * q: UW * q + UW], 8, UW) for q in range(KC // UPC)]
                for tt, (ta, tn) in enumerate(tiles):
                    for j in range(KC):
                        uo = wo_units[j // UPC]
                        mc = (j % UPC) * 128
                        (b1,) = next_banks(1)
                        accs = [(b1, tn, [(WS[uo][:, kc, mc:mc + 128], merged_ap(kc, ta, ta + tn)) for kc in range(KC)])]
                        E_pe.append(mm_job(accs, [("ws", uo)] + [("merged", kc, tt) for kc in range(KC)], f"E_{tag}{j}_{tt}"))
                        P.add("dve", lambda e, j=j, b1=b1, ta=ta, tn=tn, X=X: e.tensor_tensor(out=X[:, j, ta:ta + tn], in0=PB[b1][:, 0:tn], in1=X[:, j, ta:ta + tn], op=ALU.add),
                              reads=[("pb", b1), (XK, j, tt)], writes=[(XK, j, tt)], name=f"res1_{tag}{j}_{tt}")
                        norm_accum(j, tt, "r2" + tag)
                    norm_finish(tt, PV_G2 + l * KC, RMS_EPS, xn_out, XNK, "r2" + tag)

                alias_deps = D_pe + E_pe
                gunit = {}

                def g_units(q):
                    if q not in gunit:
                        c0 = UW * q
                        ncols = min(UW, DH - c0)
                        gunit[q] = (load_unit(w_gate[l, :, c0: c0 + ncols], 8, ncols), load_unit(w_up[l, :, c0: c0 + ncols], 8, ncols))
                    return gunit[q]

                def emit_G(hc, tt):
                    ufg, ufu = g_units(hc // UPC)
                    mc = (hc % UPC) * 128
                    ta, tn = tiles[tt]
                    b1, b2 = next_banks(2, RING_G)
                    accs = [(b1, tn, [(WS[ufg][:, kc, mc:mc + 128], XN[:, kc, ta:ta + tn]) for kc in range(KC)]),
                            (b2, tn, [(WS[ufu][:, kc, mc:mc + 128], XN[:, kc, ta:ta + tn]) for kc in range(KC)])]
                    mm_job(accs, [("ws", ufg), ("ws", ufu)] + [(XNK, kc, tt) for kc in range(KC)], f"G_{tag}{hc}_{tt}",
                           fine_keys=([(XNK, kc, tt) for kc in range(KC)] if (hc == 0 and FINE_G) else None))
                    si = sgc["n"] % NSG
                    sgc["n"] += 1
                    P.add("act", lambda e, si=si, b1=b1, tn=tn: e.activation(out=SG[si][:, 0:tn], in_=PB[b1][:, 0:tn], func=AF.Silu),
                          reads=[("pb", b1)], writes=[("sg", si)], name=f"fsilu_{tag}{hc}_{tt}")
                    P.add("dve", lambda e, si=si, b2=b2, hc=hc, ta=ta, tn=tn: e.tensor_tensor(out=f_ap(hc, ta, ta + tn), in0=PB[b2][:, 0:tn], in1=SG[si][:, 0:tn], op=ALU.mult),
                          reads=[("pb", b2), ("sg", si)], writes=[("f", hc, tt)], extra=alias_deps, name=f"f_{tag}{hc}_{tt}")

                for step in range(HC + GLAG):
                    if step < HC:
                        emit_G(step, 0)
                    if step - GLAG >= 0:
                        for tt in range(1, NT):
                            emit_G(step - GLAG, tt)

                if l == DEPTH - 1 and not st["last"]:
                    emit_xload(STS[sidx + 1], list(nb_state["cur"]))

                P.add("act", lambda e: e.activation(out=WARM[:, 0:1], in_=EPS_T[:, 0:1], func=AF.Sqrt), reads=[("epsc", 0)], writes=[("warm",)], name=f"warmH_{tag}", prio=2)
                H_pe = []
                for j in range(KC):
                    if j % UPC == 0:
                        c0 = UW * (j // UPC)
                        dunits = []
                        for r0 in range(0, HC, 8):
                            nk = min(8, HC - r0)
                            dunits.append((load_unit(w_down[l, r0 * 128:(r0 + nk) * 128, c0: c0 + UW], nk, UW), r0, nk))
                    mc = (j % UPC) * 128
                    for tt, (ta, tn) in enumerate(tiles):
                        (b1,) = next_banks(1)
                        mms = []
                        for (slot, r0, nk) in dunits:
                            for kk in range(nk):
                                mms.append((WS[slot][:, kk, mc:mc + 128], f_ap(r0 + kk, ta, ta + tn)))
                        accs = [(b1, tn, mms)]
                        H_pe.append(mm_job(accs, [("ws", u[0]) for u in dunits] + [("f", hc, tt) for hc in range(HC)], f"H_{tag}{j}_{tt}"))
                        P.add("dve", lambda e, j=j, b1=b1, ta=ta, tn=tn, X=X: e.tensor_tensor(out=X[:, j, ta:ta + tn], in0=PB[b1][:, 0:tn], in1=X[:, j, ta:ta + tn], op=ALU.add),
                              reads=[("pb", b1), (XK, j, tt)], writes=[(XK, j, tt)], name=f"res2_{tag}{j}_{tt}")
                        norm_accum(j, tt, (f"r1s{sidx}l{l + 1}" if l + 1 < DEPTH else f"fin{sidx}"))
                prev_H_tasks = H_pe
                for tt in range(NT):
                    if l + 1 < DEPTH:
                        norm_finish(tt, PV_G1 + (l + 1) * KC, RMS_EPS, xn_out, XNK, f"r1s{sidx}l{l + 1}")
                    else:
                        norm_finish(tt, PV_GF, RMS_EPS, lambda kc, a, b: ca_ap(kc, a, b), "yfm", f"fin{sidx}", extra_w=prev_H_tasks)

            if Y_SPLIT:
                sts_ = []
                for kc in range(KC):
                    t_ = P.add("sp", lambda e, g0=g0, n_st=n_st, kc=kc: e.dma_start(out=yT[:, kc, g0:g0 + n_st], in_=ca_ap(kc, 0, n_st)),
                               reads=[("yfm", kc, tt) for tt in range(NT)], dma=f"d_y{kc}", name=f"sty{sidx}_{kc}")
                    out_dma_tasks.append(t_)
                    sts_.append(t_)
                prev_H_tasks = prev_H_tasks + sts_
            else:
                t_ = P.add("sp", lambda e, g0=g0, n_st=n_st: e.dma_start(out=yT[:, :, g0:g0 + n_st], in_=sub_ap(CA, 0, [[TSMAX, KC], [1, n_st]])),
                           reads=[("yfm", kc, tt) for kc in range(KC) for tt in range(NT)], dma="d_y", name=f"sty{sidx}")
                out_dma_tasks.append(t_)
                prev_H_tasks = prev_H_tasks + [t_]

        t_ = P.add("sp", lambda e: e.dma_start(out=oA_d[:, :], in_=OUTA[:]), reads=[("outa", l, j) for l in range(DEPTH) for j in range(KC)], dma="d_oa", name="st_oa")
        out_dma_tasks.append(t_)
        t_ = P.add("sp", lambda e: e.dma_start(out=oB_d[:, :], in_=OUTB[:]), reads=[("outb", l, j) for l in range(DEPTH) for j in range(KC)], dma="d_ob", name="st_ob")
        out_dma_tasks.append(t_)
        P.add("sp", lambda e: None, extra=out_dma_tasks, name="final_wait")

        if SCHED:
            est_total = schedule(P, window=WINDOW)
            if DEBUG_SCHED:
                print(f"[kernel] scheduled estimate: {est_total / 1e3:.1f} us")
        dependents = set()
        for e_ in ENGS:
            for t in P.tasks[e_]:
                dependents.update(t.deps)
        dma_cnt = {}
        for e_ in ENGS:
            n = 0
            for t in P.tasks[e_]:
                if t.dma is not None:
                    dma_cnt[t.dma] = dma_cnt.get(t.dma, 0) + 1
                    t.ev_sem = sem(t.dma)
                    t.ev_val = 16 * dma_cnt[t.dma]
                    t.signal = True
                elif t in dependents:
                    n += 1
                    t.ev_sem = sem("p_" + e_)
                    t.ev_val = n
                    t.signal = True

        def emit(eng_name, e):
            waited = {}
            for t in P.tasks[eng_name]:
                need = {}
                for d in t.deps:
                    if d.dma is None and d.eng == eng_name and eng_name == "pe":
                        continue
                    assert d.signal, (t.name, d.name)
                    k = d.ev_sem.name
                    if need.get(k, (None, 0))[1] < d.ev_val:
                        need[k] = (d.ev_sem, d.ev_val)
                for k, (s_, v) in need.items():
                    if waited.get(k, 0) < v:
                        e.wait_ge(s_, v)
                        waited[k] = v
                inst = t.fn(e)
                if t.signal:
                    assert inst is not None, t.name
                    inst.then_inc(t.ev_sem, 16 if t.dma is not None else 1)

        block = es.enter_context(nc.Block())

        @block.tensor
        def _(e):
            emit("pe", e)

        @block.scalar
        def _(e):
            emit("act", e)

        @block.vector
        def _(e):
            emit("dve", e)

        @block.gpsimd
        def _(e):
            emit("pool", e)

        @block.sync
        def _(e):
            emit("sp", e)
    return nc


EPSC = {}


def _prep_core(c, x_prompt, x_sample, state_conv_a, state_conv_b, meta_tokens):
    toks = np.concatenate([meta_tokens, x_prompt[c], x_sample[2 * c], x_sample[2 * c + 1]], axis=0)
    xT = np.ascontiguousarray(toks.reshape(TT, KC, 128).transpose(2, 1, 0))
    sa = state_conv_a[:, 2 * c:2 * c + 2]
    shA = np.ascontiguousarray(sa.reshape(DEPTH, NSMP, HA, KC, 128).transpose(4, 0, 1, 3, 2)).reshape(128, -1)
    sbb = state_conv_b[:, 2 * c:2 * c + 2]
    shB = np.ascontiguousarray(sbb.reshape(DEPTH, NSMP, HB, KC, 128).transpose(4, 0, 1, 3, 2)).reshape(128, -1)
    return xT, shA, shB


def _vec_cols(v):
    lead = v.shape[:-1]
    a = v.reshape(-1, KC, 128)
    return np.ascontiguousarray(a.transpose(2, 0, 1)).reshape(128, -1)


_CACHE = {}


def kernel(x_prompt, x_sample, state_conv_a, state_conv_b, meta_tokens, norm1_g, w_in,
           conv_a_w, conv_a_b, ln_a_g, ln_a_b, w_a_out, conv_b_w, w_b_out, w_o, norm2_g,
           w_ffn_gate, w_ffn_up, w_ffn_down, final_norm_g):
    f32 = np.float32
    A = lambda a: np.ascontiguousarray(np.asarray(a, dtype=f32))
    x_prompt, x_sample, state_conv_a, state_conv_b, meta_tokens = map(A, (x_prompt, x_sample, state_conv_a, state_conv_b, meta_tokens))
    wa = np.asarray(conv_a_w, f32).reshape(DEPTH, KA, KC, 128).transpose(3, 0, 2, 1).reshape(128, -1)
    wb = np.asarray(conv_b_w, f32).reshape(DEPTH, KB, KC, 128).transpose(3, 0, 2, 1).reshape(128, -1)
    pvec = np.concatenate([
        _vec_cols(np.asarray(norm1_g, f32)), _vec_cols(np.asarray(conv_a_b, f32)), _vec_cols(np.asarray(ln_a_g, f32)),
        _vec_cols(np.asarray(ln_a_b, f32)), _vec_cols(np.asarray(norm2_g, f32)), _vec_cols(np.asarray(final_norm_g, f32)[None]),
        wa, wb], axis=1)
    assert pvec.shape == (128, NPV), pvec.shape
    pvec = np.ascontiguousarray(pvec)

    if "nc" not in _CACHE:
        _CACHE["nc"] = build()
    nc = _CACHE["nc"]
    shared = dict(pvec=pvec, w_in=A(w_in), w_a_out=A(w_a_out), w_b_out=A(w_b_out), w_o=A(w_o),
                  w_ffn_gate=A(w_ffn_gate), w_ffn_up=A(w_ffn_up), w_ffn_down=A(w_ffn_down))
    in_maps = []
    for c in range(NCORES):
        xT, shA, shB = _prep_core(c, x_prompt, x_sample, state_conv_a, state_conv_b, meta_tokens)
        m = dict(shared)
        m.update(xT=xT, shA=shA, shB=shB)
        in_maps.append(m)
    res = run_bass_kernel_spmd(nc, in_maps, core_ids=list(range(NCORES)))
    B = x_prompt.shape[0]
    y_prompt = np.empty((B, SEQ, D), f32)
    y_sample = np.empty((2 * NCORES, SL, D), f32)
    nap = np.empty((DEPTH, B, HA, D), f32)
    nbp = np.empty((DEPTH, B, HB, D), f32)
    nas = np.empty((DEPTH, 2 * NCORES, HA, D), f32)
    nbs = np.empty((DEPTH, 2 * NCORES, HB, D), f32)
    for c in range(NCORES):
        r = res.results[c]
        y = np.asarray(r["yT"]).reshape(128, KC, TT).transpose(2, 1, 0).reshape(TT, D)
        y_prompt[c] = y[NMETA:TP]
        y_sample[2 * c] = y[TP:TP + SL]
        y_sample[2 * c + 1] = y[TP + SL:TT]
        oa = np.asarray(r["oA"]).reshape(128, DEPTH, KC, 3, HA).transpose(1, 3, 4, 2, 0).reshape(DEPTH, 3, HA, D)
        ob = np.asarray(r["oB"]).reshape(128, DEPTH, KC, 3, HB).transpose(1, 3, 4, 2, 0).reshape(DEPTH, 3, HB, D)
        nap[:, c] = oa[:, 0]
        nas[:, 2 * c] = oa[:, 1]
        nas[:, 2 * c + 1] = oa[:, 2]
        nbp[:, c] = ob[:, 0]
        nbs[:, 2 * c] = ob[:, 1]
        nbs[:, 2 * c + 1] = ob[:, 2]
    return (y_prompt, y_sample, nap, nbp, nas, nbs)
```
